# Optimizing a Trainium2 kernel written in Bass

```python
import math
import jax, jax.numpy as jnp
from jax import lax
import numpy as np

D_MODEL = 1024
BATCH = 32
SEQ = 2048
DEPTH = 1

CHUNK = 64
D_MIX = 2 * D_MODEL
SSD_WIDTH = D_MIX // 2
SSD_HEAD_DIM = 64
SSD_HEADS = SSD_WIDTH // SSD_HEAD_DIM
SSD_GROUPS = 2
SSD_STATE = 128
SSD_HEADS_PER_GROUP = SSD_HEADS // SSD_GROUPS
CONV_WIDTH = 4
CONV_DIM = SSD_WIDTH + 2 * SSD_GROUPS * SSD_STATE
ATT_WIDTH = D_MIX - SSD_WIDTH
ATT_HEAD_DIM = 64
ATT_HEADS = ATT_WIDTH // (2 * ATT_HEAD_DIM)
ROT_DIM = ATT_HEAD_DIM // 4
ROPE_THETA = 500000.0
Q_BLOCK = 128
IN_DIM = SSD_WIDTH + CONV_DIM + SSD_HEADS + 3 * ATT_WIDTH
SPLITS = (SSD_WIDTH,
          SSD_WIDTH + CONV_DIM,
          SSD_WIDTH + CONV_DIM + SSD_HEADS,
          SSD_WIDTH + CONV_DIM + SSD_HEADS + ATT_WIDTH,
          SSD_WIDTH + CONV_DIM + SSD_HEADS + 2 * ATT_WIDTH)
D_FF = -(-8 * D_MODEL // (3 * 256)) * 256
EPS = 1e-6

kernel_name = "hybrid_ssd_diffattn_swiglu_layer"


def rms_norm(x, w):
    xf = x.astype(jnp.float32)
    y = xf * lax.rsqrt(jnp.mean(xf * xf, axis=-1, keepdims=True) + EPS)
    return (y * w.astype(jnp.float32)).astype(x.dtype)


def causal_depthwise_conv(x, w, b):
    y = lax.conv_general_dilated(x, w[:, None, :], window_strides=(1,),
                                 padding=[(CONV_WIDTH - 1, 0)],
                                 dimension_numbers=("NWC", "WIO", "NWC"),
                                 feature_group_count=x.shape[-1])
    return y + b


def ssd_mixer(z, xbc, dt_raw, conv_w, conv_b, dt_bias, a_log, d_skip, norm_w):
    b, s, _ = xbc.shape
    G, R, P, N = SSD_GROUPS, SSD_HEADS_PER_GROUP, SSD_HEAD_DIM, SSD_STATE
    xbc = jax.nn.silu(causal_depthwise_conv(xbc, conv_w, conv_b))
    xs = xbc[..., :SSD_WIDTH]
    bm = xbc[..., SSD_WIDTH:SSD_WIDTH + G * N]
    cm = xbc[..., SSD_WIDTH + G * N:]
    xh = xs.astype(jnp.float32).reshape(b, s, G, R, P)
    bm = bm.astype(jnp.float32).reshape(b, s, G, N)
    cm = cm.astype(jnp.float32).reshape(b, s, G, N)
    dt = jax.nn.softplus(dt_raw.astype(jnp.float32) + dt_bias.astype(jnp.float32)).reshape(b, s, G, R)
    a = -jnp.exp(a_log.astype(jnp.float32)).reshape(G, R)
    nc = s // CHUNK

    def to_chunks(t):
        return jnp.moveaxis(t.reshape(b, nc, CHUNK, *t.shape[2:]), 1, 0)

    xdt = to_chunks(xh * dt[..., None])
    adt = to_chunks(dt * a)
    bc_all = to_chunks(bm)
    cc_all = to_chunks(cm)
    causal = jnp.tril(jnp.ones((CHUNK, CHUNK), dtype=bool))[None, :, :, None, None]

    def step(state, inp):
        xc, ac, bc, cc = inp
        a_cum = jnp.cumsum(ac, axis=1)
        seg = a_cum[:, :, None] - a_cum[:, None, :]
        decay = jnp.exp(jnp.where(causal, seg, -jnp.inf))
        cb = jnp.einsum("blgn,bsgn->blsg", cc, bc)
        y_diag = jnp.einsum("blsg,blsgr,bsgrp->blgrp", cb, decay, xc)
        y_off = jnp.einsum("blgn,bgrpn->blgrp", cc, state) * jnp.exp(a_cum)[..., None]
        to_end = jnp.exp(a_cum[:, -1:] - a_cum)
        new_state = (state * jnp.exp(a_cum[:, -1])[..., None, None]
                     + jnp.einsum("blgn,blgr,blgrp->bgrpn", bc, to_end, xc))
        return new_state, y_diag + y_off

    state0 = jnp.zeros((b, G, R, P, N), jnp.float32)
    _, y = lax.scan(step, state0, (xdt, adt, bc_all, cc_all))
    y = jnp.moveaxis(y, 0, 1).reshape(b, s, G, R, P)
    y = y + d_skip.astype(jnp.float32).reshape(G, R)[..., None] * xh
    g = y.reshape(b, s, G, R * P) * jax.nn.silu(z.astype(jnp.float32).reshape(b, s, G, R * P))
    g = g * lax.rsqrt(jnp.mean(g * g, axis=-1, keepdims=True) + EPS)
    out = g.reshape(b, s, SSD_WIDTH) * norm_w.astype(jnp.float32)
    return out.astype(xbc.dtype)


def partial_rope(x, cos, sin):
    half = ROT_DIM // 2
    x1, x2, rest = x[..., :half], x[..., half:ROT_DIM], x[..., ROT_DIM:]
    c = cos[:, None, None, :]
    sn = sin[:, None, None, :]
    return jnp.concatenate([x1 * c - x2 * sn, x2 * c + x1 * sn, rest], axis=-1)


def diff_attention(q, k, v, q_norm_w, k_norm_w, lq1, lk1, lq2, lk2, subln_w, lambda_init):
    b, s, _ = q.shape
    H, HD = ATT_HEADS, ATT_HEAD_DIM
    q = rms_norm(q.reshape(b, s, H, 2, HD), q_norm_w)
    k = rms_norm(k.reshape(b, s, H, 2, HD), k_norm_w)
    v = v.reshape(b, s, H, 2 * HD)
    pos = jnp.arange(s, dtype=jnp.float32)
    inv_freq = ROPE_THETA ** (-jnp.arange(0, ROT_DIM, 2, dtype=jnp.float32) / ROT_DIM)
    ang = pos[:, None] * inv_freq[None, :]
    cos = jnp.cos(ang).astype(q.dtype)
    sin = jnp.sin(ang).astype(q.dtype)
    q = partial_rope(q, cos, sin)
    k = partial_rope(k, cos, sin)
    lam = (jnp.exp(jnp.sum(lq1.astype(jnp.float32) * lk1.astype(jnp.float32)))
           - jnp.exp(jnp.sum(lq2.astype(jnp.float32) * lk2.astype(jnp.float32)))
           + lambda_init)
    nb = s // Q_BLOCK
    qb = q.reshape(b, nb, Q_BLOCK, H, 2, HD).transpose(1, 0, 3, 4, 2, 5)
    kt = k.transpose(0, 2, 3, 1, 4)
    vt = v.transpose(0, 2, 1, 3)
    key_chunk = jnp.arange(s) // CHUNK
    scale = HD ** -0.5

    def attend(args):
        q_blk, blk = args
        q_chunk = (blk * Q_BLOCK + jnp.arange(Q_BLOCK)) // CHUNK
        mask = key_chunk[None, :] <= q_chunk[:, None]
        sc = jnp.einsum("bhjqd,bhjkd->bhjqk", q_blk, kt).astype(jnp.float32) * scale
        p = jax.nn.softmax(jnp.where(mask, sc, -jnp.inf), axis=-1)
        w = p[:, :, 0] - lam * p[:, :, 1]
        return jnp.einsum("bhqk,bhkv->bhqv", w.astype(vt.dtype), vt)

    o = lax.map(attend, (qb, jnp.arange(nb)))
    o = o.transpose(1, 0, 3, 2, 4).reshape(b, s, H, 2 * HD)
    o = rms_norm(o, subln_w) * (1.0 - lambda_init)
    return o.reshape(b, s, ATT_WIDTH)


def setup_inputs(seed: int = 0) -> dict:
    key = jax.random.key(seed)
    ks = jax.random.split(key, 24)
    f32 = jnp.float32
    nrm = lambda k, shape, sc: jax.random.normal(k, shape, f32) * sc
    x = jax.random.normal(ks[0], (BATCH, SEQ, D_MODEL), f32)
    dt0 = jnp.exp(jax.random.uniform(ks[6], (DEPTH, SSD_HEADS), f32, math.log(1e-3), math.log(1e-1)))
    dt_bias = dt0 + jnp.log(-jnp.expm1(-dt0))
    a_log = jnp.log(jax.random.uniform(ks[7], (DEPTH, SSD_HEADS), f32, 1.0, 16.0))
    return {
        "x": x,
        "norm1_w": 1.0 + nrm(ks[1], (DEPTH, D_MODEL), 0.02),
        "w_in": nrm(ks[2], (DEPTH, D_MODEL, IN_DIM), D_MODEL ** -0.5),
        "conv_w": nrm(ks[3], (DEPTH, CONV_WIDTH, CONV_DIM), CONV_WIDTH ** -0.5),
        "conv_b": nrm(ks[4], (DEPTH, CONV_DIM), 0.02),
        "dt_bias": dt_bias,
        "a_log": a_log,
        "d_skip": 1.0 + nrm(ks[8], (DEPTH, SSD_HEADS), 0.1),
        "ssd_norm_w": 1.0 + nrm(ks[9], (DEPTH, SSD_WIDTH), 0.02),
        "q_norm_w": 1.0 + nrm(ks[10], (DEPTH, ATT_HEAD_DIM), 0.02),
        "k_norm_w": 1.0 + nrm(ks[11], (DEPTH, ATT_HEAD_DIM), 0.02),
        "lambda_q1": nrm(ks[12], (DEPTH, ATT_HEAD_DIM), 0.1),
        "lambda_k1": nrm(ks[13], (DEPTH, ATT_HEAD_DIM), 0.1),
        "lambda_q2": nrm(ks[14], (DEPTH, ATT_HEAD_DIM), 0.1),
        "lambda_k2": nrm(ks[15], (DEPTH, ATT_HEAD_DIM), 0.1),
        "subln_w": 1.0 + nrm(ks[16], (DEPTH, 2 * ATT_HEAD_DIM), 0.02),
        "w_out": nrm(ks[17], (DEPTH, D_MIX, D_MODEL), D_MIX ** -0.5),
        "norm2_w": 1.0 + nrm(ks[18], (DEPTH, D_MODEL), 0.02),
        "w_gate": nrm(ks[19], (DEPTH, D_MODEL, D_FF), D_MODEL ** -0.5),
        "w_up": nrm(ks[20], (DEPTH, D_MODEL, D_FF), D_MODEL ** -0.5),
        "w_down": nrm(ks[21], (DEPTH, D_FF, D_MODEL), D_FF ** -0.5),
    }


def reference(x, norm1_w, w_in, conv_w, conv_b, dt_bias, a_log, d_skip, ssd_norm_w,
              q_norm_w, k_norm_w, lambda_q1, lambda_k1, lambda_q2, lambda_k2, subln_w,
              w_out, norm2_w, w_gate, w_up, w_down):
    for l in range(DEPTH):
        lambda_init = 0.8 - 0.6 * math.exp(-0.3 * l)
        h = rms_norm(x, norm1_w[l])
        proj = h @ w_in[l]
        z, xbc, dt_raw, q, k, v = jnp.split(proj, SPLITS, axis=-1)
        y_ssd = ssd_mixer(z, xbc, dt_raw, conv_w[l], conv_b[l], dt_bias[l], a_log[l],
                          d_skip[l], ssd_norm_w[l])
        y_att = diff_attention(q, k, v, q_norm_w[l], k_norm_w[l], lambda_q1[l], lambda_k1[l],
                               lambda_q2[l], lambda_k2[l], subln_w[l], lambda_init)
        x = x + jnp.concatenate([y_ssd, y_att], axis=-1) @ w_out[l]
        h = rms_norm(x, norm2_w[l])
        x = x + (jax.nn.silu(h @ w_gate[l]) * (h @ w_up[l])) @ w_down[l]
    return x
```

```python
import math
from contextlib import ExitStack
import numpy as np
import concourse.bass as bass
import concourse.mybir as mybir
from concourse.bass_utils import run_bass_kernel_spmd

F32 = mybir.dt.float32
BF16 = mybir.dt.bfloat16
AF = mybir.ActivationFunctionType
ALU = mybir.AluOpType
AX = mybir.AxisListType

D = 1024
SEQ = 2048
NSEQ_CORE = 4
BLK = 512
IN_DIM = 5648
DFF = 2816
EPS = 1e-6
LAMBDA_INIT = 0.8 - 0.6 * math.exp(0.0)
NEG = -30000.0

ENG_ATTR = {"pe": "tensor", "act": "scalar", "dve": "vector", "pool": "gpsimd", "sp": "sync"}


class Prog:
    def __init__(self, nc):
        self.nc = nc
        self.ops = {e: [] for e in ENG_ATTR}
        self.res = {}
        self.seen = {e: {} for e in ENG_ATTR}
        self.marked = {e: set() for e in ENG_ATTR}
        self.semcnt = {}
        self.tags = {}
        self.tag_last = {}

    def settag(self, name, tag):
        self.tags[name] = tag

    def _deps(self, eng, R, W, join):
        deps = set()
        raw = set()
        for r in R:
            st = self.res.setdefault(r, {"w": [], "r": []})
            for t in st["w"]:
                deps.add(t); raw.add(t)
        for w in W:
            st = self.res.setdefault(w, {"w": [], "r": []})
            for t in st["w"]:
                if join and t[0] == "s":
                    continue
                deps.add(t)
            for t in st["r"]:
                deps.add(t)
        touched = set(self.tags[x] for x in list(R) + list(W) if x in self.tags)
        for tg in touched:
            for other, last in self.tag_last.items():
                if other != tg:
                    for k, v in last.items():
                        deps.add((k[0], k[1], v)); raw.add((k[0], k[1], v))
        out = []
        for t in deps:
            kind, key, val = t
            if kind == "e" and key == eng and t not in raw:
                continue
            sk = (kind, key)
            if self.seen[eng].get(sk, 0) >= val:
                continue
            out.append(t)
        best = {}
        for kind, key, val in out:
            best[(kind, key)] = max(best.get((kind, key), 0), val)
        waits = []
        for (kind, key), val in best.items():
            self.seen[eng][(kind, key)] = val
            if kind == "e":
                self.marked[key].add(val)
            waits.append((kind, key, val))
        return waits, touched

    def _commit(self, tok, R, W, join, touched):
        for r in R:
            self.res[r]["r"].append(tok)
        for w in W:
            st = self.res[w]
            if join and st["w"] and all(t[0] == "s" for t in st["w"]) and not st["r"]:
                st["w"].append(tok)
            else:
                st["w"] = [tok]
                st["r"] = []
        for tg in touched:
            d = self.tag_last.setdefault(tg, {})
            k = (tok[0], tok[1])
            d[k] = max(d.get(k, 0), tok[2])

    def op(self, eng, fn, R=(), W=()):
        waits, touched = self._deps(eng, R, W, False)
        idx = len(self.ops[eng]) + 1
        self.ops[eng].append({"fn": fn, "waits": waits, "idx": idx, "dma": None})
        self._commit(("e", eng, idx), R, W, False, touched)

    def dma(self, q, out, in_, R=(), W=(), sem="d", join=False, **kw):
        waits, touched = self._deps(q, R, W, join)
        idx = len(self.ops[q]) + 1
        self.semcnt[sem] = self.semcnt.get(sem, 0) + 16
        fn = (lambda e, o=out, i=in_, k=kw: e.dma_start(out=o, in_=i, allow_slow_non_contiguous=True, **k))
        self.ops[q].append({"fn": fn, "waits": waits, "idx": idx, "dma": sem})
        self._commit(("s", sem, self.semcnt[sem]), R, W, join, touched)

    def retoken(self, names, sem):
        for n_ in names:
            self.res[n_]["w"] = [("s", sem, self.semcnt[sem])]

    def final_wait(self, q, sems):
        waits = [("s", s, self.semcnt[s]) for s in sems if s in self.semcnt]
        idx = len(self.ops[q]) + 1
        self.ops[q].append({"fn": None, "waits": waits, "idx": idx, "dma": None})

    def emit(self, es):
        nc = self.nc
        semh = {}
        for e in ENG_ATTR:
            semh[("e", e)] = es.enter_context(nc.semaphore("sem_" + e))
        for s in self.semcnt:
            semh[("s", s)] = es.enter_context(nc.semaphore("dsem_" + s))
        cnt = {}
        for e in ENG_ATTR:
            m = sorted(self.marked[e])
            cnt[e] = {idx: i + 1 for i, idx in enumerate(m)}
        block = es.enter_context(nc.Block())

        def replay(ename, eng):
            for o in self.ops[ename]:
                for kind, key, val in o["waits"]:
                    if kind == "e":
                        eng.wait_ge(semh[("e", key)], cnt[key][val])
                    else:
                        eng.wait_ge(semh[("s", key)], val)
                if o["fn"] is None:
                    continue
                ins = o["fn"](eng)
                if o["dma"] is not None:
                    ins.then_inc(semh[("s", o["dma"])], 16)
                elif o["idx"] in cnt[ename]:
                    ins.then_inc(semh[("e", ename)], 1)

        @block.tensor
        def _(e):
            replay("pe", e)

        @block.scalar
        def _(e):
            replay("act", e)

        @block.vector
        def _(e):
            replay("dve", e)

        @block.gpsimd
        def _(e):
            replay("pool", e)

        @block.sync
        def _(e):
            replay("sp", e)


def host_consts():
    c32 = np.zeros((128, 1024), np.float32)
    t = np.arange(128)
    c32[:, 0:128] = np.eye(128)
    c32[:, 128:256] = (t[:, None] <= t[None, :])
    c32[:, 256:384] = -1.0 * (t[:, None] <= t[None, :])
    c32[:, 384:512] = 1.0
    mk = np.where(t[None, :] < t[:, None], NEG, 0.0)
    c32[:, 512:1024] = np.tile(mk, (1, 4))
    cb = np.zeros((128, 1024), np.float32)
    cb[:, 384:512] = 1.0
    cb[:, 512:1024] = c32[:, 512:1024]
    cb[:, 0:128] = np.eye(128)
    cb[:, 128:256] = (t[:, None] // 64 == t[None, :] // 64)
    rt = np.zeros((128, 128), np.float32)
    for m in range(128):
        r = m % 64
        if r < 8:
            rt[m + 8, m] = -1.0
        elif r < 16:
            rt[m - 8, m] = 1.0
    cb[:, 256:384] = rt
    pos = np.arange(SEQ, dtype=np.float32)
    inv_freq = (500000.0 ** (-np.arange(0, 16, 2, dtype=np.float32) / 16)).astype(np.float32)
    ang = (pos[:, None] * inv_freq[None, :]).astype(np.float32)
    cosT = np.ones((128, SEQ), np.float32)
    sinT = np.zeros((128, SEQ), np.float32)
    for p in range(128):
        r = p % 64
        if r < 16:
            cosT[p] = np.cos(ang[:, r % 8])
            sinT[p] = np.sin(ang[:, r % 8])
    return c32, cb, cosT, sinT


def build_program(nseq=NSEQ_CORE, nblk=SEQ // BLK, debug=None):
    nc = bass.Bass("TRN2", target_bir_lowering=False)
    es = ExitStack()
    ntok = nseq * SEQ

    def din(name, shape):
        return nc.dram_tensor(name, list(shape), F32, kind="ExternalInput").ap()

    x_d = din("x", [ntok, D])
    out_d = nc.dram_tensor("out", [ntok, D], F32, kind="ExternalOutput").ap()
    w_in_d = din("w_in", [D, IN_DIM])
    w_out_d = din("w_out", [2 * D, D])
    w_g_d = din("w_gate", [D, DFF])
    w_u_d = din("w_up", [D, DFF])
    w_d_d = din("w_down", [DFF, D])
    norm1_d = din("norm1_w", [1, D]); norm2_d = din("norm2_w", [1, D]); ssdn_d = din("ssd_norm_w", [1, D])
    convw_d = din("conv_w", [4, 1536]); convb_d = din("conv_b", [1, 1536])
    dtb_d = din("dt_bias", [1, 16]); alog_d = din("a_log", [1, 16]); dsk_d = din("d_skip", [1, 16])
    qnw_d = din("q_norm_w", [1, 64]); knw_d = din("k_norm_w", [1, 64])
    lq1_d = din("lambda_q1", [1, 64]); lk1_d = din("lambda_k1", [1, 64])
    lq2_d = din("lambda_q2", [1, 64]); lk2_d = din("lambda_k2", [1, 64])
    subln_d = din("subln_w", [1, 128])
    c32_d = din("c32", [128, 1024]); cbf_d = din("cbf", [128, 1024])
    cos_d = din("cosT", [128, SEQ]); sin_d = din("sinT", [128, SEQ])

    def dscr(name, shape):
        return nc.dram_tensor(name, list(shape), BF16, kind="Internal").ap()

    wb_in = dscr("wb_in", [D, IN_DIM]); wb_out = dscr("wb_out", [2 * D, D])
    wb_g = dscr("wb_g", [D, DFF]); wb_u = dscr("wb_u", [D, DFF]); wb_d = dscr("wb_d", [DFF, D])
    dbg_d = None
    if debug is not None and debug[0] == "hT":
        dbg_d = nc.dram_tensor("dbg", list(debug[1]), F32, kind="ExternalOutput").ap()

    P = Prog(nc)

    def sb(name, shape, dt=F32):
        return es.enter_context(nc.sbuf_tensor("s_" + name, list(shape), dt))

    c32 = sb("c32", [128, 1024]); cbf = sb("cbf", [128, 1024], BF16)
    ident32 = c32[:, 0:128]; tri32 = c32[:, 128:256]; negtri32 = c32[:, 256:384]; ones32 = c32[:, 384:512]
    mask4 = c32[:, 512:1024]
    identb = cbf[:, 0:128]; blockones = cbf[:, 128:256]; rotT = cbf[:, 256:384]; onesb = cbf[:, 384:512]; mask4b = cbf[:, 512:1024]
    diagw = sb("diagw", [128, 12, 4, 128], BF16)
    w1T = sb("w1T", [128, 8]); w2T = sb("w2T", [128, 8]); wssdT = sb("wssdT", [128, 8])
    convwT = sb("convwT", [128, 4, 12]); convbT = sb("convbT", [128, 12])
    wq2 = sb("wq2", [128, 1]); wk2 = sb("wk2", [128, 1])
    sublnw = sb("sublnw", [128, 128]); subw = sb("subw", [128, 1])
    dtb4 = sb("dtb4", [128, 64]); a4 = sb("a4", [128, 64]); dsk = sb("dsk", [128, 16])
    lam4 = sb("lam4", [128, 4, 64]); lamt = sb("lamt", [128, 8]); neglam = sb("neglam", [128, 1])
    cosb = sb("cosb", [128, BLK]); sinb = sb("sinb", [128, BLK])
    wdt = sb("wdt", [128, 8, 16], BF16)
    kT = sb("kT", [128, 8, SEQ], BF16)
    vaug = sb("vaug", [128, 16, 8, 130], BF16)
    st = sb("st", [128, 1024]); stbf = sb("stbf", [128, 1024], BF16)
    xt = [sb(f"xt{i}", [128, 1024]) for i in range(2)]
    hT = sb("hT", [128, 8, BLK], BF16)
    yT = sb("yT", [128, 16, BLK], BF16)
    NW = 2
    wg = [sb(f"wg{i}", [128, 8, 512], BF16) for i in range(NW)]
    junk = sb("junk", [128, 1024], BF16)
    xn = [sb(f"xn{i}", [128, 1024], BF16) for i in range(2)]
    stat = sb("stat", [128, 64])
    halo = sb("halo", [128, 12, 4], BF16)
    epsc = sb("epsc", [128, 2])
    REG = 55 * 1024 + 512
    region = sb("region", [128, REG // 4])
    roff = {"S": 0, "T": 0, "F": 0}

    def rg(phase, name, shape, dt=F32):
        n = int(np.prod(shape[1:]))
        nb = n * (4 if dt == F32 else 2)
        nb = (nb + 31) // 32 * 32
        o = roff[phase]
        roff[phase] = o + nb
        assert roff[phase] <= REG, (phase, name, roff[phase])
        ap = region[:, o // 4:(o + nb) // 4]
        if dt != F32:
            ap = ap.bitcast(dt)
        ap = ap[:, 0:n]
        if len(shape) == 3:
            ap = ap.rearrange("p (a b) -> p a b", b=shape[2])
        elif len(shape) == 4:
            ap = ap.rearrange("p (a b c) -> p a b c", b=shape[2], c=shape[3])
        P.settag(name, phase)
        return ap

    zs = rg("S", "zs", [128, 4, 1024], BF16)
    xbcT = rg("S", "xbcT", [128, 12, 516], BF16)
    xsT = rg("S", "xsT", [128, 8, BLK], BF16)
    bcT = rg("S", "bcT", [128, 4, BLK], BF16)
    dtt = rg("S", "dtt", [128, 64]); adt = rg("S", "adt", [128, 64])
    xdt = rg("S", "xdt", [128, 1024], BF16); xsD = rg("S", "xsD", [128, 1024], BF16)
    R1 = [rg("S", f"R1{i}", [128, 4, 128]) for i in range(2)]
    dec = [rg("S", f"dec{i}", [128, 4, 128], BF16) for i in range(2)]
    MT = [rg("S", f"MT{i}", [128, 4, 128], BF16) for i in range(2)]
    cbT = rg("S", "cbT", [128, 256], BF16)
    ybuf = rg("S", "ybuf", [128, 1024])
    sp_t = ybuf[:, 0:256].rearrange("p (a b) -> p a b", b=64)
    gn = rg("S", "gn", [128, 1024], BF16)
    xdtE = rg("S", "xdtE", [128, 1024], BF16)
    Btok = rg("S", "Btok", [128, 256], BF16)
    acs = rg("S", "acs", [128, 96])
    qT = rg("T", "qT", [128, 8, BLK], BF16)
    sqb = [rg("T", f"sqb{i}", [128, BLK], BF16) for i in range(2)]
    rtb = [rg("T", f"rtb{i}", [128, BLK]) for i in range(2)]
    rinv = [rg("T", f"rinv{i}", [128, BLK]) for i in range(2)]
    qn = [rg("T", f"qn{i}", [128, BLK]) for i in range(2)]
    qnb = [rg("T", f"qnb{i}", [128, BLK], BF16) for i in range(2)]
    t1 = [rg("T", f"t1{i}", [128, BLK]) for i in range(2)]
    t2 = [rg("T", f"t2{i}", [128, BLK]) for i in range(2)]
    pT = [rg("T", f"pT{i}", [128, BLK], BF16) for i in range(6)]
    oS = [rg("T", f"oS{i}", [128, BLK]) for i in range(2)]
    pS = [[rg("T", f"pS{a}{c}", [128, BLK]) for c in range(2)] for a in range(2)]
    aT = rg("F", "aT", [128, 22, BLK], BF16)
    sg = rg("F", "sg", [128, BLK])
    x1 = [rg("F", f"x1_{i}", [128, 1024]) for i in range(4)]

    banks = [es.enter_context(nc.psum_tensor(f"ps{i}", [128, 512], F32)) for i in range(8)]
    rr = [0]

    def psum():
        i = 4 + rr[0] % 4
        rr[0] += 1
        return banks[i], f"ps{i}"

    def acc(i):
        return banks[i], f"ps{i}"

    rr6 = [0]

    def psum6():
        i = 3 + rr6[0] % 5
        rr6[0] += 1
        return banks[i], f"ps{i}"

    rr8 = [0]

    def psum8():
        i = rr8[0] % 8
        rr8[0] += 1
        return banks[i], f"ps{i}"

    def mm(out, lhsT, rhs, start, stop, R, W, sgc=False):
        P.op("pe", lambda e: e.matmul(out, lhsT, rhs, start=start, stop=stop, skip_group_check=sgc), R=R, W=W)

    def tr(out, in_, ident, R, W):
        P.op("pe", lambda e: e.transpose(out, in_, ident), R=R, W=W)

    def act(out, in_, func, R, W, scale=1.0, bias=None):
        if bias is None:
            P.op("act", lambda e: e.activation(out, in_, func, scale=scale), R=R, W=W)
        else:
            P.op("act", lambda e: e.activation(out, in_, func, bias=bias, scale=scale), R=R, W=W)

    def tt(eng, out, in0, in1, op, R, W):
        P.op(eng, lambda e: e.tensor_tensor(out, in0, in1, op), R=R, W=W)

    def ts(eng, out, in0, s1, s2, op0, op1, R, W):
        P.op(eng, lambda e: e.tensor_scalar(out, in0, s1, s2, op0, op1), R=R, W=W)

    def stt(out, in0, scalar, in1, op0, op1, R, W):
        P.op("dve", lambda e: e.scalar_tensor_tensor(out, in0, scalar, in1, op0, op1), R=R, W=W)

    def red(out, in_, R, W):
        P.op("dve", lambda e: e.tensor_reduce(out, in_, AX.X, ALU.add), R=R, W=W)

    def recip(out, in_, R, W):
        P.op("dve", lambda e: e.reciprocal(out, in_), R=R, W=W)

    def cp(eng, out, in_, R, W):
        if eng == "act":
            P.op("act", lambda e: e.copy(out, in_), R=R, W=W)
        else:
            P.op(eng, lambda e: e.tensor_copy(out, in_), R=R, W=W)

    def mset(eng, ap, val, W):
        P.op(eng, lambda e: e.memset(ap, val), W=W)

    def rstd_from_ss(ss_ap, n, inv_n, name):
        act(ss_ap, ss_ap, AF.Ln, R=[name, "epsc"], W=[name], scale=inv_n, bias=epsc[:, 0:1])
        act(ss_ap, ss_ap, AF.Exp, R=[name], W=[name], scale=-0.5)

    ncd = nc.allow_non_contiguous_dma(reason="tiny parameter layout loads")
    ncd.__enter__()
    for (src, dst, rows, key) in ((w_in_d, wb_in, D, "wb_in"), (w_out_d, wb_out, 2 * D, "wb_out"),
                                  (w_g_d, wb_g, D, "wb_g"), (w_u_d, wb_u, D, "wb_u"),
                                  (w_d_d, wb_d, DFF, "wb_d")):
        for r0 in range(0, rows, 128):
            P.dma("pool", dst[r0:r0 + 128, :], src[r0:r0 + 128, :], W=[key], sem=key, join=True)
    P.dma("sp", c32[:, :], c32_d[:, :], W=["c32"], sem="cst")
    P.dma("pool", cbf[:, :], cbf_d[:, :], W=["cbf"], sem="cstb")
    P.dma("sp", w1T[:, :], norm1_d[0, :].rearrange("(d p) -> p d", p=128), W=["w1T"], sem="cst")
    P.dma("sp", w2T[:, :], norm2_d[0, :].rearrange("(d p) -> p d", p=128), W=["w2T"], sem="cst")
    P.dma("sp", wssdT[:, :], ssdn_d[0, :].rearrange("(d p) -> p d", p=128), W=["wssdT"], sem="cst")
    for k in range(4):
        P.dma("sp", convwT[:, k, :], convw_d[k, :].rearrange("(c p) -> p c", p=128), W=["convwT"], sem="cst", join=True)
    P.dma("sp", convbT[:, :], convb_d[0, :].rearrange("(c p) -> p c", p=128), W=["convbT"], sem="cst")
    for hh in range(2):
        P.dma("sp", wq2[hh * 64:(hh + 1) * 64, :], qnw_d[0, :].rearrange("(p o) -> p o", o=1), W=["wq2"], sem="cst", join=True)
        P.dma("sp", wk2[hh * 64:(hh + 1) * 64, :], knw_d[0, :].rearrange("(p o) -> p o", o=1), W=["wk2"], sem="cst", join=True)
    P.dma("sp", sublnw[:, :], subln_d[0:1, :].broadcast_to([128, 128]), W=["sublnw"], sem="cst")
    P.dma("sp", subw[:, :], subln_d[0, :].rearrange("(p o) -> p o", o=1), W=["subw"], sem="cst")
    for i in range(4):
        P.dma("sp", dtb4[:, i * 16:(i + 1) * 16], dtb_d[0:1, :].broadcast_to([128, 16]), W=["dtb4"], sem="cst", join=True)
        P.dma("sp", a4[:, i * 16:(i + 1) * 16], alog_d[0:1, :].broadcast_to([128, 16]), W=["a4"], sem="cst", join=True)
    P.dma("sp", dsk[:, :], dsk_d[0:1, :].broadcast_to([128, 16]), W=["dsk"], sem="cst")
    for i, ld in enumerate((lq1_d, lk1_d, lq2_d, lk2_d)):
        P.dma("sp", lam4[:, i, :], ld[0:1, :].broadcast_to([128, 64]), W=["lam4"], sem="cst", join=True)
    P.dma("sp", wdt[:, :, :], wb_in.rearrange("(d p) c -> p d c", p=128)[:, :, 2560:2576], R=["wb_in"], W=["wdt"], sem="cst")
    ncd.__exit__(None, None, None)
    P.retoken(["c32", "w1T", "w2T", "wssdT", "convwT", "convbT", "wq2", "wk2", "sublnw", "subw", "dtb4", "a4", "dsk", "lam4", "wdt"], "cst")
    act(a4[:, :], a4[:, :], AF.Exp, R=["a4"], W=["a4"])
    ts("dve", a4[:, :], a4[:, :], -1.0, None, ALU.mult, ALU.bypass, R=["a4"], W=["a4"])
    tt("dve", lam4[:, 0, :], lam4[:, 0, :], lam4[:, 1, :], ALU.mult, R=["lam4"], W=["lam4"])
    tt("dve", lam4[:, 2, :], lam4[:, 2, :], lam4[:, 3, :], ALU.mult, R=["lam4"], W=["lam4"])
    red(lamt[:, 0:1], lam4[:, 0, :], R=["lam4"], W=["lamt"])
    red(lamt[:, 1:2], lam4[:, 2, :], R=["lam4"], W=["lamt"])
    act(lamt[:, 2:4], lamt[:, 0:2], AF.Exp, R=["lamt"], W=["lamt"])
    tt("dve", lamt[:, 4:5], lamt[:, 3:4], lamt[:, 2:3], ALU.subtract, R=["lamt"], W=["lamt"])
    ts("dve", neglam[:, :], lamt[:, 4:5], -LAMBDA_INIT, None, ALU.add, ALU.bypass, R=["lamt"], W=["neglam"])
    ts("dve", sublnw[:, :], sublnw[:, :], 1.0 - LAMBDA_INIT, None, ALU.mult, ALU.bypass, R=["sublnw"], W=["sublnw"])
    ts("dve", subw[:, :], subw[:, :], 1.0 - LAMBDA_INIT, None, ALU.mult, ALU.bypass, R=["subw"], W=["subw"])
    for c in range(12):
        for k in range(4):
            ts("dve", diagw[:, c, k, :], ident32, convwT[:, k, c:c + 1], None, ALU.mult, ALU.bypass,
               R=["c32", "convwT"], W=["diagw"])
    mset("pool", vaug[:, :, :, 128:130], 1.0, W=["vaug"])
    mset("pool", epsc[:, :], EPS, W=["epsc"])

    wctr = [0]

    def wload(src_ap, nk=8, ncol=512, key="wb_in"):
        i = wctr[0] % NW
        wctr[0] += 1
        P.dma("sp", wg[i][:, 0:nk, 0:ncol], src_ap, R=[key], W=[f"wg{i}"], sem=f"wg{i}")
        return wg[i], f"wg{i}"

    wv_in = wb_in.rearrange("(d p) c -> p d c", p=128)
    wv_out = wb_out.rearrange("(k p) c -> p k c", p=128)
    wv_g = wb_g.rearrange("(d p) c -> p d c", p=128)
    wv_u = wb_u.rearrange("(d p) c -> p d c", p=128)
    wv_d = wb_d.rearrange("(k p) c -> p k c", p=128)

    def fm_chunk(w, wr, j, src, srcname):
        ps, pr = psum()
        for d in range(8):
            mm(ps[:, :], w[:, d, j * 128:(j + 1) * 128], src[:, d, :], d == 0, d == 7, R=[wr, srcname], W=[pr])
        return ps, pr

    def tm_tile(w, wr, i, ncol=512):
        ps, pr = psum()
        for d in range(8):
            mm(ps[:, 0:ncol], hT[:, d, i * 128:(i + 1) * 128], w[:, d, 0:ncol], d == 0, d == 7, R=[wr, "hT"], W=[pr])
        return ps, pr

    def norm_to_hT(src_tiles, src_names, wT, wTname):
        for i in range(4):
            act(junk[:, :], src_tiles[i], AF.Square, R=[src_names[i]], W=["junk"])
            red(stat[:, i:i + 1], junk[:, :], R=["junk"], W=["stat"])
        rstd_from_ss(stat[:, 0:4], 4, 1.0 / D, "stat")
        for i in range(4):
            xb = xn[i % 2]
            ts("dve", xb[:, :], src_tiles[i], stat[:, i:i + 1], None, ALU.mult, ALU.bypass,
               R=[src_names[i], "stat"], W=[f"xn{i % 2}"])
            ps, pr = psum()
            pb = ps[:, :].bitcast(BF16)
            for d in range(8):
                tr(pb[:, d * 128:(d + 1) * 128], xb[:, d * 128:(d + 1) * 128], identb, R=[f"xn{i % 2}", "cbf"], W=[pr])
            tt("dve", hT[:, :, i * 128:(i + 1) * 128], pb.rearrange("p (d t) -> p d t", t=128),
               wT[:, :].unsqueeze(2).broadcast_to([128, 8, 128]), ALU.mult, R=[pr, wTname], W=["hT"])

    def dump(ap, resname, rows=None):
        P.dma("sp", dbg_d if rows is None else dbg_d[rows], ap, R=[resname], W=["dbgout"], sem="dbg", join=True)

    def step1(s, b, tiles, part):
        tok0 = s * SEQ + b * BLK
        for i in tiles:
            xb = xn[i % 2]
            if "a" in part:
                P.dma("sp", xt[i % 2][:, :], x_d[tok0 + i * 128: tok0 + (i + 1) * 128, :], W=[f"xt{i % 2}"], sem=f"xt{i % 2}")
                act(junk[:, :], xt[i % 2][:, :], AF.Square, R=[f"xt{i % 2}"], W=["junk"])
                red(stat[:, 8 + i:9 + i], junk[:, :], R=["junk"], W=["stat"])
                rstd_from_ss(stat[:, 8 + i:9 + i], 1, 1.0 / D, "stat")
                ts("dve", xb[:, :], xt[i % 2][:, :], stat[:, 8 + i:9 + i], None, ALU.mult, ALU.bypass,
                   R=[f"xt{i % 2}", "stat"], W=[f"xn{i % 2}"])
            if "b" in part:
                ps, pr = psum()
                pb = ps[:, :].bitcast(BF16)
                for d in range(8):
                    tr(pb[:, d * 128:(d + 1) * 128], xb[:, d * 128:(d + 1) * 128], identb, R=[f"xn{i % 2}", "cbf"], W=[pr])
                tt("dve", hT[:, :, i * 128:(i + 1) * 128], pb.rearrange("p (d t) -> p d t", t=128),
                   w1T[:, :].unsqueeze(2).broadcast_to([128, 8, 128]), ALU.mult, R=[pr, "w1T"], W=["hT"])

    def emit_block(s, b, nxt):
        tok0 = s * SEQ + b * BLK

        for g in range(2):
            w, wr = wload(wv_in[:, :, g * 512:(g + 1) * 512])
            for i in range(4):
                ps, pr = tm_tile(w, wr, i)
                act(zs[:, i, g * 512:(g + 1) * 512], ps[:, :], AF.Silu, R=[pr], W=["zs"])
        if b == 0:
            mset("pool", xbcT[:, :, 0:3], 0.0, W=["xbcT"])
        else:
            cp("dve", xbcT[:, :, 0:3], halo[:, :, 0:3], R=["halo"], W=["xbcT"])
        for g in range(3):
            w, wr = wload(wv_in[:, :, 1024 + g * 512: 1024 + (g + 1) * 512])
            for j in range(4):
                ps, pr = fm_chunk(w, wr, j, hT, "hT")
                cp("act", xbcT[:, g * 4 + j, 3:515], ps[:, :], R=[pr], W=["xbcT"])
        cp("dve", halo[:, :, 0:3], xbcT[:, :, 512:515], R=["xbcT"], W=["halo"])
        psd, pdr = psum()
        for i in range(4):
            for d in range(8):
                mm(psd[:, i * 16:(i + 1) * 16], hT[:, d, i * 128:(i + 1) * 128], wdt[:, d, :], d == 0, d == 7,
                   R=["hT", "wdt"], W=[pdr])
        tt("dve", sp_t[:, 0, :], psd[:, 0:64], dtb4[:, :], ALU.add, R=[pdr, "dtb4"], W=["ybuf"])
        act(sp_t[:, 1, :], sp_t[:, 0, :], AF.Abs, R=["ybuf"], W=["ybuf"])
        act(sp_t[:, 2, :], sp_t[:, 1, :], AF.Exp, R=["ybuf"], W=["ybuf"], scale=-1.0)
        ts("dve", sp_t[:, 2, :], sp_t[:, 2, :], 1.0, None, ALU.add, ALU.bypass, R=["ybuf"], W=["ybuf"])
        act(sp_t[:, 3, :], sp_t[:, 2, :], AF.Ln, R=["ybuf"], W=["ybuf"])
        stt(dtt[:, :], sp_t[:, 0, :], 0.0, sp_t[:, 3, :], ALU.max, ALU.add, R=["ybuf"], W=["dtt"])
        tt("dve", adt[:, :], dtt[:, :], a4[:, :], ALU.mult, R=["dtt", "a4"], W=["adt"])
        for c in range(12):
            ps, pr = psum()
            for k in range(4):
                mm(ps[:, :], diagw[:, c, k, :], xbcT[:, c, k:k + 512], k == 0, k == 3, R=["diagw", "xbcT"], W=[pr])
            if c < 8:
                act(xsT[:, c, :], ps[:, :], AF.Silu, R=[pr, "convbT"], W=["xsT"], bias=convbT[:, c:c + 1])
            else:
                act(bcT[:, c - 8, :], ps[:, :], AF.Silu, R=[pr, "convbT"], W=["bcT"], bias=convbT[:, c:c + 1])
        if b == 0:
            mset("pool", st[:, :], 0.0, W=["st"])
            mset("pool", stbf[:, :], 0.0, W=["stbf"])
        def tsl_(i):
            return slice(i * 128, (i + 1) * 128)

        def prepA(i):
            tsl = tsl_(i)
            hsl = slice(i * 16, (i + 1) * 16)
            psx, pxr = psum()
            pxb = psx[:, :].bitcast(BF16)
            for cc in range(8):
                tr(pxb[:, cc * 128:(cc + 1) * 128], xsT[:, cc, tsl], identb, R=["xsT", "cbf"], W=[pxr])
            px3 = pxb.rearrange("p (h d) -> p h d", d=64)
            tt("dve", xdt.rearrange("p (h d) -> p h d", d=64), px3,
               dtt[:, hsl].unsqueeze(2).broadcast_to([128, 16, 64]), ALU.mult, R=[pxr, "dtt"], W=["xdt"])
            tt("dve", xsD.rearrange("p (h d) -> p h d", d=64), px3,
               dsk[:, :].unsqueeze(2).broadcast_to([128, 16, 64]), ALU.mult, R=[pxr, "dsk"], W=["xsD"])
            psa, par = psum()
            mm(psa[:, 0:16], tri32, adt[:, hsl], True, True, R=["c32", "adt"], W=[par])
            mm(psa[:, 16:32], ones32, adt[:, hsl], True, True, R=["c32", "adt"], W=[par])
            act(acs[:, 0:16], psa[:, 0:16], AF.Exp, R=[par], W=["acs"])
            cp("act", acs[:, 16:32], psa[:, 16:32], R=[par], W=["acs"])
            tt("dve", acs[:, 32:48], acs[:, 16:32], psa[:, 0:16], ALU.subtract, R=["acs", par], W=["acs"])
            act(acs[:, 48:64], acs[:, 32:48], AF.Exp, R=["acs"], W=["acs"])
            act(acs[:, 64:80], acs[:, 16:32], AF.Exp, R=["acs"], W=["acs"])
            psc, pcr = psum()
            for g in range(2):
                mm(psc[:, g * 128:(g + 1) * 128], bcT[:, g, tsl], bcT[:, 2 + g, tsl], True, True, R=["bcT"], W=[pcr])
            cp("act", cbT[:, :], psc[:, 0:256], R=[pcr], W=["cbT"])
            tt("pool", xdtE.rearrange("p (h d) -> p h d", d=64), xdt.rearrange("p (h d) -> p h d", d=64),
               acs[:, 48:64].unsqueeze(2).broadcast_to([128, 16, 64]), ALU.mult, R=["xdt", "acs"], W=["xdtE"])
            psb, pbr2 = psum()
            pbb = psb[:, :].bitcast(BF16)
            for g in range(2):
                tr(pbb[:, g * 128:(g + 1) * 128], bcT[:, g, tsl], identb, R=["bcT", "cbf"], W=[pbr2])
            cp("act", Btok[:, :], pbb[:, 0:256], R=[pbr2], W=["Btok"])

        seg_ps = {}

        def segR1(i, hg):
            h0 = i * 16 + hg * 4
            k2 = hg % 2
            tt("dve", R1[k2][:, :, :], tri32.unsqueeze(1).broadcast_to([128, 4, 128]),
               adt[:, h0:h0 + 4].unsqueeze(2).broadcast_to([128, 4, 128]), ALU.mult, R=["c32", "adt"], W=[f"R1{k2}"])

        def segMM(i, hg):
            h0 = i * 16 + hg * 4
            k2 = hg % 2
            pss, psr = psum()
            mm(pss[:, :], ones32, R1[k2].rearrange("p a b -> p (a b)"), True, False, R=["c32", f"R1{k2}"], W=[psr])
            mm(pss[:, :].rearrange("p (a b) -> p a b", b=128), negtri32, adt[:, h0:h0 + 4].unsqueeze(2).broadcast_to([128, 4, 128]),
               False, False, R=["c32", "adt"], W=[psr])
            mm(pss[:, :], identb, mask4b, False, True, R=["cbf"], W=[psr])
            seg_ps[hg] = (pss, psr)

        def segFin(i, hg):
            k2 = hg % 2
            pss, psr = seg_ps[hg]
            act(dec[k2].rearrange("p a b -> p (a b)"), pss[:, :], AF.Exp, R=[psr], W=[f"dec{k2}"])
            g = hg // 2
            m = MT[k2]
            tt("dve", m[:, :, :], dec[k2][:, :, :], cbT[:, g * 128:(g + 1) * 128].unsqueeze(1).broadcast_to([128, 4, 128]),
               ALU.mult, R=[f"dec{k2}", "cbT"], W=[f"MT{k2}"])
            for hh in range(4):
                h = hg * 4 + hh
                pb_, pbr = acc(h // 8)
                mm(pb_[:, (h % 8) * 64:(h % 8 + 1) * 64], m[:, hh, :], xdt[:, h * 64:(h + 1) * 64], True, True,
                   R=[f"MT{k2}", "xdt"], W=[pbr])

        def prepB(i, hooks=()):
            segR1(i, 0)
            segMM(i, 0)
            segR1(i, 1)
            for hg in range(4):
                if hg + 1 < 4:
                    segMM(i, hg + 1)
                if hg + 2 < 4:
                    segR1(i, hg + 2)
                segFin(i, hg)
                if hg < len(hooks):
                    hooks[hg]()

        def yoff(i):
            tsl = tsl_(i)
            for g in range(2):
                bk, bkr = acc(2 + g)
                mm(bk[:, :], bcT[:, 2 + g, tsl], stbf[:, g * 512:(g + 1) * 512], True, True, R=["bcT", "stbf"], W=[bkr])

        upd_ps = {}

        def updMM(i):
            pst2 = [psum(), psum()]
            for g in range(2):
                mm(pst2[g][0][:, :], Btok[:, g * 128:(g + 1) * 128], xdtE[:, g * 512:(g + 1) * 512], True, True,
                   R=["Btok", "xdtE"], W=[pst2[g][1]])
            upd_ps[i] = pst2

        def U1(i):
            pst2 = upd_ps[i]
            for g in range(2):
                gs = slice(g * 512, (g + 1) * 512)
                tt("dve", st[:, gs].rearrange("p (h d) -> p h d", d=64), st[:, gs].rearrange("p (h d) -> p h d", d=64),
                   acs[:, 64 + g * 8:64 + (g + 1) * 8].unsqueeze(2).broadcast_to([128, 8, 64]), ALU.mult, R=["st", "acs"], W=["st"])
                tt("dve", st[:, gs], pst2[g][0][:, :], st[:, gs], ALU.add, R=[pst2[g][1], "st"], W=["st"])
            cp("act", stbf[:, :], st[:, :], R=["st"], W=["stbf"])

        def F1(i):
            for g in range(2):
                gs = slice(g * 512, (g + 1) * 512)
                po, por_ = acc(2 + g)
                py, pyr_ = acc(g)
                tt("dve", ybuf[:, gs].rearrange("p (h d) -> p h d", d=64), po[:, :].rearrange("p (h d) -> p h d", d=64),
                   acs[:, g * 8:(g + 1) * 8].unsqueeze(2).broadcast_to([128, 8, 64]), ALU.mult, R=[por_, "acs"], W=["ybuf"])
                tt("dve", ybuf[:, gs], py[:, :], ybuf[:, gs], ALU.add, R=[pyr_, "ybuf"], W=["ybuf"])
            tt("dve", ybuf[:, :], ybuf[:, :], xsD[:, :], ALU.add, R=["ybuf", "xsD"], W=["ybuf"])

        def F2(i):
            tt("pool", ybuf[:, :], ybuf[:, :], zs[:, i, :], ALU.mult, R=["ybuf", "zs"], W=["ybuf"])
            act(junk[:, :], ybuf[:, :], AF.Square, R=["ybuf"], W=["junk"])

        def F3(i):
            red(stat[:, 16:18], junk[:, :].rearrange("p (g d) -> p g d", d=512), R=["junk"], W=["stat"])
            rstd_from_ss(stat[:, 16:18], 2, 1.0 / 512, "stat")

        def F4(i):
            for g in range(2):
                gs = slice(g * 512, (g + 1) * 512)
                ts("dve", gn[:, gs], ybuf[:, gs], stat[:, 16 + g:17 + g], None, ALU.mult, ALU.bypass, R=["ybuf", "stat"], W=["gn"])

        def finB(i):
            tsl = tsl_(i)
            pst, ptr_ = psum()
            ptb = pst[:, :].bitcast(BF16)
            for cc in range(8):
                tr(ptb[:, cc * 128:(cc + 1) * 128], gn[:, cc * 128:(cc + 1) * 128], identb, R=["gn", "cbf"], W=[ptr_])
            tt("dve", yT[:, 0:8, tsl], ptb.rearrange("p (c t) -> p c t", t=128),
               wssdT[:, :].unsqueeze(2).broadcast_to([128, 8, 128]), ALU.mult, R=[ptr_, "wssdT"], W=["yT"])

        prepA(0)
        prepB(0)
        for i in range(4):
            yoff(i)
            updMM(i)
            F1(i)
            U1(i)
            if i + 1 < 4:
                prepA(i + 1)
                prepB(i + 1, hooks=[(lambda ii=i: F2(ii)), (lambda ii=i: F3(ii)), (lambda ii=i: F4(ii))])
            else:
                F2(i); F3(i); F4(i)
            finB(i)

        P.dma("sp", cosb[:, :], cos_d[:, b * BLK:(b + 1) * BLK], W=["cosb"], sem="cos")
        P.dma("sp", sinb[:, :], sin_d[:, b * BLK:(b + 1) * BLK], W=["sinb"], sem="sin")

        qk_state = {}

        def qkA(c):
            kind, g, j = ("q", c // 4, c % 4) if c < 8 else ("k", (c - 8) // 4, c % 4)
            if j == 0:
                base = 2576 if kind == "q" else 3600
                qk_state["w"] = wload(wv_in[:, :, base + g * 512: base + (g + 1) * 512])
            w, wr = qk_state["w"]
            ps, pr = psum8()
            for d in range(8):
                mm(ps[:, :], w[:, d, j * 128:(j + 1) * 128], hT[:, d, :], d == 0, d == 7, R=[wr, "hT"], W=[pr])
            k = c % 2
            act(sqb[k][:, :], ps[:, :], AF.Square, R=[pr], W=[f"sqb{k}"])
            qk_state[c] = (ps, pr)

        def qkB(c):
            k = c % 2
            ps, pr = qk_state[c]
            wvec, wname = (wq2, "wq2") if c < 8 else (wk2, "wk2")
            ps2, pr2 = psum8()
            mm(ps2[:, :], blockones, sqb[k][:, :], True, True, R=["cbf", f"sqb{k}"], W=[pr2])
            act(rtb[k][:, :], ps2[:, :], AF.Ln, R=[pr2, "epsc"], W=[f"rtb{k}"], scale=1.0 / 64, bias=epsc[:, 0:1])
            act(rinv[k][:, :], rtb[k][:, :], AF.Exp, R=[f"rtb{k}"], W=[f"rinv{k}"], scale=-0.5)
            stt(qn[k][:, :], ps[:, :], wvec[:, 0:1], rinv[k][:, :], ALU.mult, ALU.mult, R=[pr, wname, f"rinv{k}"], W=[f"qn{k}"])
            cp("act", qnb[k][:, :], qn[k][:, :], R=[f"qn{k}"], W=[f"qnb{k}"])

        def qkC(c):
            k = c % 2
            if c < 8:
                dest, dname = qT[:, c, :], "qT"
            else:
                dest, dname = kT[:, c - 8, b * BLK:(b + 1) * BLK], "kT"
            ps3, pr3 = psum8()
            mm(ps3[:, :], rotT, qnb[k][:, :], True, True, R=["cbf", f"qnb{k}"], W=[pr3])
            tt("dve", t1[k][:, :], qn[k][:, :], cosb[:, :], ALU.mult, R=[f"qn{k}", "cosb"], W=[f"t1{k}"])
            tt("dve", t2[k][:, :], ps3[:, :], sinb[:, :], ALU.mult, R=[pr3, "sinb"], W=[f"t2{k}"])
            tt("pool", dest, t1[k][:, :], t2[k][:, :], ALU.add, R=[f"t1{k}", f"t2{k}"], W=[dname])

        for step in range(18):
            if step < 16:
                qkA(step)
            if 0 <= step - 1 < 16:
                qkB(step - 1)
            if 0 <= step - 2 < 16:
                qkC(step - 2)
        for g in range(2):
            w, wr = wload(wv_in[:, :, 4624 + g * 512: 4624 + (g + 1) * 512])
            for i in range(4):
                ps, pr = tm_tile(w, wr, i)
                cp("act", vaug[:, 4 * b + i, g * 4:(g + 1) * 4, 0:128], ps[:, :].rearrange("p (h d) -> p h d", d=128),
                   R=[pr], W=["vaug"])
        nk = 4 * b + 4
        iters = [(h, kt) for h in range(8) for kt in range(nk)]
        NI = len(iters)
        NPT = len(pT)

        def QK(n):
            h, kt = iters[n]
            qlo = max(kt - 4 * b, 0)
            nn = BLK - qlo * 128
            sps = []
            for j in range(2):
                js = slice(j * 64, (j + 1) * 64)
                sp, spr = psum6()
                mm(sp[:, 0:nn], kT[js, h, kt * 128:(kt + 1) * 128], qT[js, h, qlo * 128:BLK], True, True,
                   R=["kT", "qT"], W=[spr])
                sps.append((sp, spr))
            for j in range(2):
                sp, spr = sps[j]
                pi = (2 * n + j) % NPT
                act(pT[pi][:, 0:nn], sp[:, 0:nn], AF.Exp, R=[spr], W=[f"pT{pi}"], scale=0.125)
                if kt >= 4 * b:
                    mset("pool", pT[pi][64:128, 0:64], 0.0, W=[f"pT{pi}"])

        def PV(n):
            h, kt = iters[n]
            qlo = max(kt - 4 * b, 0)
            nn = BLK - qlo * 128
            hb = h % 2
            for j in range(2):
                pi = (2 * n + j) % NPT
                ob, obr = acc(j)
                mm(ob[:, qlo * 128:BLK], vaug[:, kt, h, 0:128], pT[pi][:, 0:nn], kt == 0, kt == nk - 1,
                   R=[f"pT{pi}", "vaug"], W=[obr], sgc=True)
                if j == 0:
                    db, dbr = acc(2)
                    mm(db[0:1, qlo * 128:BLK], onesb[:, 0:1], pT[pi][:, 0:nn], kt == 0, kt == nk - 1,
                       R=[f"pT{pi}", "cbf"], W=[dbr], sgc=True)
                    continue
                eng = "dve"
                ps_ = pS[hb][j]; psn = f"pS{hb}{j}"
                if kt == 0:
                    cp(eng, ps_[:, :], pT[pi][:, :], R=[f"pT{pi}"], W=[psn])
                else:
                    tt(eng, ps_[:, qlo * 128:BLK], ps_[:, qlo * 128:BLK], pT[pi][:, 0:nn], ALU.add, R=[psn, f"pT{pi}"], W=[psn])

        def fin1(h):
            for j in range(2):
                ob, obr = acc(j)
                cp("dve" if j == 0 else "act", oS[j][:, :], ob[:, :], R=[obr], W=[f"oS{j}"])
            db, dbr = acc(2)
            cp("act", pS[h % 2][0][0:1, :], db[0:1, :], R=[dbr], W=[f"pS{h % 2}0"])

        def fin2(h):
            hb = h % 2
            for j in range(2):
                pb_, pbr_ = psum()
                if j == 0:
                    mm(pb_[:, :], ones32[0:1, :], pS[hb][0][0:1, :], True, True, R=["c32", f"pS{hb}0"], W=[pbr_])
                else:
                    mm(pb_[:, :], ones32, pS[hb][j][:, :], True, True, R=["c32", f"pS{hb}{j}"], W=[pbr_])
                dtmp, dname = (rtb[1], "rtb1") if j == 0 else (rinv[1], "rinv1")
                act(dtmp[:, :], pb_[:, :], AF.Ln, R=[pbr_], W=[dname])
                act(t1[j][:, :], dtmp[:, :], AF.Exp, R=[dname], W=[f"t1{j}"], scale=-1.0)
            tt("dve", t2[0][:, :], oS[0][:, :], t1[0][:, :], ALU.mult, R=["oS0", "t10"], W=["t20"])
            stt(t2[1][:, :], oS[1][:, :], neglam[:, 0:1], t1[1][:, :], ALU.mult, ALU.mult, R=["oS1", "neglam", "t11"], W=["t21"])
            tt("dve", t2[1][:, :], t2[1][:, :], t2[0][:, :], ALU.add, R=["t20", "t21"], W=["t21"])
            act(sqb[0][:, :], t2[1][:, :], AF.Square, R=["t21"], W=["sqb0"])

        def fin3(h):
            pss_, psr_ = psum()
            mm(pss_[:, :], onesb, sqb[0][:, :], True, True, R=["cbf", "sqb0"], W=[psr_])
            act(rtb[0][:, :], pss_[:, :], AF.Ln, R=[psr_, "epsc"], W=["rtb0"], scale=1.0 / 128, bias=epsc[:, 0:1])
            act(rinv[0][:, :], rtb[0][:, :], AF.Exp, R=["rtb0"], W=["rinv0"], scale=-0.5)
            stt(yT[:, 8 + h, :], t2[1][:, :], subw[:, 0:1], rinv[0][:, :], ALU.mult, ALU.mult,
                R=["t21", "subw", "rinv0"], W=["yT"])

        deferred = []
        QK(0)
        if NI > 1:
            QK(1)
        for n in range(NI):
            if n + 2 < NI:
                QK(n + 2)
            PV(n)
            for item in [d_ for d_ in deferred if d_[0] <= n]:
                deferred.remove(item)
                item[1]()
            h, kt = iters[n]
            if kt == nk - 1:
                fin1(h)
                deferred.append((n + 1, (lambda hh=h: fin2(hh))))
                deferred.append((n + 3, (lambda hh=h: fin3(hh))))
        for item in sorted(deferred, key=lambda d_: d_[0]):
            item[1]()

        if debug is not None and debug[0] == "yT":
            for c in range(16):
                cp("act", st[:, 0:512], yT[:, c, :], R=["yT"], W=["st"])
                P.dma("sp", out_d[c * 128:(c + 1) * 128, 0:512], st[:, 0:512], R=["st"], W=["outd"], sem="dbg")
                P.op("act", lambda e: e.copy(stat[:, 60:61], stat[:, 60:61]), R=["st"], W=["stat"])
            if nxt is not None:
                step1(nxt[0], nxt[1], range(4), "ab")
            return
        for ch in range(2):
            cs = slice(ch * 512, (ch + 1) * 512)
            for kh in range(2):
                w, wr = wload(wv_out[:, kh * 8:(kh + 1) * 8, cs], key="wb_out")
                for i in range(4):
                    bk, bkr = acc(i)
                    for kc in range(8):
                        mm(bk[:, :], yT[:, kh * 8 + kc, i * 128:(i + 1) * 128], w[:, kc, :], kh == 0 and kc == 0,
                           kh == 1 and kc == 7, R=["yT", wr], W=[bkr])
            for i in range(4):
                bk, bkr = acc(i)
                xr = xt[i % 2]
                P.dma("sp", xr[:, 0:512], x_d[tok0 + i * 128: tok0 + (i + 1) * 128, cs], W=[f"xt{i % 2}"], sem=f"xt{i % 2}")
                tt("dve", x1[i][:, cs], bk[:, :], xr[:, 0:512], ALU.add, R=[bkr, f"xt{i % 2}"], W=[f"x1_{i}"])
        if debug is not None and debug[0] == "x1":
            for i in range(4):
                P.dma("sp", out_d[tok0 + i * 128: tok0 + (i + 1) * 128, :], x1[i][:, :], R=[f"x1_{i}"], W=["outd"],
                      sem=f"o{i}", join=True)
            if nxt is not None:
                step1(nxt[0], nxt[1], range(4), "ab")
            return
        for i in range(4):
            act(junk[:, :], x1[i][:, :], AF.Square, R=[f"x1_{i}"], W=["junk"])
            red(stat[:, 32 + i:33 + i], junk[:, :], R=["junk"], W=["stat"])
        rstd_from_ss(stat[:, 32:36], 4, 1.0 / D, "stat")
        for i in range(4):
            xb = xn[i % 2]
            ts("dve", xb[:, :], x1[i][:, :], stat[:, 32 + i:33 + i], None, ALU.mult, ALU.bypass,
               R=[f"x1_{i}", "stat"], W=[f"xn{i % 2}"])
            ps, pr = psum()
            pb = ps[:, :].bitcast(BF16)
            for d in range(8):
                tr(pb[:, d * 128:(d + 1) * 128], xb[:, d * 128:(d + 1) * 128], identb, R=[f"xn{i % 2}", "cbf"], W=[pr])
            tt("dve", hT[:, :, i * 128:(i + 1) * 128], pb.rearrange("p (d t) -> p d t", t=128),
               w2T[:, :].unsqueeze(2).broadcast_to([128, 8, 128]), ALU.mult, R=[pr, "w2T"], W=["hT"])
        if nxt is not None:
            step1(nxt[0], nxt[1], range(0, 2), "a")
        for fg in range(6):
            ncol = 512 if fg < 5 else 256
            wgt, wgr = wload(wv_g[:, :, fg * 512: fg * 512 + ncol], ncol=ncol, key="wb_g")
            wut, wur = wload(wv_u[:, :, fg * 512: fg * 512 + ncol], ncol=ncol, key="wb_u")
            for j in range(ncol // 128):
                psg, pgr = fm_chunk(wgt, wgr, j, hT, "hT")
                psu, pur = fm_chunk(wut, wur, j, hT, "hT")
                act(sg[:, :], psg[:, :], AF.Silu, R=[pgr], W=["sg"])
                tt("dve", aT[:, fg * 4 + j, :], sg[:, :], psu[:, :], ALU.mult, R=["sg", pur], W=["aT"])
        if nxt is not None:
            step1(nxt[0], nxt[1], range(0, 2), "b")
            step1(nxt[0], nxt[1], range(2, 4), "ab")
        for ch in range(2):
            cs = slice(ch * 512, (ch + 1) * 512)
            for kg in range(3):
                nk = 8 if kg < 2 else 6
                w, wr = wload(wv_d[:, kg * 8: kg * 8 + nk, cs], nk=nk, key="wb_d")
                for i in range(4):
                    bk, bkr = acc(i)
                    for kc in range(nk):
                        mm(bk[:, :], aT[:, kg * 8 + kc, i * 128:(i + 1) * 128], w[:, kc, :], kg == 0 and kc == 0,
                           kg == 2 and kc == nk - 1, R=["aT", wr], W=[bkr])
            for i in range(4):
                bk, bkr = acc(i)
                tt("dve", x1[i][:, cs], bk[:, :], x1[i][:, cs], ALU.add, R=[bkr, f"x1_{i}"], W=[f"x1_{i}"])
        for i in range(4):
            P.dma("sp", out_d[tok0 + i * 128: tok0 + (i + 1) * 128, :], x1[i][:, :], R=[f"x1_{i}"], W=["outd"],
                  sem=f"o{i}", join=True)

    order = [(s, b) for s in range(nseq) for b in range(nblk)]
    step1(order[0][0], order[0][1], range(4), "ab")
    for n_, (s, b) in enumerate(order):
        emit_block(s, b, order[n_ + 1] if n_ + 1 < len(order) else None)
    P.final_wait("sp", [f"o{i}" for i in range(4)] + ["dbg"])
    P.emit(es)
    es.close()
    return nc


_CONSTS = None


def kernel(**inputs):
    global _CONSTS
    n = 8
    x = np.ascontiguousarray(inputs["x"], dtype=np.float32)
    B = x.shape[0]
    per = B // n
    if _CONSTS is None:
        _CONSTS = host_consts()
    c32, cb, cosT, sinT = _CONSTS
    nc = build_program(nseq=per)
    shared = {
        "w_in": inputs["w_in"][0], "w_out": inputs["w_out"][0], "w_gate": inputs["w_gate"][0],
        "w_up": inputs["w_up"][0], "w_down": inputs["w_down"][0],
        "norm1_w": inputs["norm1_w"], "norm2_w": inputs["norm2_w"], "ssd_norm_w": inputs["ssd_norm_w"],
        "conv_w": inputs["conv_w"][0], "conv_b": inputs["conv_b"],
        "dt_bias": inputs["dt_bias"], "a_log": inputs["a_log"], "d_skip": inputs["d_skip"],
        "q_norm_w": inputs["q_norm_w"], "k_norm_w": inputs["k_norm_w"],
        "lambda_q1": inputs["lambda_q1"], "lambda_k1": inputs["lambda_k1"],
        "lambda_q2": inputs["lambda_q2"], "lambda_k2": inputs["lambda_k2"],
        "subln_w": inputs["subln_w"],
        "c32": c32, "cbf": cb, "cosT": cosT, "sinT": sinT,
    }
    shared = {k: np.ascontiguousarray(v, dtype=np.float32) for k, v in shared.items()}
    in_maps = []
    for c in range(n):
        m = dict(shared)
        m["x"] = x[c * per:(c + 1) * per].reshape(per * SEQ, D)
        in_maps.append(m)
    res = run_bass_kernel_spmd(nc, in_maps, core_ids=list(range(n)))
    out = np.concatenate([np.asarray(r["out"]).reshape(per, SEQ, D) for r in res.results], axis=0)
    return out.astype(np.float32)
```

```python
import math
from contextlib import ExitStack
import numpy as np
import concourse.bass as bass
import concourse.mybir as mybir
from concourse.bass_utils import run_bass_kernel_spmd

F32 = mybir.dt.float32
BF16 = mybir.dt.bfloat16
AF = mybir.ActivationFunctionType
ALU = mybir.AluOpType
AX = mybir.AxisListType

D = 1024
SEQ = 2048
NSEQ_CORE = 4
BLK = 512
IN_DIM = 5648
DFF = 2816
EPS = 1e-6
LAMBDA_INIT = 0.8 - 0.6 * math.exp(0.0)
NEG = -30000.0

ENG_ATTR = {"pe": "tensor", "act": "scalar", "dve": "vector", "pool": "gpsimd", "sp": "sync"}


class Prog:
    def __init__(self, nc):
        self.nc = nc
        self.ops = {e: [] for e in ENG_ATTR}
        self.res = {}
        self.seen = {e: {} for e in ENG_ATTR}
        self.marked = {e: set() for e in ENG_ATTR}
        self.semcnt = {}
        self.tags = {}
        self.tag_last = {}

    def settag(self, name, tag):
        self.tags[name] = tag

    def _deps(self, eng, R, W, join):
        deps = set()
        raw = set()
        for r in R:
            st = self.res.setdefault(r, {"w": [], "r": []})
            for t in st["w"]:
                deps.add(t); raw.add(t)
        for w in W:
            st = self.res.setdefault(w, {"w": [], "r": []})
            for t in st["w"]:
                if join and t[0] == "s":
                    continue
                deps.add(t)
            for t in st["r"]:
                deps.add(t)
        touched = set(self.tags[x] for x in list(R) + list(W) if x in self.tags)
        for tg in touched:
            for other, last in self.tag_last.items():
                if other != tg:
                    for k, v in last.items():
                        deps.add((k[0], k[1], v)); raw.add((k[0], k[1], v))
        out = []
        for t in deps:
            kind, key, val = t
            if kind == "e" and key == eng and t not in raw:
                continue
            sk = (kind, key)
            if self.seen[eng].get(sk, 0) >= val:
                continue
            out.append(t)
        best = {}
        for kind, key, val in out:
            best[(kind, key)] = max(best.get((kind, key), 0), val)
        waits = []
        for (kind, key), val in best.items():
            self.seen[eng][(kind, key)] = val
            if kind == "e":
                self.marked[key].add(val)
            waits.append((kind, key, val))
        return waits, touched

    def _commit(self, tok, R, W, join, touched):
        for r in R:
            self.res[r]["r"].append(tok)
        for w in W:
            st = self.res[w]
            if join and st["w"] and all(t[0] == "s" for t in st["w"]) and not st["r"]:
                st["w"].append(tok)
            else:
                st["w"] = [tok]
                st["r"] = []
        for tg in touched:
            d = self.tag_last.setdefault(tg, {})
            k = (tok[0], tok[1])
            d[k] = max(d.get(k, 0), tok[2])

    def op(self, eng, fn, R=(), W=()):
        waits, touched = self._deps(eng, R, W, False)
        idx = len(self.ops[eng]) + 1
        self.ops[eng].append({"fn": fn, "waits": waits, "idx": idx, "dma": None})
        self._commit(("e", eng, idx), R, W, False, touched)

    def dma(self, q, out, in_, R=(), W=(), sem="d", join=False, **kw):
        waits, touched = self._deps(q, R, W, join)
        idx = len(self.ops[q]) + 1
        self.semcnt[sem] = self.semcnt.get(sem, 0) + 16
        fn = (lambda e, o=out, i=in_, k=kw: e.dma_start(out=o, in_=i, allow_slow_non_contiguous=True, **k))
        self.ops[q].append({"fn": fn, "waits": waits, "idx": idx, "dma": sem})
        self._commit(("s", sem, self.semcnt[sem]), R, W, join, touched)

    def retoken(self, names, sem):
        for n_ in names:
            self.res[n_]["w"] = [("s", sem, self.semcnt[sem])]

    def final_wait(self, q, sems):
        waits = [("s", s, self.semcnt[s]) for s in sems if s in self.semcnt]
        idx = len(self.ops[q]) + 1
        self.ops[q].append({"fn": None, "waits": waits, "idx": idx, "dma": None})

    def emit(self, es):
        nc = self.nc
        semh = {}
        for e in ENG_ATTR:
            semh[("e", e)] = es.enter_context(nc.semaphore("sem_" + e))
        for s in self.semcnt:
            semh[("s", s)] = es.enter_context(nc.semaphore("dsem_" + s))
        cnt = {}
        for e in ENG_ATTR:
            m = sorted(self.marked[e])
            cnt[e] = {idx: i + 1 for i, idx in enumerate(m)}
        block = es.enter_context(nc.Block())

        def replay(ename, eng):
            for o in self.ops[ename]:
                for kind, key, val in o["waits"]:
                    if kind == "e":
                        eng.wait_ge(semh[("e", key)], cnt[key][val])
                    else:
                        eng.wait_ge(semh[("s", key)], val)
                if o["fn"] is None:
                    continue
                ins = o["fn"](eng)
                if o["dma"] is not None:
                    ins.then_inc(semh[("s", o["dma"])], 16)
                elif o["idx"] in cnt[ename]:
                    ins.then_inc(semh[("e", ename)], 1)

        @block.tensor
        def _(e):
            replay("pe", e)

        @block.scalar
        def _(e):
            replay("act", e)

        @block.vector
        def _(e):
            replay("dve", e)

        @block.gpsimd
        def _(e):
            replay("pool", e)

        @block.sync
        def _(e):
            replay("sp", e)


def host_consts():
    c32 = np.zeros((128, 1024), np.float32)
    t = np.arange(128)
    c32[:, 0:128] = np.eye(128)
    c32[:, 128:256] = (t[:, None] <= t[None, :])
    c32[:, 256:384] = -1.0 * (t[:, None] <= t[None, :])
    c32[:, 384:512] = 1.0
    mk = np.where(t[None, :] < t[:, None], NEG, 0.0)
    c32[:, 512:1024] = np.tile(mk, (1, 4))
    cb = np.zeros((128, 1024), np.float32)
    cb[:, 384:512] = 1.0
    cb[:, 512:1024] = c32[:, 512:1024]
    cb[:, 0:128] = np.eye(128)
    cb[:, 128:256] = (t[:, None] // 64 == t[None, :] // 64)
    rt = np.zeros((128, 128), np.float32)
    for m in range(128):
        r = m % 64
        if r < 8:
            rt[m + 8, m] = -1.0
        elif r < 16:
            rt[m - 8, m] = 1.0
    cb[:, 256:384] = rt
    pos = np.arange(SEQ, dtype=np.float32)
    inv_freq = (500000.0 ** (-np.arange(0, 16, 2, dtype=np.float32) / 16)).astype(np.float32)
    ang = (pos[:, None] * inv_freq[None, :]).astype(np.float32)
    cosT = np.ones((128, SEQ), np.float32)
    sinT = np.zeros((128, SEQ), np.float32)
    for p in range(128):
        r = p % 64
        if r < 16:
            cosT[p] = np.cos(ang[:, r % 8])
            sinT[p] = np.sin(ang[:, r % 8])
    return c32, cb, cosT, sinT


def build_program(nseq=NSEQ_CORE, nblk=SEQ // BLK, debug=None):
    nc = bass.Bass("TRN2", target_bir_lowering=False)
    es = ExitStack()
    ntok = nseq * SEQ

    def din(name, shape):
        return nc.dram_tensor(name, list(shape), F32, kind="ExternalInput").ap()

    x_d = din("x", [ntok, D])
    out_d = nc.dram_tensor("out", [ntok, D], F32, kind="ExternalOutput").ap()
    w_in_d = din("w_in", [D, IN_DIM])
    w_out_d = din("w_out", [2 * D, D])
    w_g_d = din("w_gate", [D, DFF])
    w_u_d = din("w_up", [D, DFF])
    w_d_d = din("w_down", [DFF, D])
    norm1_d = din("norm1_w", [1, D]); norm2_d = din("norm2_w", [1, D]); ssdn_d = din("ssd_norm_w", [1, D])
    convw_d = din("conv_w", [4, 1536]); convb_d = din("conv_b", [1, 1536])
    dtb_d = din("dt_bias", [1, 16]); alog_d = din("a_log", [1, 16]); dsk_d = din("d_skip", [1, 16])
    qnw_d = din("q_norm_w", [1, 64]); knw_d = din("k_norm_w", [1, 64])
    lq1_d = din("lambda_q1", [1, 64]); lk1_d = din("lambda_k1", [1, 64])
    lq2_d = din("lambda_q2", [1, 64]); lk2_d = din("lambda_k2", [1, 64])
    subln_d = din("subln_w", [1, 128])
    c32_d = din("c32", [128, 1024]); cbf_d = din("cbf", [128, 1024])
    cos_d = din("cosT", [128, SEQ]); sin_d = din("sinT", [128, SEQ])

    def dscr(name, shape):
        return nc.dram_tensor(name, list(shape), BF16, kind="Internal").ap()

    wb_in = dscr("wb_in", [D, IN_DIM]); wb_out = dscr("wb_out", [2 * D, D])
    wb_g = dscr("wb_g", [D, DFF]); wb_u = dscr("wb_u", [D, DFF]); wb_d = dscr("wb_d", [DFF, D])
    dbg_d = None
    if debug is not None and debug[0] == "hT":
        dbg_d = nc.dram_tensor("dbg", list(debug[1]), F32, kind="ExternalOutput").ap()

    P = Prog(nc)

    def sb(name, shape, dt=F32):
        return es.enter_context(nc.sbuf_tensor("s_" + name, list(shape), dt))

    c32 = sb("c32", [128, 1024]); cbf = sb("cbf", [128, 1024], BF16)
    ident32 = c32[:, 0:128]; tri32 = c32[:, 128:256]; negtri32 = c32[:, 256:384]; ones32 = c32[:, 384:512]
    mask4 = c32[:, 512:1024]
    identb = cbf[:, 0:128]; blockones = cbf[:, 128:256]; rotT = cbf[:, 256:384]; onesb = cbf[:, 384:512]; mask4b = cbf[:, 512:1024]
    diagw = sb("diagw", [128, 12, 4, 128], BF16)
    w1T = sb("w1T", [128, 8]); w2T = sb("w2T", [128, 8]); wssdT = sb("wssdT", [128, 8])
    convwT = sb("convwT", [128, 4, 12]); convbT = sb("convbT", [128, 12])
    wq2 = sb("wq2", [128, 1]); wk2 = sb("wk2", [128, 1])
    sublnw = sb("sublnw", [128, 128]); subw = sb("subw", [128, 1])
    dtb4 = sb("dtb4", [128, 64]); a4 = sb("a4", [128, 64]); dsk = sb("dsk", [128, 16])
    lam4 = sb("lam4", [128, 4, 64]); lamt = sb("lamt", [128, 8]); neglam = sb("neglam", [128, 1])
    cosb = sb("cosb", [128, BLK]); sinb = sb("sinb", [128, BLK])
    wdt = sb("wdt", [128, 8, 16], BF16)
    kT = sb("kT", [128, 8, SEQ], BF16)
    vaug = sb("vaug", [128, 16, 8, 130], BF16)
    st = sb("st", [128, 1024]); stbf = sb("stbf", [128, 1024], BF16)
    xt = [sb(f"xt{i}", [128, 1024]) for i in range(2)]
    hT = sb("hT", [128, 8, BLK], BF16)
    yT = sb("yT", [128, 16, BLK], BF16)
    NW = 2
    wg = [sb(f"wg{i}", [128, 8, 512], BF16) for i in range(NW)]
    junk = sb("junk", [128, 1024], BF16)
    xn = [sb(f"xn{i}", [128, 1024], BF16) for i in range(2)]
    stat = sb("stat", [128, 64])
    halo = sb("halo", [128, 12, 4], BF16)
    epsc = sb("epsc", [128, 2])
    REG = 55 * 1024 + 512
    region = sb("region", [128, REG // 4])
    roff = {"S": 0, "T": 0, "F": 0}

    def rg(phase, name, shape, dt=F32):
        n = int(np.prod(shape[1:]))
        nb = n * (4 if dt == F32 else 2)
        nb = (nb + 31) // 32 * 32
        o = roff[phase]
        roff[phase] = o + nb
        assert roff[phase] <= REG, (phase, name, roff[phase])
        ap = region[:, o // 4:(o + nb) // 4]
        if dt != F32:
            ap = ap.bitcast(dt)
        ap = ap[:, 0:n]
        if len(shape) == 3:
            ap = ap.rearrange("p (a b) -> p a b", b=shape[2])
        elif len(shape) == 4:
            ap = ap.rearrange("p (a b c) -> p a b c", b=shape[2], c=shape[3])
        P.settag(name, phase)
        return ap

    zs = rg("S", "zs", [128, 4, 1024], BF16)
    xbcT = rg("S", "xbcT", [128, 12, 516], BF16)
    xsT = rg("S", "xsT", [128, 8, BLK], BF16)
    bcT = rg("S", "bcT", [128, 4, BLK], BF16)
    dtt = rg("S", "dtt", [128, 64]); adt = rg("S", "adt", [128, 64])
    xdt = rg("S", "xdt", [128, 1024], BF16); xsD = rg("S", "xsD", [128, 1024], BF16)
    R1 = [rg("S", f"R1{i}", [128, 4, 128]) for i in range(2)]
    dec = [rg("S", f"dec{i}", [128, 4, 128], BF16) for i in range(2)]
    MT = [rg("S", f"MT{i}", [128, 4, 128], BF16) for i in range(2)]
    cbT = rg("S", "cbT", [128, 256], BF16)
    ybuf = rg("S", "ybuf", [128, 1024])
    sp_t = ybuf[:, 0:256].rearrange("p (a b) -> p a b", b=64)
    gn = rg("S", "gn", [128, 1024], BF16)
    xdtE = rg("S", "xdtE", [128, 1024], BF16)
    Btok = rg("S", "Btok", [128, 256], BF16)
    acs = rg("S", "acs", [128, 96])
    qT = rg("T", "qT", [128, 8, BLK], BF16)
    sqb = [rg("T", f"sqb{i}", [128, BLK], BF16) for i in range(2)]
    rtb = [rg("T", f"rtb{i}", [128, BLK]) for i in range(2)]
    rinv = [rg("T", f"rinv{i}", [128, BLK]) for i in range(2)]
    qn = [rg("T", f"qn{i}", [128, BLK]) for i in range(2)]
    qnb = [rg("T", f"qnb{i}", [128, BLK], BF16) for i in range(2)]
    t1 = [rg("T", f"t1{i}", [128, BLK]) for i in range(2)]
    t2 = [rg("T", f"t2{i}", [128, BLK]) for i in range(2)]
    pT = [rg("T", f"pT{i}", [128, BLK], BF16) for i in range(6)]
    oS = [rg("T", f"oS{i}", [128, BLK]) for i in range(2)]
    pS = [[rg("T", f"pS{a}{c}", [128, BLK]) for c in range(2)] for a in range(2)]
    aT = rg("F", "aT", [128, 22, BLK], BF16)
    sg = rg("F", "sg", [128, BLK])
    x1 = [rg("F", f"x1_{i}", [128, 1024]) for i in range(4)]

    banks = [es.enter_context(nc.psum_tensor(f"ps{i}", [128, 512], F32)) for i in range(8)]
    rr = [0]

    def psum():
        i = 4 + rr[0] % 4
        rr[0] += 1
        return banks[i], f"ps{i}"

    def acc(i):
        return banks[i], f"ps{i}"

    rr6 = [0]

    def psum6():
        i = 3 + rr6[0] % 5
        rr6[0] += 1
        return banks[i], f"ps{i}"

    rr8 = [0]

    def psum8():
        i = rr8[0] % 8
        rr8[0] += 1
        return banks[i], f"ps{i}"

    def mm(out, lhsT, rhs, start, stop, R, W, sgc=False):
        P.op("pe", lambda e: e.matmul(out, lhsT, rhs, start=start, stop=stop, skip_group_check=sgc), R=R, W=W)

    def tr(out, in_, ident, R, W):
        P.op("pe", lambda e: e.transpose(out, in_, ident), R=R, W=W)

    def act(out, in_, func, R, W, scale=1.0, bias=None):
        if bias is None:
            P.op("act", lambda e: e.activation(out, in_, func, scale=scale), R=R, W=W)
        else:
            P.op("act", lambda e: e.activation(out, in_, func, bias=bias, scale=scale), R=R, W=W)

    def tt(eng, out, in0, in1, op, R, W):
        P.op(eng, lambda e: e.tensor_tensor(out, in0, in1, op), R=R, W=W)

    def ts(eng, out, in0, s1, s2, op0, op1, R, W):
        P.op(eng, lambda e: e.tensor_scalar(out, in0, s1, s2, op0, op1), R=R, W=W)

    def stt(out, in0, scalar, in1, op0, op1, R, W):
        P.op("dve", lambda e: e.scalar_tensor_tensor(out, in0, scalar, in1, op0, op1), R=R, W=W)

    def red(out, in_, R, W):
        P.op("dve", lambda e: e.tensor_reduce(out, in_, AX.X, ALU.add), R=R, W=W)

    def recip(out, in_, R, W):
        P.op("dve", lambda e: e.reciprocal(out, in_), R=R, W=W)

    def cp(eng, out, in_, R, W):
        if eng == "act":
            P.op("act", lambda e: e.copy(out, in_), R=R, W=W)
        else:
            P.op(eng, lambda e: e.tensor_copy(out, in_), R=R, W=W)

    def mset(eng, ap, val, W):
        P.op(eng, lambda e: e.memset(ap, val), W=W)

    def rstd_from_ss(ss_ap, n, inv_n, name):
        act(ss_ap, ss_ap, AF.Ln, R=[name, "epsc"], W=[name], scale=inv_n, bias=epsc[:, 0:1])
        act(ss_ap, ss_ap, AF.Exp, R=[name], W=[name], scale=-0.5)

    ncd = nc.allow_non_contiguous_dma(reason="tiny parameter layout loads")
    ncd.__enter__()
    for (src, dst, rows, key) in ((w_in_d, wb_in, D, "wb_in"), (w_out_d, wb_out, 2 * D, "wb_out"),
                                  (w_g_d, wb_g, D, "wb_g"), (w_u_d, wb_u, D, "wb_u"),
                                  (w_d_d, wb_d, DFF, "wb_d")):
        for r0 in range(0, rows, 128):
            P.dma("pool", dst[r0:r0 + 128, :], src[r0:r0 + 128, :], W=[key], sem=key, join=True)
    P.dma("sp", c32[:, :], c32_d[:, :], W=["c32"], sem="cst")
    P.dma("pool", cbf[:, :], cbf_d[:, :], W=["cbf"], sem="cstb")
    P.dma("sp", w1T[:, :], norm1_d[0, :].rearrange("(d p) -> p d", p=128), W=["w1T"], sem="cst")
    P.dma("sp", w2T[:, :], norm2_d[0, :].rearrange("(d p) -> p d", p=128), W=["w2T"], sem="cst")
    P.dma("sp", wssdT[:, :], ssdn_d[0, :].rearrange("(d p) -> p d", p=128), W=["wssdT"], sem="cst")
    for k in range(4):
        P.dma("sp", convwT[:, k, :], convw_d[k, :].rearrange("(c p) -> p c", p=128), W=["convwT"], sem="cst", join=True)
    P.dma("sp", convbT[:, :], convb_d[0, :].rearrange("(c p) -> p c", p=128), W=["convbT"], sem="cst")
    for hh in range(2):
        P.dma("sp", wq2[hh * 64:(hh + 1) * 64, :], qnw_d[0, :].rearrange("(p o) -> p o", o=1), W=["wq2"], sem="cst", join=True)
        P.dma("sp", wk2[hh * 64:(hh + 1) * 64, :], knw_d[0, :].rearrange("(p o) -> p o", o=1), W=["wk2"], sem="cst", join=True)
    P.dma("sp", sublnw[:, :], subln_d[0:1, :].broadcast_to([128, 128]), W=["sublnw"], sem="cst")
    P.dma("sp", subw[:, :], subln_d[0, :].rearrange("(p o) -> p o", o=1), W=["subw"], sem="cst")
    for i in range(4):
        P.dma("sp", dtb4[:, i * 16:(i + 1) * 16], dtb_d[0:1, :].broadcast_to([128, 16]), W=["dtb4"], sem="cst", join=True)
        P.dma("sp", a4[:, i * 16:(i + 1) * 16], alog_d[0:1, :].broadcast_to([128, 16]), W=["a4"], sem="cst", join=True)
    P.dma("sp", dsk[:, :], dsk_d[0:1, :].broadcast_to([128, 16]), W=["dsk"], sem="cst")
    for i, ld in enumerate((lq1_d, lk1_d, lq2_d, lk2_d)):
        P.dma("sp", lam4[:, i, :], ld[0:1, :].broadcast_to([128, 64]), W=["lam4"], sem="cst", join=True)
    P.dma("sp", wdt[:, :, :], wb_in.rearrange("(d p) c -> p d c", p=128)[:, :, 2560:2576], R=["wb_in"], W=["wdt"], sem="cst")
    ncd.__exit__(None, None, None)
    P.retoken(["c32", "w1T", "w2T", "wssdT", "convwT", "convbT", "wq2", "wk2", "sublnw", "subw", "dtb4", "a4", "dsk", "lam4", "wdt"], "cst")
    act(a4[:, :], a4[:, :], AF.Exp, R=["a4"], W=["a4"])
    ts("dve", a4[:, :], a4[:, :], -1.0, None, ALU.mult, ALU.bypass, R=["a4"], W=["a4"])
    tt("dve", lam4[:, 0, :], lam4[:, 0, :], lam4[:, 1, :], ALU.mult, R=["lam4"], W=["lam4"])
    tt("dve", lam4[:, 2, :], lam4[:, 2, :], lam4[:, 3, :], ALU.mult, R=["lam4"], W=["lam4"])
    red(lamt[:, 0:1], lam4[:, 0, :], R=["lam4"], W=["lamt"])
    red(lamt[:, 1:2], lam4[:, 2, :], R=["lam4"], W=["lamt"])
    act(lamt[:, 2:4], lamt[:, 0:2], AF.Exp, R=["lamt"], W=["lamt"])
    tt("dve", lamt[:, 4:5], lamt[:, 3:4], lamt[:, 2:3], ALU.subtract, R=["lamt"], W=["lamt"])
    ts("dve", neglam[:, :], lamt[:, 4:5], -LAMBDA_INIT, None, ALU.add, ALU.bypass, R=["lamt"], W=["neglam"])
    ts("dve", sublnw[:, :], sublnw[:, :], 1.0 - LAMBDA_INIT, None, ALU.mult, ALU.bypass, R=["sublnw"], W=["sublnw"])
    ts("dve", subw[:, :], subw[:, :], 1.0 - LAMBDA_INIT, None, ALU.mult, ALU.bypass, R=["subw"], W=["subw"])
    for c in range(12):
        for k in range(4):
            ts("dve", diagw[:, c, k, :], ident32, convwT[:, k, c:c + 1], None, ALU.mult, ALU.bypass,
               R=["c32", "convwT"], W=["diagw"])
    mset("pool", vaug[:, :, :, 128:130], 1.0, W=["vaug"])
    mset("pool", epsc[:, :], EPS, W=["epsc"])

    wctr = [0]

    def wload(src_ap, nk=8, ncol=512, key="wb_in"):
        i = wctr[0] % NW
        wctr[0] += 1
        P.dma("sp", wg[i][:, 0:nk, 0:ncol], src_ap, R=[key], W=[f"wg{i}"], sem=f"wg{i}")
        return wg[i], f"wg{i}"

    wv_in = wb_in.rearrange("(d p) c -> p d c", p=128)
    wv_out = wb_out.rearrange("(k p) c -> p k c", p=128)
    wv_g = wb_g.rearrange("(d p) c -> p d c", p=128)
    wv_u = wb_u.rearrange("(d p) c -> p d c", p=128)
    wv_d = wb_d.rearrange("(k p) c -> p k c", p=128)

    def fm_chunk(w, wr, j, src, srcname):
        ps, pr = psum()
        for d in range(8):
            mm(ps[:, :], w[:, d, j * 128:(j + 1) * 128], src[:, d, :], d == 0, d == 7, R=[wr, srcname], W=[pr])
        return ps, pr

    def tm_tile(w, wr, i, ncol=512):
        ps, pr = psum()
        for d in range(8):
            mm(ps[:, 0:ncol], hT[:, d, i * 128:(i + 1) * 128], w[:, d, 0:ncol], d == 0, d == 7, R=[wr, "hT"], W=[pr])
        return ps, pr

    def norm_to_hT(src_tiles, src_names, wT, wTname):
        for i in range(4):
            act(junk[:, :], src_tiles[i], AF.Square, R=[src_names[i]], W=["junk"])
            red(stat[:, i:i + 1], junk[:, :], R=["junk"], W=["stat"])
        rstd_from_ss(stat[:, 0:4], 4, 1.0 / D, "stat")
        for i in range(4):
            xb = xn[i % 2]
            ts("dve", xb[:, :], src_tiles[i], stat[:, i:i + 1], None, ALU.mult, ALU.bypass,
               R=[src_names[i], "stat"], W=[f"xn{i % 2}"])
            ps, pr = psum()
            pb = ps[:, :].bitcast(BF16)
            for d in range(8):
                tr(pb[:, d * 128:(d + 1) * 128], xb[:, d * 128:(d + 1) * 128], identb, R=[f"xn{i % 2}", "cbf"], W=[pr])
            tt("dve", hT[:, :, i * 128:(i + 1) * 128], pb.rearrange("p (d t) -> p d t", t=128),
               wT[:, :].unsqueeze(2).broadcast_to([128, 8, 128]), ALU.mult, R=[pr, wTname], W=["hT"])

    def dump(ap, resname, rows=None):
        P.dma("sp", dbg_d if rows is None else dbg_d[rows], ap, R=[resname], W=["dbgout"], sem="dbg", join=True)

    def step1(s, b, tiles, part):
        tok0 = s * SEQ + b * BLK
        for i in tiles:
            xb = xn[i % 2]
            if "a" in part:
                P.dma("sp", xt[i % 2][:, :], x_d[tok0 + i * 128: tok0 + (i + 1) * 128, :], W=[f"xt{i % 2}"], sem=f"xt{i % 2}")
                act(junk[:, :], xt[i % 2][:, :], AF.Square, R=[f"xt{i % 2}"], W=["junk"])
                red(stat[:, 8 + i:9 + i], junk[:, :], R=["junk"], W=["stat"])
                rstd_from_ss(stat[:, 8 + i:9 + i], 1, 1.0 / D, "stat")
                ts("dve", xb[:, :], xt[i % 2][:, :], stat[:, 8 + i:9 + i], None, ALU.mult, ALU.bypass,
                   R=[f"xt{i % 2}", "stat"], W=[f"xn{i % 2}"])
            if "b" in part:
                ps, pr = psum()
                pb = ps[:, :].bitcast(BF16)
                for d in range(8):
                    tr(pb[:, d * 128:(d + 1) * 128], xb[:, d * 128:(d + 1) * 128], identb, R=[f"xn{i % 2}", "cbf"], W=[pr])
                tt("dve", hT[:, :, i * 128:(i + 1) * 128], pb.rearrange("p (d t) -> p d t", t=128),
                   w1T[:, :].unsqueeze(2).broadcast_to([128, 8, 128]), ALU.mult, R=[pr, "w1T"], W=["hT"])

    def emit_block(s, b, nxt):
        tok0 = s * SEQ + b * BLK

        for g in range(2):
            w, wr = wload(wv_in[:, :, g * 512:(g + 1) * 512])
            for i in range(4):
                ps, pr = tm_tile(w, wr, i)
                act(zs[:, i, g * 512:(g + 1) * 512], ps[:, :], AF.Silu, R=[pr], W=["zs"])
        if b == 0:
            mset("pool", xbcT[:, :, 0:3], 0.0, W=["xbcT"])
        else:
            cp("dve", xbcT[:, :, 0:3], halo[:, :, 0:3], R=["halo"], W=["xbcT"])
        for g in range(3):
            w, wr = wload(wv_in[:, :, 1024 + g * 512: 1024 + (g + 1) * 512])
            for j in range(4):
                ps, pr = fm_chunk(w, wr, j, hT, "hT")
                cp("act", xbcT[:, g * 4 + j, 3:515], ps[:, :], R=[pr], W=["xbcT"])
        cp("dve", halo[:, :, 0:3], xbcT[:, :, 512:515], R=["xbcT"], W=["halo"])
        psd, pdr = psum()
        for i in range(4):
            for d in range(8):
                mm(psd[:, i * 16:(i + 1) * 16], hT[:, d, i * 128:(i + 1) * 128], wdt[:, d, :], d == 0, d == 7,
                   R=["hT", "wdt"], W=[pdr])
        tt("dve", sp_t[:, 0, :], psd[:, 0:64], dtb4[:, :], ALU.add, R=[pdr, "dtb4"], W=["ybuf"])
        act(sp_t[:, 1, :], sp_t[:, 0, :], AF.Abs, R=["ybuf"], W=["ybuf"])
        act(sp_t[:, 2, :], sp_t[:, 1, :], AF.Exp, R=["ybuf"], W=["ybuf"], scale=-1.0)
        ts("dve", sp_t[:, 2, :], sp_t[:, 2, :], 1.0, None, ALU.add, ALU.bypass, R=["ybuf"], W=["ybuf"])
        act(sp_t[:, 3, :], sp_t[:, 2, :], AF.Ln, R=["ybuf"], W=["ybuf"])
        stt(dtt[:, :], sp_t[:, 0, :], 0.0, sp_t[:, 3, :], ALU.max, ALU.add, R=["ybuf"], W=["dtt"])
        tt("dve", adt[:, :], dtt[:, :], a4[:, :], ALU.mult, R=["dtt", "a4"], W=["adt"])
        for c in range(12):
            ps, pr = psum()
            for k in range(4):
                mm(ps[:, :], diagw[:, c, k, :], xbcT[:, c, k:k + 512], k == 0, k == 3, R=["diagw", "xbcT"], W=[pr])
            if c < 8:
                act(xsT[:, c, :], ps[:, :], AF.Silu, R=[pr, "convbT"], W=["xsT"], bias=convbT[:, c:c + 1])
            else:
                act(bcT[:, c - 8, :], ps[:, :], AF.Silu, R=[pr, "convbT"], W=["bcT"], bias=convbT[:, c:c + 1])
        if b == 0:
            mset("pool", st[:, :], 0.0, W=["st"])
            mset("pool", stbf[:, :], 0.0, W=["stbf"])
        def tsl_(i):
            return slice(i * 128, (i + 1) * 128)

        def prepA(i):
            tsl = tsl_(i)
            hsl = slice(i * 16, (i + 1) * 16)
            psx, pxr = psum()
            pxb = psx[:, :].bitcast(BF16)
            for cc in range(8):
                tr(pxb[:, cc * 128:(cc + 1) * 128], xsT[:, cc, tsl], identb, R=["xsT", "cbf"], W=[pxr])
            px3 = pxb.rearrange("p (h d) -> p h d", d=64)
            tt("dve", xdt.rearrange("p (h d) -> p h d", d=64), px3,
               dtt[:, hsl].unsqueeze(2).broadcast_to([128, 16, 64]), ALU.mult, R=[pxr, "dtt"], W=["xdt"])
            tt("dve", xsD.rearrange("p (h d) -> p h d", d=64), px3,
               dsk[:, :].unsqueeze(2).broadcast_to([128, 16, 64]), ALU.mult, R=[pxr, "dsk"], W=["xsD"])
            psa, par = psum()
            mm(psa[:, 0:16], tri32, adt[:, hsl], True, True, R=["c32", "adt"], W=[par])
            mm(psa[:, 16:32], ones32, adt[:, hsl], True, True, R=["c32", "adt"], W=[par])
            act(acs[:, 0:16], psa[:, 0:16], AF.Exp, R=[par], W=["acs"])
            cp("act", acs[:, 16:32], psa[:, 16:32], R=[par], W=["acs"])
            tt("dve", acs[:, 32:48], acs[:, 16:32], psa[:, 0:16], ALU.subtract, R=["acs", par], W=["acs"])
            act(acs[:, 48:64], acs[:, 32:48], AF.Exp, R=["acs"], W=["acs"])
            act(acs[:, 64:80], acs[:, 16:32], AF.Exp, R=["acs"], W=["acs"])
            psc, pcr = psum()
            for g in range(2):
                mm(psc[:, g * 128:(g + 1) * 128], bcT[:, g, tsl], bcT[:, 2 + g, tsl], True, True, R=["bcT"], W=[pcr])
            cp("act", cbT[:, :], psc[:, 0:256], R=[pcr], W=["cbT"])
            tt("pool", xdtE.rearrange("p (h d) -> p h d", d=64), xdt.rearrange("p (h d) -> p h d", d=64),
               acs[:, 48:64].unsqueeze(2).broadcast_to([128, 16, 64]), ALU.mult, R=["xdt", "acs"], W=["xdtE"])
            psb, pbr2 = psum()
            pbb = psb[:, :].bitcast(BF16)
            for g in range(2):
                tr(pbb[:, g * 128:(g + 1) * 128], bcT[:, g, tsl], identb, R=["bcT", "cbf"], W=[pbr2])
            cp("act", Btok[:, :], pbb[:, 0:256], R=[pbr2], W=["Btok"])

        seg_ps = {}

        def segR1(i, hg):
            h0 = i * 16 + hg * 4
            k2 = hg % 2
            tt("dve", R1[k2][:, :, :], tri32.unsqueeze(1).broadcast_to([128, 4, 128]),
               adt[:, h0:h0 + 4].unsqueeze(2).broadcast_to([128, 4, 128]), ALU.mult, R=["c32", "adt"], W=[f"R1{k2}"])

        def segMM(i, hg):
            h0 = i * 16 + hg * 4
            k2 = hg % 2
            pss, psr = psum()
            mm(pss[:, :], ones32, R1[k2].rearrange("p a b -> p (a b)"), True, False, R=["c32", f"R1{k2}"], W=[psr])
            mm(pss[:, :].rearrange("p (a b) -> p a b", b=128), negtri32, adt[:, h0:h0 + 4].unsqueeze(2).broadcast_to([128, 4, 128]),
               False, False, R=["c32", "adt"], W=[psr])
            mm(pss[:, :], identb, mask4b, False, True, R=["cbf"], W=[psr])
            seg_ps[hg] = (pss, psr)

        def segFin(i, hg):
            k2 = hg % 2
            pss, psr = seg_ps[hg]
            act(dec[k2].rearrange("p a b -> p (a b)"), pss[:, :], AF.Exp, R=[psr], W=[f"dec{k2}"])
            g = hg // 2
            m = MT[k2]
            tt("dve", m[:, :, :], dec[k2][:, :, :], cbT[:, g * 128:(g + 1) * 128].unsqueeze(1).broadcast_to([128, 4, 128]),
               ALU.mult, R=[f"dec{k2}", "cbT"], W=[f"MT{k2}"])
            for hh in range(4):
                h = hg * 4 + hh
                pb_, pbr = acc(h // 8)
                mm(pb_[:, (h % 8) * 64:(h % 8 + 1) * 64], m[:, hh, :], xdt[:, h * 64:(h + 1) * 64], True, True,
                   R=[f"MT{k2}", "xdt"], W=[pbr])

        def prepB(i, hooks=()):
            segR1(i, 0)
            segMM(i, 0)
            segR1(i, 1)
            for hg in range(4):
                if hg + 1 < 4:
                    segMM(i, hg + 1)
                if hg + 2 < 4:
                    segR1(i, hg + 2)
                segFin(i, hg)
                if hg < len(hooks):
                    hooks[hg]()

        def yoff(i):
            tsl = tsl_(i)
            for g in range(2):
                bk, bkr = acc(2 + g)
                mm(bk[:, :], bcT[:, 2 + g, tsl], stbf[:, g * 512:(g + 1) * 512], True, True, R=["bcT", "stbf"], W=[bkr])

        upd_ps = {}

        def updMM(i):
            pst2 = [psum(), psum()]
            for g in range(2):
                mm(pst2[g][0][:, :], Btok[:, g * 128:(g + 1) * 128], xdtE[:, g * 512:(g + 1) * 512], True, True,
                   R=["Btok", "xdtE"], W=[pst2[g][1]])
            upd_ps[i] = pst2

        def U1(i):
            pst2 = upd_ps[i]
            for g in range(2):
                gs = slice(g * 512, (g + 1) * 512)
                tt("dve", st[:, gs].rearrange("p (h d) -> p h d", d=64), st[:, gs].rearrange("p (h d) -> p h d", d=64),
                   acs[:, 64 + g * 8:64 + (g + 1) * 8].unsqueeze(2).broadcast_to([128, 8, 64]), ALU.mult, R=["st", "acs"], W=["st"])
                tt("dve", st[:, gs], pst2[g][0][:, :], st[:, gs], ALU.add, R=[pst2[g][1], "st"], W=["st"])
            cp("act", stbf[:, :], st[:, :], R=["st"], W=["stbf"])

        def F1(i):
            for g in range(2):
                gs = slice(g * 512, (g + 1) * 512)
                po, por_ = acc(2 + g)
                py, pyr_ = acc(g)
                tt("dve", ybuf[:, gs].rearrange("p (h d) -> p h d", d=64), po[:, :].rearrange("p (h d) -> p h d", d=64),
                   acs[:, g * 8:(g + 1) * 8].unsqueeze(2).broadcast_to([128, 8, 64]), ALU.mult, R=[por_, "acs"], W=["ybuf"])
                tt("dve", ybuf[:, gs], py[:, :], ybuf[:, gs], ALU.add, R=[pyr_, "ybuf"], W=["ybuf"])
            tt("dve", ybuf[:, :], ybuf[:, :], xsD[:, :], ALU.add, R=["ybuf", "xsD"], W=["ybuf"])

        def F2(i):
            tt("pool", ybuf[:, :], ybuf[:, :], zs[:, i, :], ALU.mult, R=["ybuf", "zs"], W=["ybuf"])
            act(junk[:, :], ybuf[:, :], AF.Square, R=["ybuf"], W=["junk"])

        def F3(i):
            red(stat[:, 16:18], junk[:, :].rearrange("p (g d) -> p g d", d=512), R=["junk"], W=["stat"])
            rstd_from_ss(stat[:, 16:18], 2, 1.0 / 512, "stat")

        def F4(i):
            for g in range(2):
                gs = slice(g * 512, (g + 1) * 512)
                ts("dve", gn[:, gs], ybuf[:, gs], stat[:, 16 + g:17 + g], None, ALU.mult, ALU.bypass, R=["ybuf", "stat"], W=["gn"])

        def finB(i):
            tsl = tsl_(i)
            pst, ptr_ = psum()
            ptb = pst[:, :].bitcast(BF16)
            for cc in range(8):
                tr(ptb[:, cc * 128:(cc + 1) * 128], gn[:, cc * 128:(cc + 1) * 128], identb, R=["gn", "cbf"], W=[ptr_])
            tt("dve", yT[:, 0:8, tsl], ptb.rearrange("p (c t) -> p c t", t=128),
               wssdT[:, :].unsqueeze(2).broadcast_to([128, 8, 128]), ALU.mult, R=[ptr_, "wssdT"], W=["yT"])

        prepA(0)
        prepB(0)
        for i in range(4):
            yoff(i)
            updMM(i)
            F1(i)
            U1(i)
            if i + 1 < 4:
                prepA(i + 1)
                prepB(i + 1, hooks=[(lambda ii=i: F2(ii)), (lambda ii=i: F3(ii)), (lambda ii=i: F4(ii))])
            else:
                F2(i); F3(i); F4(i)
            finB(i)

        P.dma("sp", cosb[:, :], cos_d[:, b * BLK:(b + 1) * BLK], W=["cosb"], sem="cos")
        P.dma("sp", sinb[:, :], sin_d[:, b * BLK:(b + 1) * BLK], W=["sinb"], sem="sin")

        qk_state = {}

        def qkA(c):
            kind, g, j = ("q", c // 4, c % 4) if c < 8 else ("k", (c - 8) // 4, c % 4)
            if j == 0:
                base = 2576 if kind == "q" else 3600
                qk_state["w"] = wload(wv_in[:, :, base + g * 512: base + (g + 1) * 512])
            w, wr = qk_state["w"]
            ps, pr = psum8()
            for d in range(8):
                mm(ps[:, :], w[:, d, j * 128:(j + 1) * 128], hT[:, d, :], d == 0, d == 7, R=[wr, "hT"], W=[pr])
            k = c % 2
            act(sqb[k][:, :], ps[:, :], AF.Square, R=[pr], W=[f"sqb{k}"])
            qk_state[c] = (ps, pr)

        def qkB(c):
            k = c % 2
            ps, pr = qk_state[c]
            wvec, wname = (wq2, "wq2") if c < 8 else (wk2, "wk2")
            ps2, pr2 = psum8()
            mm(ps2[:, :], blockones, sqb[k][:, :], True, True, R=["cbf", f"sqb{k}"], W=[pr2])
            act(rtb[k][:, :], ps2[:, :], AF.Ln, R=[pr2, "epsc"], W=[f"rtb{k}"], scale=1.0 / 64, bias=epsc[:, 0:1])
            act(rinv[k][:, :], rtb[k][:, :], AF.Exp, R=[f"rtb{k}"], W=[f"rinv{k}"], scale=-0.5)
            stt(qn[k][:, :], ps[:, :], wvec[:, 0:1], rinv[k][:, :], ALU.mult, ALU.mult, R=[pr, wname, f"rinv{k}"], W=[f"qn{k}"])
            cp("act", qnb[k][:, :], qn[k][:, :], R=[f"qn{k}"], W=[f"qnb{k}"])

        def qkC(c):
            k = c % 2
            if c < 8:
                dest, dname = qT[:, c, :], "qT"
            else:
                dest, dname = kT[:, c - 8, b * BLK:(b + 1) * BLK], "kT"
            ps3, pr3 = psum8()
            mm(ps3[:, :], rotT, qnb[k][:, :], True, True, R=["cbf", f"qnb{k}"], W=[pr3])
            tt("dve", t1[k][:, :], qn[k][:, :], cosb[:, :], ALU.mult, R=[f"qn{k}", "cosb"], W=[f"t1{k}"])
            tt("dve", t2[k][:, :], ps3[:, :], sinb[:, :], ALU.mult, R=[pr3, "sinb"], W=[f"t2{k}"])
            tt("pool", dest, t1[k][:, :], t2[k][:, :], ALU.add, R=[f"t1{k}", f"t2{k}"], W=[dname])

        for step in range(18):
            if step < 16:
                qkA(step)
            if 0 <= step - 1 < 16:
                qkB(step - 1)
            if 0 <= step - 2 < 16:
                qkC(step - 2)
        for g in range(2):
            w, wr = wload(wv_in[:, :, 4624 + g * 512: 4624 + (g + 1) * 512])
            for i in range(4):
                ps, pr = tm_tile(w, wr, i)
                cp("act", vaug[:, 4 * b + i, g * 4:(g + 1) * 4, 0:128], ps[:, :].rearrange("p (h d) -> p h d", d=128),
                   R=[pr], W=["vaug"])
        nk = 4 * b + 4
        iters = [(h, kt) for h in range(8) for kt in range(nk)]
        NI = len(iters)
        NPT = len(pT)

        def QK(n):
            h, kt = iters[n]
            qlo = max(kt - 4 * b, 0)
            nn = BLK - qlo * 128
            sps = []
            for j in range(2):
                js = slice(j * 64, (j + 1) * 64)
                sp, spr = psum6()
                mm(sp[:, 0:nn], kT[js, h, kt * 128:(kt + 1) * 128], qT[js, h, qlo * 128:BLK], True, True,
                   R=["kT", "qT"], W=[spr])
                sps.append((sp, spr))
            for j in range(2):
                sp, spr = sps[j]
                pi = (2 * n + j) % NPT
                act(pT[pi][:, 0:nn], sp[:, 0:nn], AF.Exp, R=[spr], W=[f"pT{pi}"], scale=0.125)
                if kt >= 4 * b:
                    mset("pool", pT[pi][64:128, 0:64], 0.0, W=[f"pT{pi}"])

        def PV(n):
            h, kt = iters[n]
            qlo = max(kt - 4 * b, 0)
            nn = BLK - qlo * 128
            hb = h % 2
            for j in range(2):
                pi = (2 * n + j) % NPT
                ob, obr = acc(j)
                mm(ob[:, qlo * 128:BLK], vaug[:, kt, h, 0:128], pT[pi][:, 0:nn], kt == 0, kt == nk - 1,
                   R=[f"pT{pi}", "vaug"], W=[obr], sgc=True)
                if j == 0:
                    db, dbr = acc(2)
                    mm(db[0:1, qlo * 128:BLK], onesb[:, 0:1], pT[pi][:, 0:nn], kt == 0, kt == nk - 1,
                       R=[f"pT{pi}", "cbf"], W=[dbr], sgc=True)
                    continue
                eng = "dve"
                ps_ = pS[hb][j]; psn = f"pS{hb}{j}"
                if kt == 0:
                    cp(eng, ps_[:, :], pT[pi][:, :], R=[f"pT{pi}"], W=[psn])
                else:
                    tt(eng, ps_[:, qlo * 128:BLK], ps_[:, qlo * 128:BLK], pT[pi][:, 0:nn], ALU.add, R=[psn, f"pT{pi}"], W=[psn])

        def fin1(h):
            for j in range(2):
                ob, obr = acc(j)
                cp("dve" if j == 0 else "act", oS[j][:, :], ob[:, :], R=[obr], W=[f"oS{j}"])
            db, dbr = acc(2)
            cp("act", pS[h % 2][0][0:1, :], db[0:1, :], R=[dbr], W=[f"pS{h % 2}0"])

        def fin2(h):
            hb = h % 2
            for j in range(2):
                pb_, pbr_ = psum()
                if j == 0:
                    mm(pb_[:, :], ones32[0:1, :], pS[hb][0][0:1, :], True, True, R=["c32", f"pS{hb}0"], W=[pbr_])
                else:
                    mm(pb_[:, :], ones32, pS[hb][j][:, :], True, True, R=["c32", f"pS{hb}{j}"], W=[pbr_])
                dtmp, dname = (rtb[1], "rtb1") if j == 0 else (rinv[1], "rinv1")
                act(dtmp[:, :], pb_[:, :], AF.Ln, R=[pbr_], W=[dname])
                act(t1[j][:, :], dtmp[:, :], AF.Exp, R=[dname], W=[f"t1{j}"], scale=-1.0)
            tt("dve", t2[0][:, :], oS[0][:, :], t1[0][:, :], ALU.mult, R=["oS0", "t10"], W=["t20"])
            stt(t2[1][:, :], oS[1][:, :], neglam[:, 0:1], t1[1][:, :], ALU.mult, ALU.mult, R=["oS1", "neglam", "t11"], W=["t21"])
            tt("dve", t2[1][:, :], t2[1][:, :], t2[0][:, :], ALU.add, R=["t20", "t21"], W=["t21"])
            act(sqb[0][:, :], t2[1][:, :], AF.Square, R=["t21"], W=["sqb0"])

        def fin3(h):
            pss_, psr_ = psum()
            mm(pss_[:, :], onesb, sqb[0][:, :], True, True, R=["cbf", "sqb0"], W=[psr_])
            act(rtb[0][:, :], pss_[:, :], AF.Ln, R=[psr_, "epsc"], W=["rtb0"], scale=1.0 / 128, bias=epsc[:, 0:1])
            act(rinv[0][:, :], rtb[0][:, :], AF.Exp, R=["rtb0"], W=["rinv0"], scale=-0.5)
            stt(yT[:, 8 + h, :], t2[1][:, :], subw[:, 0:1], rinv[0][:, :], ALU.mult, ALU.mult,
                R=["t21", "subw", "rinv0"], W=["yT"])

        deferred = []
        QK(0)
        if NI > 1:
            QK(1)
        for n in range(NI):
            if n + 2 < NI:
                QK(n + 2)
            PV(n)
            for item in [d_ for d_ in deferred if d_[0] <= n]:
                deferred.remove(item)
                item[1]()
            h, kt = iters[n]
            if kt == nk - 1:
                fin1(h)
                deferred.append((n + 1, (lambda hh=h: fin2(hh))))
                deferred.append((n + 3, (lambda hh=h: fin3(hh))))
        for item in sorted(deferred, key=lambda d_: d_[0]):
            item[1]()

        if debug is not None and debug[0] == "yT":
            for c in range(16):
                cp("act", st[:, 0:512], yT[:, c, :], R=["yT"], W=["st"])
                P.dma("sp", out_d[c * 128:(c + 1) * 128, 0:512], st[:, 0:512], R=["st"], W=["outd"], sem="dbg")
                P.op("act", lambda e: e.copy(stat[:, 60:61], stat[:, 60:61]), R=["st"], W=["stat"])
            if nxt is not None:
                step1(nxt[0], nxt[1], range(4), "ab")
            return
        for ch in range(2):
            cs = slice(ch * 512, (ch + 1) * 512)
            for kh in range(2):
                w, wr = wload(wv_out[:, kh * 8:(kh + 1) * 8, cs], key="wb_out")
                for i in range(4):
                    bk, bkr = acc(i)
                    for kc in range(8):
                        mm(bk[:, :], yT[:, kh * 8 + kc, i * 128:(i + 1) * 128], w[:, kc, :], kh == 0 and kc == 0,
                           kh == 1 and kc == 7, R=["yT", wr], W=[bkr])
            for i in range(4):
                bk, bkr = acc(i)
                xr = xt[i % 2]
                P.dma("sp", xr[:, 0:512], x_d[tok0 + i * 128: tok0 + (i + 1) * 128, cs], W=[f"xt{i % 2}"], sem=f"xt{i % 2}")
                tt("dve", x1[i][:, cs], bk[:, :], xr[:, 0:512], ALU.add, R=[bkr, f"xt{i % 2}"], W=[f"x1_{i}"])
        if debug is not None and debug[0] == "x1":
            for i in range(4):
                P.dma("sp", out_d[tok0 + i * 128: tok0 + (i + 1) * 128, :], x1[i][:, :], R=[f"x1_{i}"], W=["outd"],
                      sem=f"o{i}", join=True)
            if nxt is not None:
                step1(nxt[0], nxt[1], range(4), "ab")
            return
        for i in range(4):
            act(junk[:, :], x1[i][:, :], AF.Square, R=[f"x1_{i}"], W=["junk"])
            red(stat[:, 32 + i:33 + i], junk[:, :], R=["junk"], W=["stat"])
        rstd_from_ss(stat[:, 32:36], 4, 1.0 / D, "stat")
        for i in range(4):
            xb = xn[i % 2]
            P.op("act", (lambda e, o=xb[:, :], a=x1[i][:, :], sc=stat[:, 32 + i:33 + i]: e.activation(o, a, AF.Copy, scale=sc)),
                 R=[f"x1_{i}", "stat"], W=[f"xn{i % 2}"])
            ps, pr = psum()
            pb = ps[:, :].bitcast(BF16)
            for d in range(8):
                tr(pb[:, d * 128:(d + 1) * 128], xb[:, d * 128:(d + 1) * 128], identb, R=[f"xn{i % 2}", "cbf"], W=[pr])
            tt("dve", hT[:, :, i * 128:(i + 1) * 128], pb.rearrange("p (d t) -> p d t", t=128),
               w2T[:, :].unsqueeze(2).broadcast_to([128, 8, 128]), ALU.mult, R=[pr, "w2T"], W=["hT"])
        if nxt is not None:
            step1(nxt[0], nxt[1], range(0, 2), "a")
        for fg in range(6):
            ncol = 512 if fg < 5 else 256
            wgt, wgr = wload(wv_g[:, :, fg * 512: fg * 512 + ncol], ncol=ncol, key="wb_g")
            wut, wur = wload(wv_u[:, :, fg * 512: fg * 512 + ncol], ncol=ncol, key="wb_u")
            for j in range(ncol // 128):
                psg, pgr = fm_chunk(wgt, wgr, j, hT, "hT")
                psu, pur = fm_chunk(wut, wur, j, hT, "hT")
                act(sg[:, :], psg[:, :], AF.Silu, R=[pgr], W=["sg"])
                tt("dve", aT[:, fg * 4 + j, :], sg[:, :], psu[:, :], ALU.mult, R=["sg", pur], W=["aT"])
        if nxt is not None:
            step1(nxt[0], nxt[1], range(0, 2), "b")
            step1(nxt[0], nxt[1], range(2, 4), "ab")
        for ch in range(2):
            cs = slice(ch * 512, (ch + 1) * 512)
            for kg in range(3):
                nk = 8 if kg < 2 else 6
                w, wr = wload(wv_d[:, kg * 8: kg * 8 + nk, cs], nk=nk, key="wb_d")
                for i in range(4):
                    bk, bkr = acc(i)
                    for kc in range(nk):
                        mm(bk[:, :], aT[:, kg * 8 + kc, i * 128:(i + 1) * 128], w[:, kc, :], kg == 0 and kc == 0,
                           kg == 2 and kc == nk - 1, R=["aT", wr], W=[bkr])
            for i in range(4):
                bk, bkr = acc(i)
                tt("dve", x1[i][:, cs], bk[:, :], x1[i][:, cs], ALU.add, R=[bkr, f"x1_{i}"], W=[f"x1_{i}"])
        for i in range(4):
            P.dma("sp", out_d[tok0 + i * 128: tok0 + (i + 1) * 128, :], x1[i][:, :], R=[f"x1_{i}"], W=["outd"],
                  sem=f"o{i}", join=True)

    order = [(s, b) for s in range(nseq) for b in range(nblk)]
    step1(order[0][0], order[0][1], range(4), "ab")
    for n_, (s, b) in enumerate(order):
        emit_block(s, b, order[n_ + 1] if n_ + 1 < len(order) else None)
    P.final_wait("sp", [f"o{i}" for i in range(4)] + ["dbg"])
    P.emit(es)
    es.close()
    return nc


_CONSTS = None


def kernel(**inputs):
    global _CONSTS
    n = 8
    x = np.ascontiguousarray(inputs["x"], dtype=np.float32)
    B = x.shape[0]
    per = B // n
    if _CONSTS is None:
        _CONSTS = host_consts()
    c32, cb, cosT, sinT = _CONSTS
    nc = build_program(nseq=per)
    shared = {
        "w_in": inputs["w_in"][0], "w_out": inputs["w_out"][0], "w_gate": inputs["w_gate"][0],
        "w_up": inputs["w_up"][0], "w_down": inputs["w_down"][0],
        "norm1_w": inputs["norm1_w"], "norm2_w": inputs["norm2_w"], "ssd_norm_w": inputs["ssd_norm_w"],
        "conv_w": inputs["conv_w"][0], "conv_b": inputs["conv_b"],
        "dt_bias": inputs["dt_bias"], "a_log": inputs["a_log"], "d_skip": inputs["d_skip"],
        "q_norm_w": inputs["q_norm_w"], "k_norm_w": inputs["k_norm_w"],
        "lambda_q1": inputs["lambda_q1"], "lambda_k1": inputs["lambda_k1"],
        "lambda_q2": inputs["lambda_q2"], "lambda_k2": inputs["lambda_k2"],
        "subln_w": inputs["subln_w"],
        "c32": c32, "cbf": cb, "cosT": cosT, "sinT": sinT,
    }
    shared = {k: np.ascontiguousarray(v, dtype=np.float32) for k, v in shared.items()}
    in_maps = []
    for c in range(n):
        m = dict(shared)
        m["x"] = x[c * per:(c + 1) * per].reshape(per * SEQ, D)
        in_maps.append(m)
    res = run_bass_kernel_spmd(nc, in_maps, core_ids=list(range(n)))
    out = np.concatenate([np.asarray(r["out"]).reshape(per, SEQ, D) for r in res.results], axis=0)
    return out.astype(np.float32)
```

```python
import math
from contextlib import ExitStack
import numpy as np
import concourse.bass as bass
import concourse.mybir as mybir
from concourse.bass_utils import run_bass_kernel_spmd

F32 = mybir.dt.float32
BF16 = mybir.dt.bfloat16
AF = mybir.ActivationFunctionType
ALU = mybir.AluOpType
AX = mybir.AxisListType

D = 1024
SEQ = 2048
NSEQ_CORE = 4
BLK = 512
IN_DIM = 5648
DFF = 2816
EPS = 1e-6
LAMBDA_INIT = 0.8 - 0.6 * math.exp(0.0)
NEG = -30000.0

ENG_ATTR = {"pe": "tensor", "act": "scalar", "dve": "vector", "pool": "gpsimd", "sp": "sync"}


class Prog:
    def __init__(self, nc):
        self.nc = nc
        self.ops = {e: [] for e in ENG_ATTR}
        self.res = {}
        self.seen = {e: {} for e in ENG_ATTR}
        self.marked = {e: set() for e in ENG_ATTR}
        self.semcnt = {}
        self.tags = {}
        self.tag_last = {}

    def settag(self, name, tag):
        self.tags[name] = tag

    def _deps(self, eng, R, W, join):
        deps = set()
        raw = set()
        for r in R:
            st = self.res.setdefault(r, {"w": [], "r": []})
            for t in st["w"]:
                deps.add(t); raw.add(t)
        for w in W:
            st = self.res.setdefault(w, {"w": [], "r": []})
            for t in st["w"]:
                if join and t[0] == "s":
                    continue
                deps.add(t)
            for t in st["r"]:
                deps.add(t)
        touched = set(self.tags[x] for x in list(R) + list(W) if x in self.tags)
        for tg in touched:
            for other, last in self.tag_last.items():
                if other != tg:
                    for k, v in last.items():
                        deps.add((k[0], k[1], v)); raw.add((k[0], k[1], v))
        out = []
        for t in deps:
            kind, key, val = t
            if kind == "e" and key == eng and t not in raw:
                continue
            sk = (kind, key)
            if self.seen[eng].get(sk, 0) >= val:
                continue
            out.append(t)
        best = {}
        for kind, key, val in out:
            best[(kind, key)] = max(best.get((kind, key), 0), val)
        waits = []
        for (kind, key), val in best.items():
            self.seen[eng][(kind, key)] = val
            if kind == "e":
                self.marked[key].add(val)
            waits.append((kind, key, val))
        return waits, touched

    def _commit(self, tok, R, W, join, touched):
        for r in R:
            self.res[r]["r"].append(tok)
        for w in W:
            st = self.res[w]
            if join and st["w"] and all(t[0] == "s" for t in st["w"]) and not st["r"]:
                st["w"].append(tok)
            else:
                st["w"] = [tok]
                st["r"] = []
        for tg in touched:
            d = self.tag_last.setdefault(tg, {})
            k = (tok[0], tok[1])
            d[k] = max(d.get(k, 0), tok[2])

    def op(self, eng, fn, R=(), W=()):
        waits, touched = self._deps(eng, R, W, False)
        idx = len(self.ops[eng]) + 1
        self.ops[eng].append({"fn": fn, "waits": waits, "idx": idx, "dma": None})
        self._commit(("e", eng, idx), R, W, False, touched)

    def dma(self, q, out, in_, R=(), W=(), sem="d", join=False, **kw):
        waits, touched = self._deps(q, R, W, join)
        idx = len(self.ops[q]) + 1
        self.semcnt[sem] = self.semcnt.get(sem, 0) + 16
        fn = (lambda e, o=out, i=in_, k=kw: e.dma_start(out=o, in_=i, allow_slow_non_contiguous=True, **k))
        self.ops[q].append({"fn": fn, "waits": waits, "idx": idx, "dma": sem})
        self._commit(("s", sem, self.semcnt[sem]), R, W, join, touched)

    def retoken(self, names, sem):
        for n_ in names:
            self.res[n_]["w"] = [("s", sem, self.semcnt[sem])]

    def final_wait(self, q, sems):
        waits = [("s", s, self.semcnt[s]) for s in sems if s in self.semcnt]
        idx = len(self.ops[q]) + 1
        self.ops[q].append({"fn": None, "waits": waits, "idx": idx, "dma": None})

    def emit(self, es):
        nc = self.nc
        semh = {}
        for e in ENG_ATTR:
            semh[("e", e)] = es.enter_context(nc.semaphore("sem_" + e))
        for s in self.semcnt:
            semh[("s", s)] = es.enter_context(nc.semaphore("dsem_" + s))
        cnt = {}
        for e in ENG_ATTR:
            m = sorted(self.marked[e])
            cnt[e] = {idx: i + 1 for i, idx in enumerate(m)}
        block = es.enter_context(nc.Block())

        def replay(ename, eng):
            for o in self.ops[ename]:
                for kind, key, val in o["waits"]:
                    if kind == "e":
                        eng.wait_ge(semh[("e", key)], cnt[key][val])
                    else:
                        eng.wait_ge(semh[("s", key)], val)
                if o["fn"] is None:
                    continue
                ins = o["fn"](eng)
                if o["dma"] is not None:
                    ins.then_inc(semh[("s", o["dma"])], 16)
                elif o["idx"] in cnt[ename]:
                    ins.then_inc(semh[("e", ename)], 1)

        @block.tensor
        def _(e):
            replay("pe", e)

        @block.scalar
        def _(e):
            replay("act", e)

        @block.vector
        def _(e):
            replay("dve", e)

        @block.gpsimd
        def _(e):
            replay("pool", e)

        @block.sync
        def _(e):
            replay("sp", e)


def host_consts():
    c32 = np.zeros((128, 1024), np.float32)
    t = np.arange(128)
    c32[:, 0:128] = np.eye(128)
    c32[:, 128:256] = (t[:, None] <= t[None, :])
    c32[:, 256:384] = -1.0 * (t[:, None] <= t[None, :])
    c32[:, 384:512] = 1.0
    mk = np.where(t[None, :] < t[:, None], NEG, 0.0)
    c32[:, 512:1024] = np.tile(mk, (1, 4))
    cb = np.zeros((128, 1024), np.float32)
    cb[:, 384:512] = 1.0
    cb[:, 512:1024] = c32[:, 512:1024]
    cb[:, 0:128] = np.eye(128)
    cb[:, 128:256] = (t[:, None] // 64 == t[None, :] // 64)
    rt = np.zeros((128, 128), np.float32)
    for m in range(128):
        r = m % 64
        if r < 8:
            rt[m + 8, m] = -1.0
        elif r < 16:
            rt[m - 8, m] = 1.0
    cb[:, 256:384] = rt
    pos = np.arange(SEQ, dtype=np.float32)
    inv_freq = (500000.0 ** (-np.arange(0, 16, 2, dtype=np.float32) / 16)).astype(np.float32)
    ang = (pos[:, None] * inv_freq[None, :]).astype(np.float32)
    cosT = np.ones((128, SEQ), np.float32)
    sinT = np.zeros((128, SEQ), np.float32)
    for p in range(128):
        r = p % 64
        if r < 16:
            cosT[p] = np.cos(ang[:, r % 8])
            sinT[p] = np.sin(ang[:, r % 8])
    return c32, cb, cosT, sinT


def build_program(nseq=NSEQ_CORE, nblk=SEQ // BLK, debug=None):
    nc = bass.Bass("TRN2", target_bir_lowering=False)
    es = ExitStack()
    ntok = nseq * SEQ

    def din(name, shape):
        return nc.dram_tensor(name, list(shape), F32, kind="ExternalInput").ap()

    x_d = din("x", [ntok, D])
    out_d = nc.dram_tensor("out", [ntok, D], F32, kind="ExternalOutput").ap()
    w_in_d = din("w_in", [D, IN_DIM])
    w_out_d = din("w_out", [2 * D, D])
    w_g_d = din("w_gate", [D, DFF])
    w_u_d = din("w_up", [D, DFF])
    w_d_d = din("w_down", [DFF, D])
    norm1_d = din("norm1_w", [1, D]); norm2_d = din("norm2_w", [1, D]); ssdn_d = din("ssd_norm_w", [1, D])
    convw_d = din("conv_w", [4, 1536]); convb_d = din("conv_b", [1, 1536])
    dtb_d = din("dt_bias", [1, 16]); alog_d = din("a_log", [1, 16]); dsk_d = din("d_skip", [1, 16])
    qnw_d = din("q_norm_w", [1, 64]); knw_d = din("k_norm_w", [1, 64])
    lq1_d = din("lambda_q1", [1, 64]); lk1_d = din("lambda_k1", [1, 64])
    lq2_d = din("lambda_q2", [1, 64]); lk2_d = din("lambda_k2", [1, 64])
    subln_d = din("subln_w", [1, 128])
    c32_d = din("c32", [128, 1024]); cbf_d = din("cbf", [128, 1024])
    cos_d = din("cosT", [128, SEQ]); sin_d = din("sinT", [128, SEQ])

    def dscr(name, shape):
        return nc.dram_tensor(name, list(shape), BF16, kind="Internal").ap()

    wb_in = dscr("wb_in", [D, IN_DIM]); wb_out = dscr("wb_out", [2 * D, D])
    wb_g = dscr("wb_g", [D, DFF]); wb_u = dscr("wb_u", [D, DFF]); wb_d = dscr("wb_d", [DFF, D])
    dbg_d = None
    if debug is not None and debug[0] == "hT":
        dbg_d = nc.dram_tensor("dbg", list(debug[1]), F32, kind="ExternalOutput").ap()

    P = Prog(nc)

    def sb(name, shape, dt=F32):
        return es.enter_context(nc.sbuf_tensor("s_" + name, list(shape), dt))

    c32 = sb("c32", [128, 1024]); cbf = sb("cbf", [128, 1024], BF16)
    ident32 = c32[:, 0:128]; tri32 = c32[:, 128:256]; negtri32 = c32[:, 256:384]; ones32 = c32[:, 384:512]
    mask4 = c32[:, 512:1024]
    identb = cbf[:, 0:128]; blockones = cbf[:, 128:256]; rotT = cbf[:, 256:384]; onesb = cbf[:, 384:512]; mask4b = cbf[:, 512:1024]
    diagw = sb("diagw", [128, 12, 4, 128], BF16)
    w1T = sb("w1T", [128, 8]); w2T = sb("w2T", [128, 8]); wssdT = sb("wssdT", [128, 8])
    convwT = sb("convwT", [128, 4, 12]); convbT = sb("convbT", [128, 12])
    wq2 = sb("wq2", [128, 1]); wk2 = sb("wk2", [128, 1])
    sublnw = sb("sublnw", [128, 128]); subw = sb("subw", [128, 1])
    dtb4 = sb("dtb4", [128, 64]); a4 = sb("a4", [128, 64]); dsk = sb("dsk", [128, 16])
    lam4 = sb("lam4", [128, 4, 64]); lamt = sb("lamt", [128, 8]); neglam = sb("neglam", [128, 1])
    cosb = sb("cosb", [128, BLK]); sinb = sb("sinb", [128, BLK])
    wdt = sb("wdt", [128, 8, 16], BF16)
    kT = sb("kT", [128, 8, SEQ], BF16)
    vaug = sb("vaug", [128, 16, 8, 130], BF16)
    st = sb("st", [128, 1024]); stbf = sb("stbf", [128, 1024], BF16)
    xt = [sb(f"xt{i}", [128, 1024]) for i in range(2)]
    hT = sb("hT", [128, 8, BLK], BF16)
    yT = sb("yT", [128, 16, BLK], BF16)
    NW = 2
    wg = [sb(f"wg{i}", [128, 8, 512], BF16) for i in range(NW)]
    junk = sb("junk", [128, 1024], BF16)
    xn = [sb(f"xn{i}", [128, 1024], BF16) for i in range(2)]
    stat = sb("stat", [128, 64])
    halo = sb("halo", [128, 12, 4], BF16)
    epsc = sb("epsc", [128, 2])
    REG = 55 * 1024 + 512
    region = sb("region", [128, REG // 4])
    roff = {"S": 0, "T": 0, "F": 0}

    def rg(phase, name, shape, dt=F32):
        n = int(np.prod(shape[1:]))
        nb = n * (4 if dt == F32 else 2)
        nb = (nb + 31) // 32 * 32
        o = roff[phase]
        roff[phase] = o + nb
        assert roff[phase] <= REG, (phase, name, roff[phase])
        ap = region[:, o // 4:(o + nb) // 4]
        if dt != F32:
            ap = ap.bitcast(dt)
        ap = ap[:, 0:n]
        if len(shape) == 3:
            ap = ap.rearrange("p (a b) -> p a b", b=shape[2])
        elif len(shape) == 4:
            ap = ap.rearrange("p (a b c) -> p a b c", b=shape[2], c=shape[3])
        P.settag(name, phase)
        return ap

    zs = rg("S", "zs", [128, 4, 1024], BF16)
    xbcT = rg("S", "xbcT", [128, 12, 516], BF16)
    xsT = rg("S", "xsT", [128, 8, BLK], BF16)
    bcT = rg("S", "bcT", [128, 4, BLK], BF16)
    dtt = rg("S", "dtt", [128, 64]); adt = rg("S", "adt", [128, 64])
    xdt = rg("S", "xdt", [128, 1024], BF16); xsD = rg("S", "xsD", [128, 1024], BF16)
    R1 = [rg("S", f"R1{i}", [128, 4, 128]) for i in range(2)]
    dec = [rg("S", f"dec{i}", [128, 4, 128], BF16) for i in range(2)]
    MT = [rg("S", f"MT{i}", [128, 4, 128], BF16) for i in range(2)]
    cbT = rg("S", "cbT", [128, 256], BF16)
    ybuf = rg("S", "ybuf", [128, 1024])
    sp_t = ybuf[:, 0:256].rearrange("p (a b) -> p a b", b=64)
    gn = rg("S", "gn", [128, 1024], BF16)
    xdtE = rg("S", "xdtE", [128, 1024], BF16)
    Btok = rg("S", "Btok", [128, 256], BF16)
    acs = rg("S", "acs", [128, 96])
    qT = rg("T", "qT", [128, 8, BLK], BF16)
    sqb = [rg("T", f"sqb{i}", [128, BLK], BF16) for i in range(2)]
    rtb = [rg("T", f"rtb{i}", [128, BLK]) for i in range(2)]
    rinv = [rg("T", f"rinv{i}", [128, BLK]) for i in range(2)]
    qn = [rg("T", f"qn{i}", [128, BLK]) for i in range(2)]
    qnb = [rg("T", f"qnb{i}", [128, BLK], BF16) for i in range(2)]
    t1 = [rg("T", f"t1{i}", [128, BLK]) for i in range(2)]
    t2 = [rg("T", f"t2{i}", [128, BLK]) for i in range(2)]
    pT = [rg("T", f"pT{i}", [128, BLK], BF16) for i in range(6)]
    oS = [rg("T", f"oS{i}", [128, BLK]) for i in range(2)]
    pS = [[rg("T", f"pS{a}{c}", [128, BLK]) for c in range(2)] for a in range(2)]
    aT = rg("F", "aT", [128, 22, BLK], BF16)
    sg4 = [rg("F", f"sg{i}", [128, BLK]) for i in range(4)]
    x1 = [rg("F", f"x1_{i}", [128, 1024]) for i in range(4)]

    banks = [es.enter_context(nc.psum_tensor(f"ps{i}", [128, 512], F32)) for i in range(8)]
    rr = [0]

    def psum():
        i = 4 + rr[0] % 4
        rr[0] += 1
        return banks[i], f"ps{i}"

    def acc(i):
        return banks[i], f"ps{i}"

    rr6 = [0]

    def psum6():
        i = 3 + rr6[0] % 5
        rr6[0] += 1
        return banks[i], f"ps{i}"

    rr8 = [0]

    def psum8():
        i = rr8[0] % 8
        rr8[0] += 1
        return banks[i], f"ps{i}"

    def mm(out, lhsT, rhs, start, stop, R, W, sgc=False):
        P.op("pe", lambda e: e.matmul(out, lhsT, rhs, start=start, stop=stop, skip_group_check=sgc), R=R, W=W)

    def tr(out, in_, ident, R, W):
        P.op("pe", lambda e: e.transpose(out, in_, ident), R=R, W=W)

    def act(out, in_, func, R, W, scale=1.0, bias=None):
        if bias is None:
            P.op("act", lambda e: e.activation(out, in_, func, scale=scale), R=R, W=W)
        else:
            P.op("act", lambda e: e.activation(out, in_, func, bias=bias, scale=scale), R=R, W=W)

    def tt(eng, out, in0, in1, op, R, W):
        P.op(eng, lambda e: e.tensor_tensor(out, in0, in1, op), R=R, W=W)

    def ts(eng, out, in0, s1, s2, op0, op1, R, W):
        P.op(eng, lambda e: e.tensor_scalar(out, in0, s1, s2, op0, op1), R=R, W=W)

    def stt(out, in0, scalar, in1, op0, op1, R, W):
        P.op("dve", lambda e: e.scalar_tensor_tensor(out, in0, scalar, in1, op0, op1), R=R, W=W)

    def red(out, in_, R, W):
        P.op("dve", lambda e: e.tensor_reduce(out, in_, AX.X, ALU.add), R=R, W=W)

    def recip(out, in_, R, W):
        P.op("dve", lambda e: e.reciprocal(out, in_), R=R, W=W)

    def cp(eng, out, in_, R, W):
        if eng == "act":
            P.op("act", lambda e: e.copy(out, in_), R=R, W=W)
        else:
            P.op(eng, lambda e: e.tensor_copy(out, in_), R=R, W=W)

    def mset(eng, ap, val, W):
        P.op(eng, lambda e: e.memset(ap, val), W=W)

    def rstd_from_ss(ss_ap, n, inv_n, name):
        act(ss_ap, ss_ap, AF.Ln, R=[name, "epsc"], W=[name], scale=inv_n, bias=epsc[:, 0:1])
        act(ss_ap, ss_ap, AF.Exp, R=[name], W=[name], scale=-0.5)

    ncd = nc.allow_non_contiguous_dma(reason="tiny parameter layout loads")
    ncd.__enter__()
    for (src, dst, rows, key) in ((w_in_d, wb_in, D, "wb_in"), (w_out_d, wb_out, 2 * D, "wb_out"),
                                  (w_g_d, wb_g, D, "wb_g"), (w_u_d, wb_u, D, "wb_u"),
                                  (w_d_d, wb_d, DFF, "wb_d")):
        for r0 in range(0, rows, 128):
            P.dma("pool", dst[r0:r0 + 128, :], src[r0:r0 + 128, :], W=[key], sem=key, join=True)
    P.dma("sp", c32[:, :], c32_d[:, :], W=["c32"], sem="cst")
    P.dma("pool", cbf[:, :], cbf_d[:, :], W=["cbf"], sem="cstb")
    P.dma("sp", w1T[:, :], norm1_d[0, :].rearrange("(d p) -> p d", p=128), W=["w1T"], sem="cst")
    P.dma("sp", w2T[:, :], norm2_d[0, :].rearrange("(d p) -> p d", p=128), W=["w2T"], sem="cst")
    P.dma("sp", wssdT[:, :], ssdn_d[0, :].rearrange("(d p) -> p d", p=128), W=["wssdT"], sem="cst")
    for k in range(4):
        P.dma("sp", convwT[:, k, :], convw_d[k, :].rearrange("(c p) -> p c", p=128), W=["convwT"], sem="cst", join=True)
    P.dma("sp", convbT[:, :], convb_d[0, :].rearrange("(c p) -> p c", p=128), W=["convbT"], sem="cst")
    for hh in range(2):
        P.dma("sp", wq2[hh * 64:(hh + 1) * 64, :], qnw_d[0, :].rearrange("(p o) -> p o", o=1), W=["wq2"], sem="cst", join=True)
        P.dma("sp", wk2[hh * 64:(hh + 1) * 64, :], knw_d[0, :].rearrange("(p o) -> p o", o=1), W=["wk2"], sem="cst", join=True)
    P.dma("sp", sublnw[:, :], subln_d[0:1, :].broadcast_to([128, 128]), W=["sublnw"], sem="cst")
    P.dma("sp", subw[:, :], subln_d[0, :].rearrange("(p o) -> p o", o=1), W=["subw"], sem="cst")
    for i in range(4):
        P.dma("sp", dtb4[:, i * 16:(i + 1) * 16], dtb_d[0:1, :].broadcast_to([128, 16]), W=["dtb4"], sem="cst", join=True)
        P.dma("sp", a4[:, i * 16:(i + 1) * 16], alog_d[0:1, :].broadcast_to([128, 16]), W=["a4"], sem="cst", join=True)
    P.dma("sp", dsk[:, :], dsk_d[0:1, :].broadcast_to([128, 16]), W=["dsk"], sem="cst")
    for i, ld in enumerate((lq1_d, lk1_d, lq2_d, lk2_d)):
        P.dma("sp", lam4[:, i, :], ld[0:1, :].broadcast_to([128, 64]), W=["lam4"], sem="cst", join=True)
    P.dma("sp", wdt[:, :, :], wb_in.rearrange("(d p) c -> p d c", p=128)[:, :, 2560:2576], R=["wb_in"], W=["wdt"], sem="cst")
    ncd.__exit__(None, None, None)
    P.retoken(["c32", "w1T", "w2T", "wssdT", "convwT", "convbT", "wq2", "wk2", "sublnw", "subw", "dtb4", "a4", "dsk", "lam4", "wdt"], "cst")
    act(a4[:, :], a4[:, :], AF.Exp, R=["a4"], W=["a4"])
    ts("dve", a4[:, :], a4[:, :], -1.0, None, ALU.mult, ALU.bypass, R=["a4"], W=["a4"])
    tt("dve", lam4[:, 0, :], lam4[:, 0, :], lam4[:, 1, :], ALU.mult, R=["lam4"], W=["lam4"])
    tt("dve", lam4[:, 2, :], lam4[:, 2, :], lam4[:, 3, :], ALU.mult, R=["lam4"], W=["lam4"])
    red(lamt[:, 0:1], lam4[:, 0, :], R=["lam4"], W=["lamt"])
    red(lamt[:, 1:2], lam4[:, 2, :], R=["lam4"], W=["lamt"])
    act(lamt[:, 2:4], lamt[:, 0:2], AF.Exp, R=["lamt"], W=["lamt"])
    tt("dve", lamt[:, 4:5], lamt[:, 3:4], lamt[:, 2:3], ALU.subtract, R=["lamt"], W=["lamt"])
    ts("dve", neglam[:, :], lamt[:, 4:5], -LAMBDA_INIT, None, ALU.add, ALU.bypass, R=["lamt"], W=["neglam"])
    ts("dve", sublnw[:, :], sublnw[:, :], 1.0 - LAMBDA_INIT, None, ALU.mult, ALU.bypass, R=["sublnw"], W=["sublnw"])
    ts("dve", subw[:, :], subw[:, :], 1.0 - LAMBDA_INIT, None, ALU.mult, ALU.bypass, R=["subw"], W=["subw"])
    for c in range(12):
        for k in range(4):
            ts("dve", diagw[:, c, k, :], ident32, convwT[:, k, c:c + 1], None, ALU.mult, ALU.bypass,
               R=["c32", "convwT"], W=["diagw"])
    mset("pool", vaug[:, :, :, 128:130], 1.0, W=["vaug"])
    mset("pool", epsc[:, :], EPS, W=["epsc"])

    wctr = [0]

    def wload(src_ap, nk=8, ncol=512, key="wb_in"):
        i = wctr[0] % NW
        wctr[0] += 1
        P.dma("sp", wg[i][:, 0:nk, 0:ncol], src_ap, R=[key], W=[f"wg{i}"], sem=f"wg{i}")
        return wg[i], f"wg{i}"

    wv_in = wb_in.rearrange("(d p) c -> p d c", p=128)
    wv_out = wb_out.rearrange("(k p) c -> p k c", p=128)
    wv_g = wb_g.rearrange("(d p) c -> p d c", p=128)
    wv_u = wb_u.rearrange("(d p) c -> p d c", p=128)
    wv_d = wb_d.rearrange("(k p) c -> p k c", p=128)

    def fm_chunk(w, wr, j, src, srcname):
        ps, pr = psum()
        for d in range(8):
            mm(ps[:, :], w[:, d, j * 128:(j + 1) * 128], src[:, d, :], d == 0, d == 7, R=[wr, srcname], W=[pr])
        return ps, pr

    def tm_tile(w, wr, i, ncol=512):
        ps, pr = psum()
        for d in range(8):
            mm(ps[:, 0:ncol], hT[:, d, i * 128:(i + 1) * 128], w[:, d, 0:ncol], d == 0, d == 7, R=[wr, "hT"], W=[pr])
        return ps, pr

    def norm_to_hT(src_tiles, src_names, wT, wTname):
        for i in range(4):
            act(junk[:, :], src_tiles[i], AF.Square, R=[src_names[i]], W=["junk"])
            red(stat[:, i:i + 1], junk[:, :], R=["junk"], W=["stat"])
        rstd_from_ss(stat[:, 0:4], 4, 1.0 / D, "stat")
        for i in range(4):
            xb = xn[i % 2]
            ts("dve", xb[:, :], src_tiles[i], stat[:, i:i + 1], None, ALU.mult, ALU.bypass,
               R=[src_names[i], "stat"], W=[f"xn{i % 2}"])
            ps, pr = psum()
            pb = ps[:, :].bitcast(BF16)
            for d in range(8):
                tr(pb[:, d * 128:(d + 1) * 128], xb[:, d * 128:(d + 1) * 128], identb, R=[f"xn{i % 2}", "cbf"], W=[pr])
            tt("dve", hT[:, :, i * 128:(i + 1) * 128], pb.rearrange("p (d t) -> p d t", t=128),
               wT[:, :].unsqueeze(2).broadcast_to([128, 8, 128]), ALU.mult, R=[pr, wTname], W=["hT"])

    def dump(ap, resname, rows=None):
        P.dma("sp", dbg_d if rows is None else dbg_d[rows], ap, R=[resname], W=["dbgout"], sem="dbg", join=True)

    def step1(s, b, tiles, part):
        tok0 = s * SEQ + b * BLK
        for i in tiles:
            xb = xn[i % 2]
            if "a" in part:
                P.dma("sp", xt[i % 2][:, :], x_d[tok0 + i * 128: tok0 + (i + 1) * 128, :], W=[f"xt{i % 2}"], sem=f"xt{i % 2}")
                act(junk[:, :], xt[i % 2][:, :], AF.Square, R=[f"xt{i % 2}"], W=["junk"])
                red(stat[:, 8 + i:9 + i], junk[:, :], R=["junk"], W=["stat"])
                rstd_from_ss(stat[:, 8 + i:9 + i], 1, 1.0 / D, "stat")
                ts("dve", xb[:, :], xt[i % 2][:, :], stat[:, 8 + i:9 + i], None, ALU.mult, ALU.bypass,
                   R=[f"xt{i % 2}", "stat"], W=[f"xn{i % 2}"])
            if "b" in part:
                ps, pr = psum()
                pb = ps[:, :].bitcast(BF16)
                for d in range(8):
                    tr(pb[:, d * 128:(d + 1) * 128], xb[:, d * 128:(d + 1) * 128], identb, R=[f"xn{i % 2}", "cbf"], W=[pr])
                tt("dve", hT[:, :, i * 128:(i + 1) * 128], pb.rearrange("p (d t) -> p d t", t=128),
                   w1T[:, :].unsqueeze(2).broadcast_to([128, 8, 128]), ALU.mult, R=[pr, "w1T"], W=["hT"])

    def emit_block(s, b, nxt):
        tok0 = s * SEQ + b * BLK

        for g in range(2):
            w, wr = wload(wv_in[:, :, g * 512:(g + 1) * 512])
            for i in range(4):
                ps, pr = tm_tile(w, wr, i)
                act(zs[:, i, g * 512:(g + 1) * 512], ps[:, :], AF.Silu, R=[pr], W=["zs"])
        if b == 0:
            mset("pool", xbcT[:, :, 0:3], 0.0, W=["xbcT"])
        else:
            cp("dve", xbcT[:, :, 0:3], halo[:, :, 0:3], R=["halo"], W=["xbcT"])
        for g in range(3):
            w, wr = wload(wv_in[:, :, 1024 + g * 512: 1024 + (g + 1) * 512])
            for j in range(4):
                ps, pr = fm_chunk(w, wr, j, hT, "hT")
                cp("act", xbcT[:, g * 4 + j, 3:515], ps[:, :], R=[pr], W=["xbcT"])
        cp("dve", halo[:, :, 0:3], xbcT[:, :, 512:515], R=["xbcT"], W=["halo"])
        psd, pdr = psum()
        for i in range(4):
            for d in range(8):
                mm(psd[:, i * 16:(i + 1) * 16], hT[:, d, i * 128:(i + 1) * 128], wdt[:, d, :], d == 0, d == 7,
                   R=["hT", "wdt"], W=[pdr])
        tt("dve", sp_t[:, 0, :], psd[:, 0:64], dtb4[:, :], ALU.add, R=[pdr, "dtb4"], W=["ybuf"])
        act(sp_t[:, 1, :], sp_t[:, 0, :], AF.Abs, R=["ybuf"], W=["ybuf"])
        act(sp_t[:, 2, :], sp_t[:, 1, :], AF.Exp, R=["ybuf"], W=["ybuf"], scale=-1.0)
        ts("dve", sp_t[:, 2, :], sp_t[:, 2, :], 1.0, None, ALU.add, ALU.bypass, R=["ybuf"], W=["ybuf"])
        act(sp_t[:, 3, :], sp_t[:, 2, :], AF.Ln, R=["ybuf"], W=["ybuf"])
        stt(dtt[:, :], sp_t[:, 0, :], 0.0, sp_t[:, 3, :], ALU.max, ALU.add, R=["ybuf"], W=["dtt"])
        tt("dve", adt[:, :], dtt[:, :], a4[:, :], ALU.mult, R=["dtt", "a4"], W=["adt"])
        for c in range(12):
            ps, pr = psum()
            for k in range(4):
                mm(ps[:, :], diagw[:, c, k, :], xbcT[:, c, k:k + 512], k == 0, k == 3, R=["diagw", "xbcT"], W=[pr])
            if c < 8:
                act(xsT[:, c, :], ps[:, :], AF.Silu, R=[pr, "convbT"], W=["xsT"], bias=convbT[:, c:c + 1])
            else:
                act(bcT[:, c - 8, :], ps[:, :], AF.Silu, R=[pr, "convbT"], W=["bcT"], bias=convbT[:, c:c + 1])
        if b == 0:
            mset("pool", st[:, :], 0.0, W=["st"])
            mset("pool", stbf[:, :], 0.0, W=["stbf"])
        def tsl_(i):
            return slice(i * 128, (i + 1) * 128)

        def prepA(i):
            tsl = tsl_(i)
            hsl = slice(i * 16, (i + 1) * 16)
            psx, pxr = psum()
            pxb = psx[:, :].bitcast(BF16)
            for cc in range(8):
                tr(pxb[:, cc * 128:(cc + 1) * 128], xsT[:, cc, tsl], identb, R=["xsT", "cbf"], W=[pxr])
            px3 = pxb.rearrange("p (h d) -> p h d", d=64)
            tt("dve", xdt.rearrange("p (h d) -> p h d", d=64), px3,
               dtt[:, hsl].unsqueeze(2).broadcast_to([128, 16, 64]), ALU.mult, R=[pxr, "dtt"], W=["xdt"])
            tt("dve", xsD.rearrange("p (h d) -> p h d", d=64), px3,
               dsk[:, :].unsqueeze(2).broadcast_to([128, 16, 64]), ALU.mult, R=[pxr, "dsk"], W=["xsD"])
            psa, par = psum()
            mm(psa[:, 0:16], tri32, adt[:, hsl], True, True, R=["c32", "adt"], W=[par])
            mm(psa[:, 16:32], ones32, adt[:, hsl], True, True, R=["c32", "adt"], W=[par])
            act(acs[:, 0:16], psa[:, 0:16], AF.Exp, R=[par], W=["acs"])
            cp("act", acs[:, 16:32], psa[:, 16:32], R=[par], W=["acs"])
            tt("dve", acs[:, 32:48], acs[:, 16:32], psa[:, 0:16], ALU.subtract, R=["acs", par], W=["acs"])
            act(acs[:, 48:64], acs[:, 32:48], AF.Exp, R=["acs"], W=["acs"])
            act(acs[:, 64:80], acs[:, 16:32], AF.Exp, R=["acs"], W=["acs"])
            psc, pcr = psum()
            for g in range(2):
                mm(psc[:, g * 128:(g + 1) * 128], bcT[:, g, tsl], bcT[:, 2 + g, tsl], True, True, R=["bcT"], W=[pcr])
            cp("act", cbT[:, :], psc[:, 0:256], R=[pcr], W=["cbT"])
            tt("pool", xdtE.rearrange("p (h d) -> p h d", d=64), xdt.rearrange("p (h d) -> p h d", d=64),
               acs[:, 48:64].unsqueeze(2).broadcast_to([128, 16, 64]), ALU.mult, R=["xdt", "acs"], W=["xdtE"])
            psb, pbr2 = psum()
            pbb = psb[:, :].bitcast(BF16)
            for g in range(2):
                tr(pbb[:, g * 128:(g + 1) * 128], bcT[:, g, tsl], identb, R=["bcT", "cbf"], W=[pbr2])
            cp("act", Btok[:, :], pbb[:, 0:256], R=[pbr2], W=["Btok"])

        seg_ps = {}

        def segR1(i, hg):
            h0 = i * 16 + hg * 4
            k2 = hg % 2
            tt("dve", R1[k2][:, :, :], tri32.unsqueeze(1).broadcast_to([128, 4, 128]),
               adt[:, h0:h0 + 4].unsqueeze(2).broadcast_to([128, 4, 128]), ALU.mult, R=["c32", "adt"], W=[f"R1{k2}"])

        def segMM(i, hg):
            h0 = i * 16 + hg * 4
            k2 = hg % 2
            pss, psr = psum()
            mm(pss[:, :], ones32, R1[k2].rearrange("p a b -> p (a b)"), True, False, R=["c32", f"R1{k2}"], W=[psr])
            mm(pss[:, :].rearrange("p (a b) -> p a b", b=128), negtri32, adt[:, h0:h0 + 4].unsqueeze(2).broadcast_to([128, 4, 128]),
               False, False, R=["c32", "adt"], W=[psr])
            mm(pss[:, :], identb, mask4b, False, True, R=["cbf"], W=[psr])
            seg_ps[hg] = (pss, psr)

        def segFin(i, hg):
            k2 = hg % 2
            pss, psr = seg_ps[hg]
            act(dec[k2].rearrange("p a b -> p (a b)"), pss[:, :], AF.Exp, R=[psr], W=[f"dec{k2}"])
            g = hg // 2
            m = MT[k2]
            tt("dve", m[:, :, :], dec[k2][:, :, :], cbT[:, g * 128:(g + 1) * 128].unsqueeze(1).broadcast_to([128, 4, 128]),
               ALU.mult, R=[f"dec{k2}", "cbT"], W=[f"MT{k2}"])
            for hh in range(4):
                h = hg * 4 + hh
                pb_, pbr = acc(h // 8)
                mm(pb_[:, (h % 8) * 64:(h % 8 + 1) * 64], m[:, hh, :], xdt[:, h * 64:(h + 1) * 64], True, True,
                   R=[f"MT{k2}", "xdt"], W=[pbr])

        def prepB(i, hooks=()):
            segR1(i, 0)
            segMM(i, 0)
            segR1(i, 1)
            for hg in range(4):
                if hg + 1 < 4:
                    segMM(i, hg + 1)
                if hg + 2 < 4:
                    segR1(i, hg + 2)
                segFin(i, hg)
                if hg < len(hooks):
                    hooks[hg]()

        def yoff(i):
            tsl = tsl_(i)
            for g in range(2):
                bk, bkr = acc(2 + g)
                mm(bk[:, :], bcT[:, 2 + g, tsl], stbf[:, g * 512:(g + 1) * 512], True, True, R=["bcT", "stbf"], W=[bkr])

        upd_ps = {}

        def updMM(i):
            pst2 = [psum(), psum()]
            for g in range(2):
                mm(pst2[g][0][:, :], Btok[:, g * 128:(g + 1) * 128], xdtE[:, g * 512:(g + 1) * 512], True, True,
                   R=["Btok", "xdtE"], W=[pst2[g][1]])
            upd_ps[i] = pst2

        def U1(i):
            pst2 = upd_ps[i]
            for g in range(2):
                gs = slice(g * 512, (g + 1) * 512)
                tt("dve", st[:, gs].rearrange("p (h d) -> p h d", d=64), st[:, gs].rearrange("p (h d) -> p h d", d=64),
                   acs[:, 64 + g * 8:64 + (g + 1) * 8].unsqueeze(2).broadcast_to([128, 8, 64]), ALU.mult, R=["st", "acs"], W=["st"])
                tt("dve", st[:, gs], pst2[g][0][:, :], st[:, gs], ALU.add, R=[pst2[g][1], "st"], W=["st"])
            cp("act", stbf[:, :], st[:, :], R=["st"], W=["stbf"])

        def F1(i):
            for g in range(2):
                gs = slice(g * 512, (g + 1) * 512)
                po, por_ = acc(2 + g)
                py, pyr_ = acc(g)
                tt("dve", ybuf[:, gs].rearrange("p (h d) -> p h d", d=64), po[:, :].rearrange("p (h d) -> p h d", d=64),
                   acs[:, g * 8:(g + 1) * 8].unsqueeze(2).broadcast_to([128, 8, 64]), ALU.mult, R=[por_, "acs"], W=["ybuf"])
                tt("dve", ybuf[:, gs], py[:, :], ybuf[:, gs], ALU.add, R=[pyr_, "ybuf"], W=["ybuf"])
            tt("dve", ybuf[:, :], ybuf[:, :], xsD[:, :], ALU.add, R=["ybuf", "xsD"], W=["ybuf"])

        def F2(i):
            tt("pool", ybuf[:, :], ybuf[:, :], zs[:, i, :], ALU.mult, R=["ybuf", "zs"], W=["ybuf"])
            act(junk[:, :], ybuf[:, :], AF.Square, R=["ybuf"], W=["junk"])

        def F3(i):
            red(stat[:, 16:18], junk[:, :].rearrange("p (g d) -> p g d", d=512), R=["junk"], W=["stat"])
            rstd_from_ss(stat[:, 16:18], 2, 1.0 / 512, "stat")

        def F4(i):
            for g in range(2):
                gs = slice(g * 512, (g + 1) * 512)
                ts("dve", gn[:, gs], ybuf[:, gs], stat[:, 16 + g:17 + g], None, ALU.mult, ALU.bypass, R=["ybuf", "stat"], W=["gn"])

        def finB(i):
            tsl = tsl_(i)
            pst, ptr_ = psum()
            ptb = pst[:, :].bitcast(BF16)
            for cc in range(8):
                tr(ptb[:, cc * 128:(cc + 1) * 128], gn[:, cc * 128:(cc + 1) * 128], identb, R=["gn", "cbf"], W=[ptr_])
            tt("dve", yT[:, 0:8, tsl], ptb.rearrange("p (c t) -> p c t", t=128),
               wssdT[:, :].unsqueeze(2).broadcast_to([128, 8, 128]), ALU.mult, R=[ptr_, "wssdT"], W=["yT"])

        prepA(0)
        prepB(0)
        for i in range(4):
            yoff(i)
            updMM(i)
            F1(i)
            U1(i)
            if i + 1 < 4:
                prepA(i + 1)
                prepB(i + 1, hooks=[(lambda ii=i: F2(ii)), (lambda ii=i: F3(ii)), (lambda ii=i: F4(ii))])
            else:
                F2(i); F3(i); F4(i)
            finB(i)

        P.dma("sp", cosb[:, :], cos_d[:, b * BLK:(b + 1) * BLK], W=["cosb"], sem="cos")
        P.dma("sp", sinb[:, :], sin_d[:, b * BLK:(b + 1) * BLK], W=["sinb"], sem="sin")

        qk_state = {}

        def qkA(c):
            kind, g, j = ("q", c // 4, c % 4) if c < 8 else ("k", (c - 8) // 4, c % 4)
            if j == 0:
                base = 2576 if kind == "q" else 3600
                qk_state["w"] = wload(wv_in[:, :, base + g * 512: base + (g + 1) * 512])
            w, wr = qk_state["w"]
            ps, pr = psum8()
            for d in range(8):
                mm(ps[:, :], w[:, d, j * 128:(j + 1) * 128], hT[:, d, :], d == 0, d == 7, R=[wr, "hT"], W=[pr])
            k = c % 2
            act(sqb[k][:, :], ps[:, :], AF.Square, R=[pr], W=[f"sqb{k}"])
            qk_state[c] = (ps, pr)

        def qkB(c):
            k = c % 2
            ps, pr = qk_state[c]
            wvec, wname = (wq2, "wq2") if c < 8 else (wk2, "wk2")
            ps2, pr2 = psum8()
            mm(ps2[:, :], blockones, sqb[k][:, :], True, True, R=["cbf", f"sqb{k}"], W=[pr2])
            act(rtb[k][:, :], ps2[:, :], AF.Ln, R=[pr2, "epsc"], W=[f"rtb{k}"], scale=1.0 / 64, bias=epsc[:, 0:1])
            act(rinv[k][:, :], rtb[k][:, :], AF.Exp, R=[f"rtb{k}"], W=[f"rinv{k}"], scale=-0.5)
            stt(qn[k][:, :], ps[:, :], wvec[:, 0:1], rinv[k][:, :], ALU.mult, ALU.mult, R=[pr, wname, f"rinv{k}"], W=[f"qn{k}"])
            cp("act", qnb[k][:, :], qn[k][:, :], R=[f"qn{k}"], W=[f"qnb{k}"])

        def qkC(c):
            k = c % 2
            if c < 8:
                dest, dname = qT[:, c, :], "qT"
            else:
                dest, dname = kT[:, c - 8, b * BLK:(b + 1) * BLK], "kT"
            ps3, pr3 = psum8()
            mm(ps3[:, :], rotT, qnb[k][:, :], True, True, R=["cbf", f"qnb{k}"], W=[pr3])
            tt("dve", t1[k][:, :], qn[k][:, :], cosb[:, :], ALU.mult, R=[f"qn{k}", "cosb"], W=[f"t1{k}"])
            tt("dve", t2[k][:, :], ps3[:, :], sinb[:, :], ALU.mult, R=[pr3, "sinb"], W=[f"t2{k}"])
            tt("pool", dest, t1[k][:, :], t2[k][:, :], ALU.add, R=[f"t1{k}", f"t2{k}"], W=[dname])

        for step in range(18):
            if step < 16:
                qkA(step)
            if 0 <= step - 1 < 16:
                qkB(step - 1)
            if 0 <= step - 2 < 16:
                qkC(step - 2)
        for g in range(2):
            w, wr = wload(wv_in[:, :, 4624 + g * 512: 4624 + (g + 1) * 512])
            for i in range(4):
                ps, pr = tm_tile(w, wr, i)
                cp("act", vaug[:, 4 * b + i, g * 4:(g + 1) * 4, 0:128], ps[:, :].rearrange("p (h d) -> p h d", d=128),
                   R=[pr], W=["vaug"])
        nk = 4 * b + 4
        iters = [(h, kt) for h in range(8) for kt in range(nk)]
        NI = len(iters)
        NPT = len(pT)

        def QK(n):
            h, kt = iters[n]
            qlo = max(kt - 4 * b, 0)
            nn = BLK - qlo * 128
            sps = []
            for j in range(2):
                js = slice(j * 64, (j + 1) * 64)
                sp, spr = psum6()
                mm(sp[:, 0:nn], kT[js, h, kt * 128:(kt + 1) * 128], qT[js, h, qlo * 128:BLK], True, True,
                   R=["kT", "qT"], W=[spr])
                sps.append((sp, spr))
            for j in range(2):
                sp, spr = sps[j]
                pi = (2 * n + j) % NPT
                act(pT[pi][:, 0:nn], sp[:, 0:nn], AF.Exp, R=[spr], W=[f"pT{pi}"], scale=0.125)
                if kt >= 4 * b:
                    mset("pool", pT[pi][64:128, 0:64], 0.0, W=[f"pT{pi}"])

        def PV(n):
            h, kt = iters[n]
            qlo = max(kt - 4 * b, 0)
            nn = BLK - qlo * 128
            hb = h % 2
            for j in range(2):
                pi = (2 * n + j) % NPT
                ob, obr = acc(j)
                mm(ob[:, qlo * 128:BLK], vaug[:, kt, h, 0:128], pT[pi][:, 0:nn], kt == 0, kt == nk - 1,
                   R=[f"pT{pi}", "vaug"], W=[obr], sgc=True)
                if j == 0:
                    db, dbr = acc(2)
                    mm(db[0:1, qlo * 128:BLK], onesb[:, 0:1], pT[pi][:, 0:nn], kt == 0, kt == nk - 1,
                       R=[f"pT{pi}", "cbf"], W=[dbr], sgc=True)
                    continue
                eng = "dve"
                ps_ = pS[hb][j]; psn = f"pS{hb}{j}"
                if kt == 0:
                    cp(eng, ps_[:, :], pT[pi][:, :], R=[f"pT{pi}"], W=[psn])
                else:
                    tt(eng, ps_[:, qlo * 128:BLK], ps_[:, qlo * 128:BLK], pT[pi][:, 0:nn], ALU.add, R=[psn, f"pT{pi}"], W=[psn])

        def fin1(h):
            for j in range(2):
                ob, obr = acc(j)
                cp("dve" if j == 0 else "act", oS[j][:, :], ob[:, :], R=[obr], W=[f"oS{j}"])
            db, dbr = acc(2)
            cp("act", pS[h % 2][0][0:1, :], db[0:1, :], R=[dbr], W=[f"pS{h % 2}0"])

        def fin2(h):
            hb = h % 2
            for j in range(2):
                pb_, pbr_ = psum()
                if j == 0:
                    mm(pb_[:, :], ones32[0:1, :], pS[hb][0][0:1, :], True, True, R=["c32", f"pS{hb}0"], W=[pbr_])
                else:
                    mm(pb_[:, :], ones32, pS[hb][j][:, :], True, True, R=["c32", f"pS{hb}{j}"], W=[pbr_])
                dtmp, dname = (rtb[1], "rtb1") if j == 0 else (rinv[1], "rinv1")
                act(dtmp[:, :], pb_[:, :], AF.Ln, R=[pbr_], W=[dname])
                act(t1[j][:, :], dtmp[:, :], AF.Exp, R=[dname], W=[f"t1{j}"], scale=-1.0)
            tt("dve", t2[0][:, :], oS[0][:, :], t1[0][:, :], ALU.mult, R=["oS0", "t10"], W=["t20"])
            stt(t2[1][:, :], oS[1][:, :], neglam[:, 0:1], t1[1][:, :], ALU.mult, ALU.mult, R=["oS1", "neglam", "t11"], W=["t21"])
            tt("dve", t2[1][:, :], t2[1][:, :], t2[0][:, :], ALU.add, R=["t20", "t21"], W=["t21"])
            act(sqb[0][:, :], t2[1][:, :], AF.Square, R=["t21"], W=["sqb0"])

        def fin3(h):
            pss_, psr_ = psum()
            mm(pss_[:, :], onesb, sqb[0][:, :], True, True, R=["cbf", "sqb0"], W=[psr_])
            act(rtb[0][:, :], pss_[:, :], AF.Ln, R=[psr_, "epsc"], W=["rtb0"], scale=1.0 / 128, bias=epsc[:, 0:1])
            act(rinv[0][:, :], rtb[0][:, :], AF.Exp, R=["rtb0"], W=["rinv0"], scale=-0.5)
            stt(yT[:, 8 + h, :], t2[1][:, :], subw[:, 0:1], rinv[0][:, :], ALU.mult, ALU.mult,
                R=["t21", "subw", "rinv0"], W=["yT"])

        deferred = []
        QK(0)
        if NI > 1:
            QK(1)
        for n in range(NI):
            if n + 2 < NI:
                QK(n + 2)
            PV(n)
            for item in [d_ for d_ in deferred if d_[0] <= n]:
                deferred.remove(item)
                item[1]()
            h, kt = iters[n]
            if kt == nk - 1:
                fin1(h)
                deferred.append((n + 1, (lambda hh=h: fin2(hh))))
                deferred.append((n + 3, (lambda hh=h: fin3(hh))))
        for item in sorted(deferred, key=lambda d_: d_[0]):
            item[1]()

        if debug is not None and debug[0] == "yT":
            for c in range(16):
                cp("act", st[:, 0:512], yT[:, c, :], R=["yT"], W=["st"])
                P.dma("sp", out_d[c * 128:(c + 1) * 128, 0:512], st[:, 0:512], R=["st"], W=["outd"], sem="dbg")
                P.op("act", lambda e: e.copy(stat[:, 60:61], stat[:, 60:61]), R=["st"], W=["stat"])
            if nxt is not None:
                step1(nxt[0], nxt[1], range(4), "ab")
            return
        for ch in range(2):
            cs = slice(ch * 512, (ch + 1) * 512)
            for kh in range(2):
                w, wr = wload(wv_out[:, kh * 8:(kh + 1) * 8, cs], key="wb_out")
                for i in range(4):
                    bk, bkr = acc(i)
                    for kc in range(8):
                        mm(bk[:, :], yT[:, kh * 8 + kc, i * 128:(i + 1) * 128], w[:, kc, :], kh == 0 and kc == 0,
                           kh == 1 and kc == 7, R=["yT", wr], W=[bkr])
            for i in range(4):
                bk, bkr = acc(i)
                xr = xt[i % 2]
                P.dma("sp", xr[:, 0:512], x_d[tok0 + i * 128: tok0 + (i + 1) * 128, cs], W=[f"xt{i % 2}"], sem=f"xt{i % 2}")
                tt("dve", x1[i][:, cs], bk[:, :], xr[:, 0:512], ALU.add, R=[bkr, f"xt{i % 2}"], W=[f"x1_{i}"])
        if debug is not None and debug[0] == "x1":
            for i in range(4):
                P.dma("sp", out_d[tok0 + i * 128: tok0 + (i + 1) * 128, :], x1[i][:, :], R=[f"x1_{i}"], W=["outd"],
                      sem=f"o{i}", join=True)
            if nxt is not None:
                step1(nxt[0], nxt[1], range(4), "ab")
            return
        for i in range(4):
            act(junk[:, :], x1[i][:, :], AF.Square, R=[f"x1_{i}"], W=["junk"])
            red(stat[:, 32 + i:33 + i], junk[:, :], R=["junk"], W=["stat"])
        rstd_from_ss(stat[:, 32:36], 4, 1.0 / D, "stat")
        for i in range(4):
            xb = xn[i % 2]
            ts("dve", xb[:, :], x1[i][:, :], stat[:, 32 + i:33 + i], None, ALU.mult, ALU.bypass,
               R=[f"x1_{i}", "stat"], W=[f"xn{i % 2}"])
            ps, pr = psum()
            pb = ps[:, :].bitcast(BF16)
            for d in range(8):
                tr(pb[:, d * 128:(d + 1) * 128], xb[:, d * 128:(d + 1) * 128], identb, R=[f"xn{i % 2}", "cbf"], W=[pr])
            tt("dve", hT[:, :, i * 128:(i + 1) * 128], pb.rearrange("p (d t) -> p d t", t=128),
               w2T[:, :].unsqueeze(2).broadcast_to([128, 8, 128]), ALU.mult, R=[pr, "w2T"], W=["hT"])
        if nxt is not None:
            step1(nxt[0], nxt[1], range(0, 2), "a")
        for fg in range(6):
            ncol = 512 if fg < 5 else 256
            wgt, wgr = wload(wv_g[:, :, fg * 512: fg * 512 + ncol], ncol=ncol, key="wb_g")
            for j in range(ncol // 128):
                psg, pgr = fm_chunk(wgt, wgr, j, hT, "hT")
                act(sg4[j][:, :], psg[:, :], AF.Silu, R=[pgr], W=[f"sg{j}"])
            wut, wur = wload(wv_u[:, :, fg * 512: fg * 512 + ncol], ncol=ncol, key="wb_u")
            for j in range(ncol // 128):
                psu, pur = fm_chunk(wut, wur, j, hT, "hT")
                tt("dve", aT[:, fg * 4 + j, :], sg4[j][:, :], psu[:, :], ALU.mult, R=[f"sg{j}", pur], W=["aT"])
        if nxt is not None:
            step1(nxt[0], nxt[1], range(0, 2), "b")
            step1(nxt[0], nxt[1], range(2, 4), "ab")
        for ch in range(2):
            cs = slice(ch * 512, (ch + 1) * 512)
            for kg in range(3):
                nk = 8 if kg < 2 else 6
                w, wr = wload(wv_d[:, kg * 8: kg * 8 + nk, cs], nk=nk, key="wb_d")
                for i in range(4):
                    bk, bkr = acc(i)
                    for kc in range(nk):
                        mm(bk[:, :], aT[:, kg * 8 + kc, i * 128:(i + 1) * 128], w[:, kc, :], kg == 0 and kc == 0,
                           kg == 2 and kc == nk - 1, R=["aT", wr], W=[bkr])
            for i in range(4):
                bk, bkr = acc(i)
                tt("dve", x1[i][:, cs], bk[:, :], x1[i][:, cs], ALU.add, R=[bkr, f"x1_{i}"], W=[f"x1_{i}"])
        for i in range(4):
            P.dma("sp", out_d[tok0 + i * 128: tok0 + (i + 1) * 128, :], x1[i][:, :], R=[f"x1_{i}"], W=["outd"],
                  sem=f"o{i}", join=True)

    order = [(s, b) for s in range(nseq) for b in range(nblk)]
    step1(order[0][0], order[0][1], range(4), "ab")
    for n_, (s, b) in enumerate(order):
        emit_block(s, b, order[n_ + 1] if n_ + 1 < len(order) else None)
    P.final_wait("sp", [f"o{i}" for i in range(4)] + ["dbg"])
    P.emit(es)
    es.close()
    return nc


_CONSTS = None


def kernel(**inputs):
    global _CONSTS
    n = 8
    x = np.ascontiguousarray(inputs["x"], dtype=np.float32)
    B = x.shape[0]
    per = B // n
    if _CONSTS is None:
        _CONSTS = host_consts()
    c32, cb, cosT, sinT = _CONSTS
    nc = build_program(nseq=per)
    shared = {
        "w_in": inputs["w_in"][0], "w_out": inputs["w_out"][0], "w_gate": inputs["w_gate"][0],
        "w_up": inputs["w_up"][0], "w_down": inputs["w_down"][0],
        "norm1_w": inputs["norm1_w"], "norm2_w": inputs["norm2_w"], "ssd_norm_w": inputs["ssd_norm_w"],
        "conv_w": inputs["conv_w"][0], "conv_b": inputs["conv_b"],
        "dt_bias": inputs["dt_bias"], "a_log": inputs["a_log"], "d_skip": inputs["d_skip"],
        "q_norm_w": inputs["q_norm_w"], "k_norm_w": inputs["k_norm_w"],
        "lambda_q1": inputs["lambda_q1"], "lambda_k1": inputs["lambda_k1"],
        "lambda_q2": inputs["lambda_q2"], "lambda_k2": inputs["lambda_k2"],
        "subln_w": inputs["subln_w"],
        "c32": c32, "cbf": cb, "cosT": cosT, "sinT": sinT,
    }
    shared = {k: np.ascontiguousarray(v, dtype=np.float32) for k, v in shared.items()}
    in_maps = []
    for c in range(n):
        m = dict(shared)
        m["x"] = x[c * per:(c + 1) * per].reshape(per * SEQ, D)
        in_maps.append(m)
    res = run_bass_kernel_spmd(nc, in_maps, core_ids=list(range(n)))
    out = np.concatenate([np.asarray(r["out"]).reshape(per, SEQ, D) for r in res.results], axis=0)
    return out.astype(np.float32)
```

```python
import math
from contextlib import ExitStack
import numpy as np
import concourse.bass as bass
import concourse.mybir as mybir
from concourse.bass_utils import run_bass_kernel_spmd

F32 = mybir.dt.float32
BF16 = mybir.dt.bfloat16
AF = mybir.ActivationFunctionType
ALU = mybir.AluOpType
AX = mybir.AxisListType

D = 1024
SEQ = 2048
NSEQ_CORE = 4
BLK = 512
IN_DIM = 5648
DFF = 2816
EPS = 1e-6
LAMBDA_INIT = 0.8 - 0.6 * math.exp(0.0)
NEG = -30000.0

ENG_ATTR = {"pe": "tensor", "act": "scalar", "dve": "vector", "pool": "gpsimd", "sp": "sync"}


class Prog:
    def __init__(self, nc):
        self.nc = nc
        self.ops = {e: [] for e in ENG_ATTR}
        self.res = {}
        self.seen = {e: {} for e in ENG_ATTR}
        self.marked = {e: set() for e in ENG_ATTR}
        self.semcnt = {}
        self.tags = {}
        self.tag_last = {}

    def settag(self, name, tag):
        self.tags[name] = tag

    def _deps(self, eng, R, W, join):
        deps = set()
        raw = set()
        for r in R:
            st = self.res.setdefault(r, {"w": [], "r": []})
            for t in st["w"]:
                deps.add(t); raw.add(t)
        for w in W:
            st = self.res.setdefault(w, {"w": [], "r": []})
            for t in st["w"]:
                if join and t[0] == "s":
                    continue
                deps.add(t)
            for t in st["r"]:
                deps.add(t)
        touched = set(self.tags[x] for x in list(R) + list(W) if x in self.tags)
        for tg in touched:
            for other, last in self.tag_last.items():
                if other != tg:
                    for k, v in last.items():
                        deps.add((k[0], k[1], v)); raw.add((k[0], k[1], v))
        out = []
        for t in deps:
            kind, key, val = t
            if kind == "e" and key == eng and t not in raw:
                continue
            sk = (kind, key)
            if self.seen[eng].get(sk, 0) >= val:
                continue
            out.append(t)
        best = {}
        for kind, key, val in out:
            best[(kind, key)] = max(best.get((kind, key), 0), val)
        waits = []
        for (kind, key), val in best.items():
            self.seen[eng][(kind, key)] = val
            if kind == "e":
                self.marked[key].add(val)
            waits.append((kind, key, val))
        return waits, touched

    def _commit(self, tok, R, W, join, touched):
        for r in R:
            self.res[r]["r"].append(tok)
        for w in W:
            st = self.res[w]
            if join and st["w"] and all(t[0] == "s" for t in st["w"]) and not st["r"]:
                st["w"].append(tok)
            else:
                st["w"] = [tok]
                st["r"] = []
        for tg in touched:
            d = self.tag_last.setdefault(tg, {})
            k = (tok[0], tok[1])
            d[k] = max(d.get(k, 0), tok[2])

    def op(self, eng, fn, R=(), W=()):
        waits, touched = self._deps(eng, R, W, False)
        idx = len(self.ops[eng]) + 1
        self.ops[eng].append({"fn": fn, "waits": waits, "idx": idx, "dma": None})
        self._commit(("e", eng, idx), R, W, False, touched)

    def dma(self, q, out, in_, R=(), W=(), sem="d", join=False, **kw):
        waits, touched = self._deps(q, R, W, join)
        idx = len(self.ops[q]) + 1
        self.semcnt[sem] = self.semcnt.get(sem, 0) + 16
        fn = (lambda e, o=out, i=in_, k=kw: e.dma_start(out=o, in_=i, allow_slow_non_contiguous=True, **k))
        self.ops[q].append({"fn": fn, "waits": waits, "idx": idx, "dma": sem})
        self._commit(("s", sem, self.semcnt[sem]), R, W, join, touched)

    def retoken(self, names, sem):
        for n_ in names:
            self.res[n_]["w"] = [("s", sem, self.semcnt[sem])]

    def final_wait(self, q, sems):
        waits = [("s", s, self.semcnt[s]) for s in sems if s in self.semcnt]
        idx = len(self.ops[q]) + 1
        self.ops[q].append({"fn": None, "waits": waits, "idx": idx, "dma": None})

    def emit(self, es):
        nc = self.nc
        semh = {}
        for e in ENG_ATTR:
            semh[("e", e)] = es.enter_context(nc.semaphore("sem_" + e))
        for s in self.semcnt:
            semh[("s", s)] = es.enter_context(nc.semaphore("dsem_" + s))
        cnt = {}
        for e in ENG_ATTR:
            m = sorted(self.marked[e])
            cnt[e] = {idx: i + 1 for i, idx in enumerate(m)}
        block = es.enter_context(nc.Block())

        def replay(ename, eng):
            for o in self.ops[ename]:
                for kind, key, val in o["waits"]:
                    if kind == "e":
                        eng.wait_ge(semh[("e", key)], cnt[key][val])
                    else:
                        eng.wait_ge(semh[("s", key)], val)
                if o["fn"] is None:
                    continue
                ins = o["fn"](eng)
                if o["dma"] is not None:
                    ins.then_inc(semh[("s", o["dma"])], 16)
                elif o["idx"] in cnt[ename]:
                    ins.then_inc(semh[("e", ename)], 1)

        @block.tensor
        def _(e):
            replay("pe", e)

        @block.scalar
        def _(e):
            replay("act", e)

        @block.vector
        def _(e):
            replay("dve", e)

        @block.gpsimd
        def _(e):
            replay("pool", e)

        @block.sync
        def _(e):
            replay("sp", e)


def host_consts():
    c32 = np.zeros((128, 1024), np.float32)
    t = np.arange(128)
    c32[:, 0:128] = np.eye(128)
    c32[:, 128:256] = (t[:, None] <= t[None, :])
    c32[:, 256:384] = -1.0 * (t[:, None] <= t[None, :])
    c32[:, 384:512] = 1.0
    mk = np.where(t[None, :] < t[:, None], NEG, 0.0)
    c32[:, 512:1024] = np.tile(mk, (1, 4))
    cb = np.zeros((128, 1024), np.float32)
    cb[:, 384:512] = 1.0
    cb[:, 512:1024] = c32[:, 512:1024]
    cb[:, 0:128] = np.eye(128)
    cb[:, 128:256] = (t[:, None] // 64 == t[None, :] // 64)
    rt = np.zeros((128, 128), np.float32)
    for m in range(128):
        r = m % 64
        if r < 8:
            rt[m + 8, m] = -1.0
        elif r < 16:
            rt[m - 8, m] = 1.0
    cb[:, 256:384] = rt
    pos = np.arange(SEQ, dtype=np.float32)
    inv_freq = (500000.0 ** (-np.arange(0, 16, 2, dtype=np.float32) / 16)).astype(np.float32)
    ang = (pos[:, None] * inv_freq[None, :]).astype(np.float32)
    cosT = np.ones((128, SEQ), np.float32)
    sinT = np.zeros((128, SEQ), np.float32)
    for p in range(128):
        r = p % 64
        if r < 16:
            cosT[p] = np.cos(ang[:, r % 8])
            sinT[p] = np.sin(ang[:, r % 8])
    return c32, cb, cosT, sinT


def build_program(nseq=NSEQ_CORE, nblk=SEQ // BLK, debug=None):
    nc = bass.Bass("TRN2", target_bir_lowering=False)
    es = ExitStack()
    ntok = nseq * SEQ

    def din(name, shape):
        return nc.dram_tensor(name, list(shape), F32, kind="ExternalInput").ap()

    x_d = din("x", [ntok, D])
    out_d = nc.dram_tensor("out", [ntok, D], F32, kind="ExternalOutput").ap()
    w_in_d = din("w_in", [D, IN_DIM])
    w_out_d = din("w_out", [2 * D, D])
    w_g_d = din("w_gate", [D, DFF])
    w_u_d = din("w_up", [D, DFF])
    w_d_d = din("w_down", [DFF, D])
    norm1_d = din("norm1_w", [1, D]); norm2_d = din("norm2_w", [1, D]); ssdn_d = din("ssd_norm_w", [1, D])
    convw_d = din("conv_w", [4, 1536]); convb_d = din("conv_b", [1, 1536])
    dtb_d = din("dt_bias", [1, 16]); alog_d = din("a_log", [1, 16]); dsk_d = din("d_skip", [1, 16])
    qnw_d = din("q_norm_w", [1, 64]); knw_d = din("k_norm_w", [1, 64])
    lq1_d = din("lambda_q1", [1, 64]); lk1_d = din("lambda_k1", [1, 64])
    lq2_d = din("lambda_q2", [1, 64]); lk2_d = din("lambda_k2", [1, 64])
    subln_d = din("subln_w", [1, 128])
    c32_d = din("c32", [128, 1024]); cbf_d = din("cbf", [128, 1024])
    cos_d = din("cosT", [128, SEQ]); sin_d = din("sinT", [128, SEQ])

    def dscr(name, shape):
        return nc.dram_tensor(name, list(shape), BF16, kind="Internal").ap()

    wb_in = dscr("wb_in", [D, IN_DIM]); wb_out = dscr("wb_out", [2 * D, D])
    wb_g = dscr("wb_g", [D, DFF]); wb_u = dscr("wb_u", [D, DFF]); wb_d = dscr("wb_d", [DFF, D])
    dbg_d = None
    if debug is not None and debug[0] == "hT":
        dbg_d = nc.dram_tensor("dbg", list(debug[1]), F32, kind="ExternalOutput").ap()

    P = Prog(nc)

    def sb(name, shape, dt=F32):
        return es.enter_context(nc.sbuf_tensor("s_" + name, list(shape), dt))

    c32 = sb("c32", [128, 1024]); cbf = sb("cbf", [128, 1024], BF16)
    ident32 = c32[:, 0:128]; tri32 = c32[:, 128:256]; negtri32 = c32[:, 256:384]; ones32 = c32[:, 384:512]
    mask4 = c32[:, 512:1024]
    identb = cbf[:, 0:128]; blockones = cbf[:, 128:256]; rotT = cbf[:, 256:384]; onesb = cbf[:, 384:512]; mask4b = cbf[:, 512:1024]
    diagw = sb("diagw", [128, 12, 4, 128], BF16)
    w1T = sb("w1T", [128, 8]); w2T = sb("w2T", [128, 8]); wssdT = sb("wssdT", [128, 8])
    convwT = sb("convwT", [128, 4, 12]); convbT = sb("convbT", [128, 12])
    wq2 = sb("wq2", [128, 1]); wk2 = sb("wk2", [128, 1])
    sublnw = sb("sublnw", [128, 128]); subw = sb("subw", [128, 1])
    dtb4 = sb("dtb4", [128, 64]); a4 = sb("a4", [128, 64]); dsk = sb("dsk", [128, 16])
    lam4 = sb("lam4", [128, 4, 64]); lamt = sb("lamt", [128, 8]); neglam = sb("neglam", [128, 1])
    cosb = sb("cosb", [128, BLK]); sinb = sb("sinb", [128, BLK])
    wdt = sb("wdt", [128, 8, 16], BF16)
    kT = sb("kT", [128, 8, SEQ], BF16)
    vaug = sb("vaug", [128, 16, 8, 130], BF16)
    st = sb("st", [128, 1024]); stbf = sb("stbf", [128, 1024], BF16)
    xt = [sb(f"xt{i}", [128, 1024]) for i in range(2)]
    hT = sb("hT", [128, 8, BLK], BF16)
    yT = sb("yT", [128, 16, BLK], BF16)
    NW = 2
    wg = [sb(f"wg{i}", [128, 8, 512], BF16) for i in range(NW)]
    junk = sb("junk", [128, 1024], BF16)
    xn = [sb(f"xn{i}", [128, 1024], BF16) for i in range(2)]
    stat = sb("stat", [128, 64])
    halo = sb("halo", [128, 12, 4], BF16)
    epsc = sb("epsc", [128, 2])
    REG = 55 * 1024 + 512
    region = sb("region", [128, REG // 4])
    roff = {"S": 0, "T": 0, "F": 0}

    def rg(phase, name, shape, dt=F32):
        n = int(np.prod(shape[1:]))
        nb = n * (4 if dt == F32 else 2)
        nb = (nb + 31) // 32 * 32
        o = roff[phase]
        roff[phase] = o + nb
        assert roff[phase] <= REG, (phase, name, roff[phase])
        ap = region[:, o // 4:(o + nb) // 4]
        if dt != F32:
            ap = ap.bitcast(dt)
        ap = ap[:, 0:n]
        if len(shape) == 3:
            ap = ap.rearrange("p (a b) -> p a b", b=shape[2])
        elif len(shape) == 4:
            ap = ap.rearrange("p (a b c) -> p a b c", b=shape[2], c=shape[3])
        P.settag(name, phase)
        return ap

    zs = rg("S", "zs", [128, 4, 1024], BF16)
    xbcT = rg("S", "xbcT", [128, 12, 516], BF16)
    xsT = rg("S", "xsT", [128, 8, BLK], BF16)
    bcT = rg("S", "bcT", [128, 4, BLK], BF16)
    dtt = rg("S", "dtt", [128, 64]); adt = rg("S", "adt", [128, 64])
    xdt = rg("S", "xdt", [128, 1024], BF16); xsD = rg("S", "xsD", [128, 1024], BF16)
    R1 = [rg("S", f"R1{i}", [128, 4, 128]) for i in range(2)]
    dec = [rg("S", f"dec{i}", [128, 4, 128], BF16) for i in range(2)]
    MT = [rg("S", f"MT{i}", [128, 4, 128], BF16) for i in range(2)]
    cbT = rg("S", "cbT", [128, 256], BF16)
    ybuf = rg("S", "ybuf", [128, 1024])
    sp_t = ybuf[:, 0:256].rearrange("p (a b) -> p a b", b=64)
    gn = rg("S", "gn", [128, 1024], BF16)
    xdtE = rg("S", "xdtE", [128, 1024], BF16)
    Btok = rg("S", "Btok", [128, 256], BF16)
    acs = rg("S", "acs", [128, 96])
    qT = rg("T", "qT", [128, 8, BLK], BF16)
    sqb = [rg("T", f"sqb{i}", [128, BLK], BF16) for i in range(2)]
    rtb = [rg("T", f"rtb{i}", [128, BLK]) for i in range(2)]
    rinv = [rg("T", f"rinv{i}", [128, BLK]) for i in range(2)]
    qn = [rg("T", f"qn{i}", [128, BLK]) for i in range(2)]
    qnb = [rg("T", f"qnb{i}", [128, BLK], BF16) for i in range(2)]
    t1 = [rg("T", f"t1{i}", [128, BLK]) for i in range(2)]
    t2 = [rg("T", f"t2{i}", [128, BLK]) for i in range(2)]
    pT = [rg("T", f"pT{i}", [128, BLK], BF16) for i in range(6)]
    oS = [rg("T", f"oS{i}", [128, BLK]) for i in range(2)]
    pS = [[rg("T", f"pS{a}{c}", [128, BLK]) for c in range(2)] for a in range(2)]
    aT = rg("F", "aT", [128, 22, BLK], BF16)
    sg4 = [rg("F", f"sg{i}", [128, BLK]) for i in range(4)]
    x1 = [rg("F", f"x1_{i}", [128, 1024]) for i in range(4)]

    banks = [es.enter_context(nc.psum_tensor(f"ps{i}", [128, 512], F32)) for i in range(8)]
    rr = [0]

    def psum():
        i = 4 + rr[0] % 4
        rr[0] += 1
        return banks[i], f"ps{i}"

    def acc(i):
        return banks[i], f"ps{i}"

    rr6 = [0]

    def psum6():
        i = 3 + rr6[0] % 5
        rr6[0] += 1
        return banks[i], f"ps{i}"

    rr8 = [0]

    def psum8():
        i = rr8[0] % 8
        rr8[0] += 1
        return banks[i], f"ps{i}"

    def mm(out, lhsT, rhs, start, stop, R, W, sgc=False):
        P.op("pe", lambda e: e.matmul(out, lhsT, rhs, start=start, stop=stop, skip_group_check=sgc), R=R, W=W)

    def tr(out, in_, ident, R, W):
        P.op("pe", lambda e: e.transpose(out, in_, ident), R=R, W=W)

    def act(out, in_, func, R, W, scale=1.0, bias=None):
        if bias is None:
            P.op("act", lambda e: e.activation(out, in_, func, scale=scale), R=R, W=W)
        else:
            P.op("act", lambda e: e.activation(out, in_, func, bias=bias, scale=scale), R=R, W=W)

    def tt(eng, out, in0, in1, op, R, W):
        P.op(eng, lambda e: e.tensor_tensor(out, in0, in1, op), R=R, W=W)

    def ts(eng, out, in0, s1, s2, op0, op1, R, W):
        P.op(eng, lambda e: e.tensor_scalar(out, in0, s1, s2, op0, op1), R=R, W=W)

    def stt(out, in0, scalar, in1, op0, op1, R, W):
        P.op("dve", lambda e: e.scalar_tensor_tensor(out, in0, scalar, in1, op0, op1), R=R, W=W)

    def red(out, in_, R, W):
        P.op("dve", lambda e: e.tensor_reduce(out, in_, AX.X, ALU.add), R=R, W=W)

    def recip(out, in_, R, W):
        P.op("dve", lambda e: e.reciprocal(out, in_), R=R, W=W)

    def cp(eng, out, in_, R, W):
        if eng == "act":
            P.op("act", lambda e: e.copy(out, in_), R=R, W=W)
        else:
            P.op(eng, lambda e: e.tensor_copy(out, in_), R=R, W=W)

    def mset(eng, ap, val, W):
        P.op(eng, lambda e: e.memset(ap, val), W=W)

    def rstd_from_ss(ss_ap, n, inv_n, name):
        act(ss_ap, ss_ap, AF.Ln, R=[name, "epsc"], W=[name], scale=inv_n, bias=epsc[:, 0:1])
        act(ss_ap, ss_ap, AF.Exp, R=[name], W=[name], scale=-0.5)

    ncd = nc.allow_non_contiguous_dma(reason="tiny parameter layout loads")
    ncd.__enter__()
    for (src, dst, rows, key) in ((w_in_d, wb_in, D, "wb_in"), (w_out_d, wb_out, 2 * D, "wb_out"),
                                  (w_g_d, wb_g, D, "wb_g"), (w_u_d, wb_u, D, "wb_u"),
                                  (w_d_d, wb_d, DFF, "wb_d")):
        for r0 in range(0, rows, 128):
            P.dma("pool", dst[r0:r0 + 128, :], src[r0:r0 + 128, :], W=[key], sem=key, join=True)
    P.dma("sp", c32[:, :], c32_d[:, :], W=["c32"], sem="cst")
    P.dma("pool", cbf[:, :], cbf_d[:, :], W=["cbf"], sem="cstb")
    P.dma("sp", w1T[:, :], norm1_d[0, :].rearrange("(d p) -> p d", p=128), W=["w1T"], sem="cst")
    P.dma("sp", w2T[:, :], norm2_d[0, :].rearrange("(d p) -> p d", p=128), W=["w2T"], sem="cst")
    P.dma("sp", wssdT[:, :], ssdn_d[0, :].rearrange("(d p) -> p d", p=128), W=["wssdT"], sem="cst")
    for k in range(4):
        P.dma("sp", convwT[:, k, :], convw_d[k, :].rearrange("(c p) -> p c", p=128), W=["convwT"], sem="cst", join=True)
    P.dma("sp", convbT[:, :], convb_d[0, :].rearrange("(c p) -> p c", p=128), W=["convbT"], sem="cst")
    for hh in range(2):
        P.dma("sp", wq2[hh * 64:(hh + 1) * 64, :], qnw_d[0, :].rearrange("(p o) -> p o", o=1), W=["wq2"], sem="cst", join=True)
        P.dma("sp", wk2[hh * 64:(hh + 1) * 64, :], knw_d[0, :].rearrange("(p o) -> p o", o=1), W=["wk2"], sem="cst", join=True)
    P.dma("sp", sublnw[:, :], subln_d[0:1, :].broadcast_to([128, 128]), W=["sublnw"], sem="cst")
    P.dma("sp", subw[:, :], subln_d[0, :].rearrange("(p o) -> p o", o=1), W=["subw"], sem="cst")
    for i in range(4):
        P.dma("sp", dtb4[:, i * 16:(i + 1) * 16], dtb_d[0:1, :].broadcast_to([128, 16]), W=["dtb4"], sem="cst", join=True)
        P.dma("sp", a4[:, i * 16:(i + 1) * 16], alog_d[0:1, :].broadcast_to([128, 16]), W=["a4"], sem="cst", join=True)
    P.dma("sp", dsk[:, :], dsk_d[0:1, :].broadcast_to([128, 16]), W=["dsk"], sem="cst")
    for i, ld in enumerate((lq1_d, lk1_d, lq2_d, lk2_d)):
        P.dma("sp", lam4[:, i, :], ld[0:1, :].broadcast_to([128, 64]), W=["lam4"], sem="cst", join=True)
    P.dma("sp", wdt[:, :, :], wb_in.rearrange("(d p) c -> p d c", p=128)[:, :, 2560:2576], R=["wb_in"], W=["wdt"], sem="cst")
    ncd.__exit__(None, None, None)
    P.retoken(["c32", "w1T", "w2T", "wssdT", "convwT", "convbT", "wq2", "wk2", "sublnw", "subw", "dtb4", "a4", "dsk", "lam4", "wdt"], "cst")
    act(a4[:, :], a4[:, :], AF.Exp, R=["a4"], W=["a4"])
    ts("dve", a4[:, :], a4[:, :], -1.0, None, ALU.mult, ALU.bypass, R=["a4"], W=["a4"])
    tt("dve", lam4[:, 0, :], lam4[:, 0, :], lam4[:, 1, :], ALU.mult, R=["lam4"], W=["lam4"])
    tt("dve", lam4[:, 2, :], lam4[:, 2, :], lam4[:, 3, :], ALU.mult, R=["lam4"], W=["lam4"])
    red(lamt[:, 0:1], lam4[:, 0, :], R=["lam4"], W=["lamt"])
    red(lamt[:, 1:2], lam4[:, 2, :], R=["lam4"], W=["lamt"])
    act(lamt[:, 2:4], lamt[:, 0:2], AF.Exp, R=["lamt"], W=["lamt"])
    tt("dve", lamt[:, 4:5], lamt[:, 3:4], lamt[:, 2:3], ALU.subtract, R=["lamt"], W=["lamt"])
    ts("dve", neglam[:, :], lamt[:, 4:5], -LAMBDA_INIT, None, ALU.add, ALU.bypass, R=["lamt"], W=["neglam"])
    ts("dve", sublnw[:, :], sublnw[:, :], 1.0 - LAMBDA_INIT, None, ALU.mult, ALU.bypass, R=["sublnw"], W=["sublnw"])
    ts("dve", subw[:, :], subw[:, :], 1.0 - LAMBDA_INIT, None, ALU.mult, ALU.bypass, R=["subw"], W=["subw"])
    for c in range(12):
        for k in range(4):
            ts("dve", diagw[:, c, k, :], ident32, convwT[:, k, c:c + 1], None, ALU.mult, ALU.bypass,
               R=["c32", "convwT"], W=["diagw"])
    mset("pool", vaug[:, :, :, 128:130], 1.0, W=["vaug"])
    mset("pool", epsc[:, :], EPS, W=["epsc"])

    wctr = [0]

    def wload(src_ap, nk=8, ncol=512, key="wb_in"):
        i = wctr[0] % NW
        wctr[0] += 1
        P.dma("sp", wg[i][:, 0:nk, 0:ncol], src_ap, R=[key], W=[f"wg{i}"], sem=f"wg{i}")
        return wg[i], f"wg{i}"

    wv_in = wb_in.rearrange("(d p) c -> p d c", p=128)
    wv_out = wb_out.rearrange("(k p) c -> p k c", p=128)
    wv_g = wb_g.rearrange("(d p) c -> p d c", p=128)
    wv_u = wb_u.rearrange("(d p) c -> p d c", p=128)
    wv_d = wb_d.rearrange("(k p) c -> p k c", p=128)

    def fm_chunk(w, wr, j, src, srcname):
        ps, pr = psum()
        for d in range(8):
            mm(ps[:, :], w[:, d, j * 128:(j + 1) * 128], src[:, d, :], d == 0, d == 7, R=[wr, srcname], W=[pr])
        return ps, pr

    def tm_tile(w, wr, i, ncol=512):
        ps, pr = psum()
        for d in range(8):
            mm(ps[:, 0:ncol], hT[:, d, i * 128:(i + 1) * 128], w[:, d, 0:ncol], d == 0, d == 7, R=[wr, "hT"], W=[pr])
        return ps, pr

    def norm_to_hT(src_tiles, src_names, wT, wTname):
        for i in range(4):
            act(junk[:, :], src_tiles[i], AF.Square, R=[src_names[i]], W=["junk"])
            red(stat[:, i:i + 1], junk[:, :], R=["junk"], W=["stat"])
        rstd_from_ss(stat[:, 0:4], 4, 1.0 / D, "stat")
        for i in range(4):
            xb = xn[i % 2]
            ts("dve", xb[:, :], src_tiles[i], stat[:, i:i + 1], None, ALU.mult, ALU.bypass,
               R=[src_names[i], "stat"], W=[f"xn{i % 2}"])
            ps, pr = psum()
            pb = ps[:, :].bitcast(BF16)
            for d in range(8):
                tr(pb[:, d * 128:(d + 1) * 128], xb[:, d * 128:(d + 1) * 128], identb, R=[f"xn{i % 2}", "cbf"], W=[pr])
            tt("dve", hT[:, :, i * 128:(i + 1) * 128], pb.rearrange("p (d t) -> p d t", t=128),
               wT[:, :].unsqueeze(2).broadcast_to([128, 8, 128]), ALU.mult, R=[pr, wTname], W=["hT"])

    def dump(ap, resname, rows=None):
        P.dma("sp", dbg_d if rows is None else dbg_d[rows], ap, R=[resname], W=["dbgout"], sem="dbg", join=True)

    def step1(s, b, tiles, part):
        tok0 = s * SEQ + b * BLK
        for i in tiles:
            xb = xn[i % 2]
            if "a" in part:
                P.dma("sp", xt[i % 2][:, :], x_d[tok0 + i * 128: tok0 + (i + 1) * 128, :], W=[f"xt{i % 2}"], sem=f"xt{i % 2}")
                act(junk[:, :], xt[i % 2][:, :], AF.Square, R=[f"xt{i % 2}"], W=["junk"])
                red(stat[:, 8 + i:9 + i], junk[:, :], R=["junk"], W=["stat"])
                rstd_from_ss(stat[:, 8 + i:9 + i], 1, 1.0 / D, "stat")
                ts("dve", xb[:, :], xt[i % 2][:, :], stat[:, 8 + i:9 + i], None, ALU.mult, ALU.bypass,
                   R=[f"xt{i % 2}", "stat"], W=[f"xn{i % 2}"])
            if "b" in part:
                ps, pr = psum()
                pb = ps[:, :].bitcast(BF16)
                for d in range(8):
                    tr(pb[:, d * 128:(d + 1) * 128], xb[:, d * 128:(d + 1) * 128], identb, R=[f"xn{i % 2}", "cbf"], W=[pr])
                tt("dve", hT[:, :, i * 128:(i + 1) * 128], pb.rearrange("p (d t) -> p d t", t=128),
                   w1T[:, :].unsqueeze(2).broadcast_to([128, 8, 128]), ALU.mult, R=[pr, "w1T"], W=["hT"])

    def emit_block(s, b, nxt):
        tok0 = s * SEQ + b * BLK

        for g in range(2):
            w, wr = wload(wv_in[:, :, g * 512:(g + 1) * 512])
            for i in range(4):
                ps, pr = tm_tile(w, wr, i)
                act(zs[:, i, g * 512:(g + 1) * 512], ps[:, :], AF.Silu, R=[pr], W=["zs"])
        if b == 0:
            mset("pool", xbcT[:, :, 0:3], 0.0, W=["xbcT"])
        else:
            cp("dve", xbcT[:, :, 0:3], halo[:, :, 0:3], R=["halo"], W=["xbcT"])
        for g in range(3):
            w, wr = wload(wv_in[:, :, 1024 + g * 512: 1024 + (g + 1) * 512])
            for j in range(4):
                ps, pr = fm_chunk(w, wr, j, hT, "hT")
                cp("act", xbcT[:, g * 4 + j, 3:515], ps[:, :], R=[pr], W=["xbcT"])
        cp("dve", halo[:, :, 0:3], xbcT[:, :, 512:515], R=["xbcT"], W=["halo"])
        psd, pdr = psum()
        for i in range(4):
            for d in range(8):
                mm(psd[:, i * 16:(i + 1) * 16], hT[:, d, i * 128:(i + 1) * 128], wdt[:, d, :], d == 0, d == 7,
                   R=["hT", "wdt"], W=[pdr])
        tt("dve", sp_t[:, 0, :], psd[:, 0:64], dtb4[:, :], ALU.add, R=[pdr, "dtb4"], W=["ybuf"])
        act(sp_t[:, 1, :], sp_t[:, 0, :], AF.Abs, R=["ybuf"], W=["ybuf"])
        act(sp_t[:, 2, :], sp_t[:, 1, :], AF.Exp, R=["ybuf"], W=["ybuf"], scale=-1.0)
        ts("dve", sp_t[:, 2, :], sp_t[:, 2, :], 1.0, None, ALU.add, ALU.bypass, R=["ybuf"], W=["ybuf"])
        act(sp_t[:, 3, :], sp_t[:, 2, :], AF.Ln, R=["ybuf"], W=["ybuf"])
        stt(dtt[:, :], sp_t[:, 0, :], 0.0, sp_t[:, 3, :], ALU.max, ALU.add, R=["ybuf"], W=["dtt"])
        tt("dve", adt[:, :], dtt[:, :], a4[:, :], ALU.mult, R=["dtt", "a4"], W=["adt"])
        for c in range(12):
            ps, pr = psum()
            for k in range(4):
                mm(ps[:, :], diagw[:, c, k, :], xbcT[:, c, k:k + 512], k == 0, k == 3, R=["diagw", "xbcT"], W=[pr])
            if c < 8:
                act(xsT[:, c, :], ps[:, :], AF.Silu, R=[pr, "convbT"], W=["xsT"], bias=convbT[:, c:c + 1])
            else:
                act(bcT[:, c - 8, :], ps[:, :], AF.Silu, R=[pr, "convbT"], W=["bcT"], bias=convbT[:, c:c + 1])
        if b == 0:
            mset("pool", st[:, :], 0.0, W=["st"])
            mset("pool", stbf[:, :], 0.0, W=["stbf"])
        def tsl_(i):
            return slice(i * 128, (i + 1) * 128)

        def prepA(i):
            tsl = tsl_(i)
            hsl = slice(i * 16, (i + 1) * 16)
            psx, pxr = psum()
            pxb = psx[:, :].bitcast(BF16)
            for cc in range(8):
                tr(pxb[:, cc * 128:(cc + 1) * 128], xsT[:, cc, tsl], identb, R=["xsT", "cbf"], W=[pxr])
            px3 = pxb.rearrange("p (h d) -> p h d", d=64)
            tt("dve", xdt.rearrange("p (h d) -> p h d", d=64), px3,
               dtt[:, hsl].unsqueeze(2).broadcast_to([128, 16, 64]), ALU.mult, R=[pxr, "dtt"], W=["xdt"])
            tt("dve", xsD.rearrange("p (h d) -> p h d", d=64), px3,
               dsk[:, :].unsqueeze(2).broadcast_to([128, 16, 64]), ALU.mult, R=[pxr, "dsk"], W=["xsD"])
            psa, par = psum()
            mm(psa[:, 0:16], tri32, adt[:, hsl], True, True, R=["c32", "adt"], W=[par])
            mm(psa[:, 16:32], ones32, adt[:, hsl], True, True, R=["c32", "adt"], W=[par])
            act(acs[:, 0:16], psa[:, 0:16], AF.Exp, R=[par], W=["acs"])
            cp("act", acs[:, 16:32], psa[:, 16:32], R=[par], W=["acs"])
            tt("dve", acs[:, 32:48], acs[:, 16:32], psa[:, 0:16], ALU.subtract, R=["acs", par], W=["acs"])
            act(acs[:, 48:64], acs[:, 32:48], AF.Exp, R=["acs"], W=["acs"])
            act(acs[:, 64:80], acs[:, 16:32], AF.Exp, R=["acs"], W=["acs"])
            psc, pcr = psum()
            for g in range(2):
                mm(psc[:, g * 128:(g + 1) * 128], bcT[:, g, tsl], bcT[:, 2 + g, tsl], True, True, R=["bcT"], W=[pcr])
            cp("act", cbT[:, :], psc[:, 0:256], R=[pcr], W=["cbT"])
            tt("pool", xdtE.rearrange("p (h d) -> p h d", d=64), xdt.rearrange("p (h d) -> p h d", d=64),
               acs[:, 48:64].unsqueeze(2).broadcast_to([128, 16, 64]), ALU.mult, R=["xdt", "acs"], W=["xdtE"])
            psb, pbr2 = psum()
            pbb = psb[:, :].bitcast(BF16)
            for g in range(2):
                tr(pbb[:, g * 128:(g + 1) * 128], bcT[:, g, tsl], identb, R=["bcT", "cbf"], W=[pbr2])
            cp("act", Btok[:, :], pbb[:, 0:256], R=[pbr2], W=["Btok"])

        seg_ps = {}

        def segR1(i, hg):
            h0 = i * 16 + hg * 4
            k2 = hg % 2
            tt("dve", R1[k2][:, :, :], tri32.unsqueeze(1).broadcast_to([128, 4, 128]),
               adt[:, h0:h0 + 4].unsqueeze(2).broadcast_to([128, 4, 128]), ALU.mult, R=["c32", "adt"], W=[f"R1{k2}"])

        def segMM(i, hg):
            h0 = i * 16 + hg * 4
            k2 = hg % 2
            pss, psr = psum()
            mm(pss[:, :], ones32, R1[k2].rearrange("p a b -> p (a b)"), True, False, R=["c32", f"R1{k2}"], W=[psr])
            mm(pss[:, :].rearrange("p (a b) -> p a b", b=128), negtri32, adt[:, h0:h0 + 4].unsqueeze(2).broadcast_to([128, 4, 128]),
               False, False, R=["c32", "adt"], W=[psr])
            mm(pss[:, :], identb, mask4b, False, True, R=["cbf"], W=[psr])
            seg_ps[hg] = (pss, psr)

        def segFin(i, hg):
            k2 = hg % 2
            pss, psr = seg_ps[hg]
            act(dec[k2].rearrange("p a b -> p (a b)"), pss[:, :], AF.Exp, R=[psr], W=[f"dec{k2}"])
            g = hg // 2
            m = MT[k2]
            tt("dve", m[:, :, :], dec[k2][:, :, :], cbT[:, g * 128:(g + 1) * 128].unsqueeze(1).broadcast_to([128, 4, 128]),
               ALU.mult, R=[f"dec{k2}", "cbT"], W=[f"MT{k2}"])
            for hh in range(4):
                h = hg * 4 + hh
                pb_, pbr = acc(h // 8)
                mm(pb_[:, (h % 8) * 64:(h % 8 + 1) * 64], m[:, hh, :], xdt[:, h * 64:(h + 1) * 64], True, True,
                   R=[f"MT{k2}", "xdt"], W=[pbr])

        def prepB(i, hooks=()):
            segR1(i, 0)
            segMM(i, 0)
            segR1(i, 1)
            for hg in range(4):
                if hg + 1 < 4:
                    segMM(i, hg + 1)
                if hg + 2 < 4:
                    segR1(i, hg + 2)
                segFin(i, hg)
                if hg < len(hooks):
                    hooks[hg]()

        def yoff(i):
            tsl = tsl_(i)
            for g in range(2):
                bk, bkr = acc(2 + g)
                mm(bk[:, :], bcT[:, 2 + g, tsl], stbf[:, g * 512:(g + 1) * 512], True, True, R=["bcT", "stbf"], W=[bkr])

        upd_ps = {}

        def updMM(i):
            pst2 = [psum(), psum()]
            for g in range(2):
                mm(pst2[g][0][:, :], Btok[:, g * 128:(g + 1) * 128], xdtE[:, g * 512:(g + 1) * 512], True, True,
                   R=["Btok", "xdtE"], W=[pst2[g][1]])
            upd_ps[i] = pst2

        def U1(i):
            pst2 = upd_ps[i]
            for g in range(2):
                gs = slice(g * 512, (g + 1) * 512)
                tt("dve", st[:, gs].rearrange("p (h d) -> p h d", d=64), st[:, gs].rearrange("p (h d) -> p h d", d=64),
                   acs[:, 64 + g * 8:64 + (g + 1) * 8].unsqueeze(2).broadcast_to([128, 8, 64]), ALU.mult, R=["st", "acs"], W=["st"])
                tt("dve", st[:, gs], pst2[g][0][:, :], st[:, gs], ALU.add, R=[pst2[g][1], "st"], W=["st"])
            cp("act", stbf[:, :], st[:, :], R=["st"], W=["stbf"])

        def F1(i):
            for g in range(2):
                gs = slice(g * 512, (g + 1) * 512)
                po, por_ = acc(2 + g)
                py, pyr_ = acc(g)
                tt("dve", ybuf[:, gs].rearrange("p (h d) -> p h d", d=64), po[:, :].rearrange("p (h d) -> p h d", d=64),
                   acs[:, g * 8:(g + 1) * 8].unsqueeze(2).broadcast_to([128, 8, 64]), ALU.mult, R=[por_, "acs"], W=["ybuf"])
                tt("dve", ybuf[:, gs], py[:, :], ybuf[:, gs], ALU.add, R=[pyr_, "ybuf"], W=["ybuf"])
            tt("dve", ybuf[:, :], ybuf[:, :], xsD[:, :], ALU.add, R=["ybuf", "xsD"], W=["ybuf"])

        def F2(i):
            tt("pool", ybuf[:, :], ybuf[:, :], zs[:, i, :], ALU.mult, R=["ybuf", "zs"], W=["ybuf"])
            act(junk[:, :], ybuf[:, :], AF.Square, R=["ybuf"], W=["junk"])

        def F3(i):
            red(stat[:, 16:18], junk[:, :].rearrange("p (g d) -> p g d", d=512), R=["junk"], W=["stat"])
            rstd_from_ss(stat[:, 16:18], 2, 1.0 / 512, "stat")

        def F4(i):
            for g in range(2):
                gs = slice(g * 512, (g + 1) * 512)
                ts("dve", gn[:, gs], ybuf[:, gs], stat[:, 16 + g:17 + g], None, ALU.mult, ALU.bypass, R=["ybuf", "stat"], W=["gn"])

        def finB(i):
            tsl = tsl_(i)
            pst, ptr_ = psum()
            ptb = pst[:, :].bitcast(BF16)
            for cc in range(8):
                tr(ptb[:, cc * 128:(cc + 1) * 128], gn[:, cc * 128:(cc + 1) * 128], identb, R=["gn", "cbf"], W=[ptr_])
            tt("dve", yT[:, 0:8, tsl], ptb.rearrange("p (c t) -> p c t", t=128),
               wssdT[:, :].unsqueeze(2).broadcast_to([128, 8, 128]), ALU.mult, R=[ptr_, "wssdT"], W=["yT"])

        prepA(0)
        prepB(0)
        for i in range(4):
            yoff(i)
            updMM(i)
            F1(i)
            U1(i)
            if i + 1 < 4:
                prepA(i + 1)
                prepB(i + 1, hooks=[(lambda ii=i: F2(ii)), (lambda ii=i: F3(ii)), (lambda ii=i: F4(ii))])
            else:
                F2(i); F3(i); F4(i)
            finB(i)

        P.dma("sp", cosb[:, :], cos_d[:, b * BLK:(b + 1) * BLK], W=["cosb"], sem="cos")
        P.dma("sp", sinb[:, :], sin_d[:, b * BLK:(b + 1) * BLK], W=["sinb"], sem="sin")

        qk_state = {}

        def qkA(c):
            kind, g, j = ("q", c // 4, c % 4) if c < 8 else ("k", (c - 8) // 4, c % 4)
            if j == 0:
                base = 2576 if kind == "q" else 3600
                qk_state["w"] = wload(wv_in[:, :, base + g * 512: base + (g + 1) * 512])
            w, wr = qk_state["w"]
            ps, pr = psum8()
            for d in range(8):
                mm(ps[:, :], w[:, d, j * 128:(j + 1) * 128], hT[:, d, :], d == 0, d == 7, R=[wr, "hT"], W=[pr])
            k = c % 2
            act(sqb[k][:, :], ps[:, :], AF.Square, R=[pr], W=[f"sqb{k}"])
            qk_state[c] = (ps, pr)

        def qkB(c):
            k = c % 2
            ps, pr = qk_state[c]
            wvec, wname = (wq2, "wq2") if c < 8 else (wk2, "wk2")
            ps2, pr2 = psum8()
            mm(ps2[:, :], blockones, sqb[k][:, :], True, True, R=["cbf", f"sqb{k}"], W=[pr2])
            act(rtb[k][:, :], ps2[:, :], AF.Ln, R=[pr2, "epsc"], W=[f"rtb{k}"], scale=1.0 / 64, bias=epsc[:, 0:1])
            act(rinv[k][:, :], rtb[k][:, :], AF.Exp, R=[f"rtb{k}"], W=[f"rinv{k}"], scale=-0.5)
            stt(qn[k][:, :], ps[:, :], wvec[:, 0:1], rinv[k][:, :], ALU.mult, ALU.mult, R=[pr, wname, f"rinv{k}"], W=[f"qn{k}"])
            cp("act", qnb[k][:, :], qn[k][:, :], R=[f"qn{k}"], W=[f"qnb{k}"])

        def qkC(c):
            k = c % 2
            if c < 8:
                dest, dname = qT[:, c, :], "qT"
            else:
                dest, dname = kT[:, c - 8, b * BLK:(b + 1) * BLK], "kT"
            ps3, pr3 = psum8()
            mm(ps3[:, :], rotT, qnb[k][:, :], True, True, R=["cbf", f"qnb{k}"], W=[pr3])
            tt("dve", t1[k][:, :], qn[k][:, :], cosb[:, :], ALU.mult, R=[f"qn{k}", "cosb"], W=[f"t1{k}"])
            tt("dve", t2[k][:, :], ps3[:, :], sinb[:, :], ALU.mult, R=[pr3, "sinb"], W=[f"t2{k}"])
            tt("pool", dest, t1[k][:, :], t2[k][:, :], ALU.add, R=[f"t1{k}", f"t2{k}"], W=[dname])

        for step in range(18):
            if step < 16:
                qkA(step)
            if 0 <= step - 1 < 16:
                qkB(step - 1)
            if 0 <= step - 2 < 16:
                qkC(step - 2)
        for g in range(2):
            w, wr = wload(wv_in[:, :, 4624 + g * 512: 4624 + (g + 1) * 512])
            for i in range(4):
                ps, pr = tm_tile(w, wr, i)
                cp("act", vaug[:, 4 * b + i, g * 4:(g + 1) * 4, 0:128], ps[:, :].rearrange("p (h d) -> p h d", d=128),
                   R=[pr], W=["vaug"])
        nk = 4 * b + 4
        iters = [(h, kt) for h in range(8) for kt in range(nk)]
        NI = len(iters)
        NPT = len(pT)

        def QK(n):
            h, kt = iters[n]
            qlo = max(kt - 4 * b, 0)
            nn = BLK - qlo * 128
            sps = []
            for j in range(2):
                js = slice(j * 64, (j + 1) * 64)
                sp, spr = psum6()
                mm(sp[:, 0:nn], kT[js, h, kt * 128:(kt + 1) * 128], qT[js, h, qlo * 128:BLK], True, True,
                   R=["kT", "qT"], W=[spr])
                sps.append((sp, spr))
            for j in range(2):
                sp, spr = sps[j]
                pi = (2 * n + j) % NPT
                act(pT[pi][:, 0:nn], sp[:, 0:nn], AF.Exp, R=[spr], W=[f"pT{pi}"], scale=0.125)
                if kt >= 4 * b:
                    mset("pool", pT[pi][64:128, 0:64], 0.0, W=[f"pT{pi}"])

        def PV(n):
            h, kt = iters[n]
            qlo = max(kt - 4 * b, 0)
            nn = BLK - qlo * 128
            hb = h % 2
            for j in range(2):
                pi = (2 * n + j) % NPT
                ob, obr = acc(j)
                mm(ob[:, qlo * 128:BLK], vaug[:, kt, h, 0:128], pT[pi][:, 0:nn], kt == 0, kt == nk - 1,
                   R=[f"pT{pi}", "vaug"], W=[obr], sgc=True)
                if j == 0:
                    db, dbr = acc(2)
                    mm(db[0:1, qlo * 128:BLK], onesb[:, 0:1], pT[pi][:, 0:nn], kt == 0, kt == nk - 1,
                       R=[f"pT{pi}", "cbf"], W=[dbr], sgc=True)
                    continue
                eng = "dve"
                ps_ = pS[hb][j]; psn = f"pS{hb}{j}"
                if kt == 0:
                    cp(eng, ps_[:, :], pT[pi][:, :], R=[f"pT{pi}"], W=[psn])
                else:
                    tt(eng, ps_[:, qlo * 128:BLK], ps_[:, qlo * 128:BLK], pT[pi][:, 0:nn], ALU.add, R=[psn, f"pT{pi}"], W=[psn])

        def fin1(h):
            for j in range(2):
                ob, obr = acc(j)
                cp("dve" if j == 0 else "act", oS[j][:, :], ob[:, :], R=[obr], W=[f"oS{j}"])
            db, dbr = acc(2)
            cp("act", pS[h % 2][0][0:1, :], db[0:1, :], R=[dbr], W=[f"pS{h % 2}0"])

        def fin2(h):
            hb = h % 2
            for j in range(2):
                pb_, pbr_ = psum()
                if j == 0:
                    mm(pb_[:, :], ones32[0:1, :], pS[hb][0][0:1, :], True, True, R=["c32", f"pS{hb}0"], W=[pbr_])
                else:
                    mm(pb_[:, :], ones32, pS[hb][j][:, :], True, True, R=["c32", f"pS{hb}{j}"], W=[pbr_])
                dtmp, dname = (rtb[1], "rtb1") if j == 0 else (rinv[1], "rinv1")
                act(dtmp[:, :], pb_[:, :], AF.Ln, R=[pbr_], W=[dname])
                act(t1[j][:, :], dtmp[:, :], AF.Exp, R=[dname], W=[f"t1{j}"], scale=-1.0)
            tt("dve", t2[0][:, :], oS[0][:, :], t1[0][:, :], ALU.mult, R=["oS0", "t10"], W=["t20"])
            stt(t2[1][:, :], oS[1][:, :], neglam[:, 0:1], t1[1][:, :], ALU.mult, ALU.mult, R=["oS1", "neglam", "t11"], W=["t21"])
            tt("dve", t2[1][:, :], t2[1][:, :], t2[0][:, :], ALU.add, R=["t20", "t21"], W=["t21"])
            act(sqb[0][:, :], t2[1][:, :], AF.Square, R=["t21"], W=["sqb0"])

        def fin3(h):
            pss_, psr_ = psum()
            mm(pss_[:, :], onesb, sqb[0][:, :], True, True, R=["cbf", "sqb0"], W=[psr_])
            act(rtb[0][:, :], pss_[:, :], AF.Ln, R=[psr_, "epsc"], W=["rtb0"], scale=1.0 / 128, bias=epsc[:, 0:1])
            act(rinv[0][:, :], rtb[0][:, :], AF.Exp, R=["rtb0"], W=["rinv0"], scale=-0.5)
            stt(yT[:, 8 + h, :], t2[1][:, :], subw[:, 0:1], rinv[0][:, :], ALU.mult, ALU.mult,
                R=["t21", "subw", "rinv0"], W=["yT"])

        deferred = []
        QK(0)
        if NI > 1:
            QK(1)
        for n in range(NI):
            if n + 2 < NI:
                QK(n + 2)
            PV(n)
            for item in [d_ for d_ in deferred if d_[0] <= n]:
                deferred.remove(item)
                item[1]()
            h, kt = iters[n]
            if kt == nk - 1:
                fin1(h)
                deferred.append((n + 1, (lambda hh=h: fin2(hh))))
                deferred.append((n + 3, (lambda hh=h: fin3(hh))))
        for item in sorted(deferred, key=lambda d_: d_[0]):
            item[1]()

        if debug is not None and debug[0] == "yT":
            for c in range(16):
                cp("act", st[:, 0:512], yT[:, c, :], R=["yT"], W=["st"])
                P.dma("sp", out_d[c * 128:(c + 1) * 128, 0:512], st[:, 0:512], R=["st"], W=["outd"], sem="dbg")
                P.op("act", lambda e: e.copy(stat[:, 60:61], stat[:, 60:61]), R=["st"], W=["stat"])
            if nxt is not None:
                step1(nxt[0], nxt[1], range(4), "ab")
            return
        for ch in range(2):
            cs = slice(ch * 512, (ch + 1) * 512)
            for kh in range(2):
                w, wr = wload(wv_out[:, kh * 8:(kh + 1) * 8, cs], key="wb_out")
                for i in range(4):
                    bk, bkr = acc(i)
                    for kc in range(8):
                        mm(bk[:, :], yT[:, kh * 8 + kc, i * 128:(i + 1) * 128], w[:, kc, :], kh == 0 and kc == 0,
                           kh == 1 and kc == 7, R=["yT", wr], W=[bkr])
            for i in range(4):
                bk, bkr = acc(i)
                xr = xt[i % 2]
                P.dma("sp", xr[:, 0:512], x_d[tok0 + i * 128: tok0 + (i + 1) * 128, cs], W=[f"xt{i % 2}"], sem=f"xt{i % 2}")
                tt("dve", x1[i][:, cs], bk[:, :], xr[:, 0:512], ALU.add, R=[bkr, f"xt{i % 2}"], W=[f"x1_{i}"])
        if debug is not None and debug[0] == "x1":
            for i in range(4):
                P.dma("sp", out_d[tok0 + i * 128: tok0 + (i + 1) * 128, :], x1[i][:, :], R=[f"x1_{i}"], W=["outd"],
                      sem=f"o{i}", join=True)
            if nxt is not None:
                step1(nxt[0], nxt[1], range(4), "ab")
            return
        for i in range(4):
            act(junk[:, :], x1[i][:, :], AF.Square, R=[f"x1_{i}"], W=["junk"])
            red(stat[:, 32 + i:33 + i], junk[:, :], R=["junk"], W=["stat"])
        rstd_from_ss(stat[:, 32:36], 4, 1.0 / D, "stat")
        for i in range(4):
            xb = xn[i % 2]
            ts("dve", xb[:, :], x1[i][:, :], stat[:, 32 + i:33 + i], None, ALU.mult, ALU.bypass,
               R=[f"x1_{i}", "stat"], W=[f"xn{i % 2}"])
            ps, pr = psum()
            pb = ps[:, :].bitcast(BF16)
            for d in range(8):
                tr(pb[:, d * 128:(d + 1) * 128], xb[:, d * 128:(d + 1) * 128], identb, R=[f"xn{i % 2}", "cbf"], W=[pr])
            tt("dve", hT[:, :, i * 128:(i + 1) * 128], pb.rearrange("p (d t) -> p d t", t=128),
               w2T[:, :].unsqueeze(2).broadcast_to([128, 8, 128]), ALU.mult, R=[pr, "w2T"], W=["hT"])
        if nxt is not None:
            step1(nxt[0], nxt[1], range(0, 2), "a")
        for fg in range(6):
            ncol = 512 if fg < 5 else 256
            wgt, wgr = wload(wv_g[:, :, fg * 512: fg * 512 + ncol], ncol=ncol, key="wb_g")
            for j in range(ncol // 128):
                psg, pgr = fm_chunk(wgt, wgr, j, hT, "hT")
                act(sg4[j][:, :], psg[:, :], AF.Silu, R=[pgr], W=[f"sg{j}"])
            wut, wur = wload(wv_u[:, :, fg * 512: fg * 512 + ncol], ncol=ncol, key="wb_u")
            for j in range(ncol // 128):
                psu, pur = fm_chunk(wut, wur, j, hT, "hT")
                tt("dve", aT[:, fg * 4 + j, :], sg4[j][:, :], psu[:, :], ALU.mult, R=[f"sg{j}", pur], W=["aT"])
        if nxt is not None:
            step1(nxt[0], nxt[1], range(0, 2), "b")
        for ch in range(2):
            cs = slice(ch * 512, (ch + 1) * 512)
            for kg in range(3):
                nk = 8 if kg < 2 else 6
                w, wr = wload(wv_d[:, kg * 8: kg * 8 + nk, cs], nk=nk, key="wb_d")
                for i in range(4):
                    bk, bkr = acc(i)
                    for kc in range(nk):
                        mm(bk[:, :], aT[:, kg * 8 + kc, i * 128:(i + 1) * 128], w[:, kc, :], kg == 0 and kc == 0,
                           kg == 2 and kc == nk - 1, R=["aT", wr], W=[bkr])
            if ch == 0 and nxt is not None:
                step1(nxt[0], nxt[1], range(2, 4), "a")
            for i in range(4):
                bk, bkr = acc(i)
                tt("dve", x1[i][:, cs], bk[:, :], x1[i][:, cs], ALU.add, R=[bkr, f"x1_{i}"], W=[f"x1_{i}"])
            if ch == 0 and nxt is not None:
                step1(nxt[0], nxt[1], range(2, 4), "b")
        for i in range(4):
            P.dma("sp", out_d[tok0 + i * 128: tok0 + (i + 1) * 128, :], x1[i][:, :], R=[f"x1_{i}"], W=["outd"],
                  sem=f"o{i}", join=True)

    order = [(s, b) for s in range(nseq) for b in range(nblk)]
    step1(order[0][0], order[0][1], range(4), "ab")
    for n_, (s, b) in enumerate(order):
        emit_block(s, b, order[n_ + 1] if n_ + 1 < len(order) else None)
    P.final_wait("sp", [f"o{i}" for i in range(4)] + ["dbg"])
    P.emit(es)
    es.close()
    return nc


_CONSTS = None


def kernel(**inputs):
    global _CONSTS
    n = 8
    x = np.ascontiguousarray(inputs["x"], dtype=np.float32)
    B = x.shape[0]
    per = B // n
    if _CONSTS is None:
        _CONSTS = host_consts()
    c32, cb, cosT, sinT = _CONSTS
    nc = build_program(nseq=per)
    shared = {
        "w_in": inputs["w_in"][0], "w_out": inputs["w_out"][0], "w_gate": inputs["w_gate"][0],
        "w_up": inputs["w_up"][0], "w_down": inputs["w_down"][0],
        "norm1_w": inputs["norm1_w"], "norm2_w": inputs["norm2_w"], "ssd_norm_w": inputs["ssd_norm_w"],
        "conv_w": inputs["conv_w"][0], "conv_b": inputs["conv_b"],
        "dt_bias": inputs["dt_bias"], "a_log": inputs["a_log"], "d_skip": inputs["d_skip"],
        "q_norm_w": inputs["q_norm_w"], "k_norm_w": inputs["k_norm_w"],
        "lambda_q1": inputs["lambda_q1"], "lambda_k1": inputs["lambda_k1"],
        "lambda_q2": inputs["lambda_q2"], "lambda_k2": inputs["lambda_k2"],
        "subln_w": inputs["subln_w"],
        "c32": c32, "cbf": cb, "cosT": cosT, "sinT": sinT,
    }
    shared = {k: np.ascontiguousarray(v, dtype=np.float32) for k, v in shared.items()}
    in_maps = []
    for c in range(n):
        m = dict(shared)
        m["x"] = x[c * per:(c + 1) * per].reshape(per * SEQ, D)
        in_maps.append(m)
    res = run_bass_kernel_spmd(nc, in_maps, core_ids=list(range(n)))
    out = np.concatenate([np.asarray(r["out"]).reshape(per, SEQ, D) for r in res.results], axis=0)
    return out.astype(np.float32)
```
